# Optimizing a Trainium2 kernel written in Bass

```python
import math
import jax, jax.numpy as jnp
from jax import lax
import numpy as np

D_MODEL = 1024
BATCH = 4
SEQ = 4096
DEPTH = 2
DEC_BATCH = 128
DEC_SEQ = 1
PAST_LEN = 2048
PAGE_SIZE = 128

A_WIDTH = D_MODEL // 2
A_GROUPS = 4
A_GROUP_DIM = A_WIDTH // A_GROUPS
A_CHUNK = 128
B_HEADS = 4
B_KDIM = 128
B_WIDTH = D_MODEL // 2
B_VDIM = B_WIDTH // B_HEADS
B_FDIM = B_HEADS * B_KDIM
B_CHUNK = 64
C_HEADS = 8
C_HEAD_DIM = 64
C_WIDTH = C_HEADS * C_HEAD_DIM
C_QBLOCK = 128
C_BIAS_INIT = -6.0
N_BRANCH = 3
D_FF = 4 * D_MODEL
IN_COLS = 2 * A_WIDTH + 2 * B_FDIM + 2 * B_WIDTH + 3 * C_WIDTH + N_BRANCH * D_MODEL
EPS = 1e-6

kernel_name = "hybrid_gmlp_hgrn2_stickbreaking_decoder_step"


def _in_split_points():
    widths = (A_WIDTH, A_WIDTH, B_FDIM, B_FDIM, B_WIDTH, B_WIDTH, C_WIDTH, C_WIDTH, C_WIDTH)
    pts, acc = [], 0
    for w in widths:
        acc += w
        pts.append(acc)
    return pts


def rmsnorm(x, w):
    xf = x.astype(jnp.float32)
    y = xf * lax.rsqrt(jnp.mean(xf * xf, axis=-1, keepdims=True) + EPS)
    return (y * w.astype(jnp.float32)).astype(x.dtype)


def chunk_spatial_gating(u, v, w_s, b_s):
    Bn, L = v.shape[:2]
    n = -(-L // A_CHUNK)
    vp = jnp.pad(v, ((0, 0), (0, n * A_CHUNK - L), (0, 0), (0, 0)))
    vp = vp.reshape(Bn, n, A_CHUNK, A_GROUPS, A_GROUP_DIM)
    causal = jnp.tril(jnp.ones((A_CHUNK, A_CHUNK), bool))
    w = jnp.where(causal[None], w_s, 0)
    mixed = jnp.einsum('gts,bnsgc->bntgc', w, vp) + b_s.T[None, None, :, :, None]
    mixed = mixed.reshape(Bn, n * A_CHUNK, A_GROUPS, A_GROUP_DIM)[:, :L]
    return u * mixed


def hgrn2_recurrence(q, k, v, log_f, S0):
    Bn, L = q.shape[:2]
    C = min(B_CHUNK, L)
    n = -(-L // C)
    pad = ((0, 0), (0, n * C - L), (0, 0), (0, 0))

    def chunks(t):
        t = jnp.pad(t, pad)
        return t.reshape(Bn, n, C, *t.shape[2:]).swapaxes(0, 1)

    qc, kc, vc, gc = chunks(q), chunks(k), chunks(v), chunks(log_f)
    causal = jnp.tril(jnp.ones((C, C), bool))[None, :, :, None, None]

    def step(S, xs):
        qi, ki, vi, gi = xs
        b = jnp.cumsum(gi, axis=1)
        b_last = b[:, -1]
        o_inter = jnp.einsum('bthk,bhkv->bthv', qi * jnp.exp(b).astype(qi.dtype), S)
        rel = jnp.where(causal, b[:, :, None] - b[:, None, :], -jnp.inf)
        scores = jnp.einsum('btshk,bshk->btsh', jnp.exp(rel).astype(qi.dtype) * qi[:, :, None], ki)
        o_intra = jnp.einsum('btsh,bshv->bthv', scores, vi)
        k_dec = ki * jnp.exp(b_last[:, None] - b).astype(ki.dtype)
        S_new = jnp.exp(b_last)[..., None].astype(S.dtype) * S + jnp.einsum('bshk,bshv->bhkv', k_dec, vi)
        return S_new, o_inter + o_intra

    S_fin, o = lax.scan(step, S0, (qc, kc, vc, gc))
    o = o.swapaxes(0, 1).reshape(Bn, n * C, *o.shape[3:])[:, :L]
    return o, S_fin


def stick_breaking_attention(q, k, v, bias):
    Bn, Lq = q.shape[:2]
    Lk = k.shape[1]
    qb = min(C_QBLOCK, Lq)
    n = -(-Lq // qb)
    qp = jnp.pad(q, ((0, 0), (0, n * qb - Lq), (0, 0), (0, 0)))
    qp = qp.reshape(Bn, n, qb, C_HEADS, C_HEAD_DIM).swapaxes(0, 1)
    kf = k.astype(jnp.float32)
    kpos = jnp.arange(Lk)
    scale = C_HEAD_DIM ** -0.5
    bias_f = bias.astype(jnp.float32)[None, :, None, None]

    def block(args):
        qblk, i = args
        qpos = (Lk - Lq) + i * qb + jnp.arange(qb)
        z = jnp.einsum('bqhd,bkhd->bhqk', qblk.astype(jnp.float32), kf) * scale + bias_f
        mask = kpos[None, :] < qpos[:, None]
        sp = jnp.where(mask, jax.nn.softplus(z), 0.0)
        later = lax.cumsum(sp, axis=3, reverse=True) - sp
        w = jnp.where(mask, jnp.exp(jax.nn.log_sigmoid(z) - later), 0.0)
        return jnp.einsum('bhqk,bkhd->bqhd', w.astype(v.dtype), v)

    o = lax.map(block, (qp, jnp.arange(n)))
    return o.swapaxes(0, 1).reshape(Bn, n * qb, C_HEADS, C_HEAD_DIM)[:, :Lq]


def decoder_layer(x, c, past_k, past_v, S0, lb, w_ada, b_ada, norm1_w, norm2_w, w_in,
                  vnorm_w, w_s, b_s, sb_bias, onorm_w, w_branch_a, w_branch_b, w_branch_c,
                  w_out, w_ff1, w_ff2):
    Bn, L, _ = x.shape
    ada = jnp.einsum('bd,de->be', jax.nn.silu(c), w_ada) + b_ada
    sh1, sc1, g1, sh2, sc2, g2 = jnp.split(ada[:, None, :], 6, axis=-1)

    h = rmsnorm(x, norm1_w) * (1 + sc1) + sh1
    proj = jnp.einsum('bld,de->ble', h, w_in)
    a_u, a_v, b_q, b_f, b_i, b_g, c_q, c_k, c_v, gate_pre = jnp.split(proj, _in_split_points(), axis=-1)

    u = jax.nn.gelu(a_u).reshape(Bn, L, A_GROUPS, A_GROUP_DIM)
    v_a = rmsnorm(jax.nn.gelu(a_v), vnorm_w)
    o_a = chunk_spatial_gating(u, v_a.reshape(Bn, L, A_GROUPS, A_GROUP_DIM), w_s, b_s)
    o_a = o_a.reshape(Bn, L, A_WIDTH)

    q_b = jax.nn.silu(b_q).reshape(Bn, L, B_HEADS, B_KDIM)
    lbf = lb.astype(jnp.float32)
    log_f = jnp.logaddexp(jnp.log(lbf), jnp.log1p(-lbf) + jax.nn.log_sigmoid(b_f.astype(jnp.float32)))
    log_f = log_f.reshape(Bn, L, B_HEADS, B_KDIM)
    k_b = (1.0 - jnp.exp(log_f)).astype(x.dtype)
    v_b = b_i.reshape(Bn, L, B_HEADS, B_VDIM)
    o_b, S_new = hgrn2_recurrence(q_b, k_b, v_b, log_f, S0)
    o_b = rmsnorm(o_b, onorm_w.reshape(B_HEADS, B_VDIM)) * jax.nn.silu(b_g.reshape(Bn, L, B_HEADS, B_VDIM))
    o_b = o_b.reshape(Bn, L, B_WIDTH)

    q_c = c_q.reshape(Bn, L, C_HEADS, C_HEAD_DIM)
    k_c = c_k.reshape(Bn, L, C_HEADS, C_HEAD_DIM)
    v_c = c_v.reshape(Bn, L, C_HEADS, C_HEAD_DIM)
    if past_k is None:
        k_all, v_all = k_c, v_c
    else:
        k_all = jnp.concatenate([past_k, k_c], axis=1)
        v_all = jnp.concatenate([past_v, v_c], axis=1)
    o_c = stick_breaking_attention(q_c, k_all, v_all, sb_bias).reshape(Bn, L, C_WIDTH)

    g_a, g_b, g_c = jnp.split(jax.nn.sigmoid(gate_pre), N_BRANCH, axis=-1)
    merged = (g_a * jnp.einsum('blw,wd->bld', o_a, w_branch_a)
              + g_b * jnp.einsum('blw,wd->bld', o_b, w_branch_b)
              + g_c * jnp.einsum('blw,wd->bld', o_c, w_branch_c))
    x = x + g1 * jnp.einsum('bld,de->ble', merged, w_out)

    h2 = rmsnorm(x, norm2_w) * (1 + sc2) + sh2
    ff = jnp.square(jax.nn.relu(jnp.einsum('bld,df->blf', h2, w_ff1)))
    x = x + g2 * jnp.einsum('blf,fd->bld', ff, w_ff2)
    return x, k_c, v_c, S_new, v_a


def setup_inputs(seed: int = 0) -> dict:
    key = jax.random.key(seed)
    ks = jax.random.split(key, 32)

    def nrm(k, shape, scale=1.0):
        return jax.random.normal(k, shape, jnp.float32) * scale

    n_pages = PAST_LEN // PAGE_SIZE
    n_used = DEC_BATCH * n_pages
    n_pool = n_used + max(1, n_used // 4)
    page_table = jax.random.permutation(ks[7], n_pool)[:n_used].reshape(DEC_BATCH, n_pages).astype(jnp.int32)

    return {
        "x_prompt": nrm(ks[0], (BATCH, SEQ, D_MODEL)),
        "x_sample": nrm(ks[1], (DEC_BATCH, DEC_SEQ, D_MODEL)),
        "c_prompt": nrm(ks[2], (BATCH, D_MODEL)),
        "c_sample": nrm(ks[3], (DEC_BATCH, D_MODEL)),
        "cache_k": nrm(ks[4], (DEPTH, n_pool, PAGE_SIZE, C_HEADS, C_HEAD_DIM)),
        "cache_v": nrm(ks[5], (DEPTH, n_pool, PAGE_SIZE, C_HEADS, C_HEAD_DIM)),
        "state_hgrn": nrm(ks[6], (DEPTH, DEC_BATCH, B_HEADS, B_KDIM, B_VDIM), 0.5),
        "page_table": page_table,
        "w_ada": nrm(ks[8], (DEPTH, D_MODEL, 6 * D_MODEL), D_MODEL ** -0.5),
        "b_ada": nrm(ks[9], (DEPTH, 6 * D_MODEL), 0.02),
        "norm1_w": 1.0 + nrm(ks[10], (DEPTH, D_MODEL), 0.05),
        "norm2_w": 1.0 + nrm(ks[11], (DEPTH, D_MODEL), 0.05),
        "w_in": nrm(ks[12], (DEPTH, D_MODEL, IN_COLS), D_MODEL ** -0.5),
        "gmlp_vnorm_w": 1.0 + nrm(ks[13], (DEPTH, A_WIDTH), 0.05),
        "gmlp_w_s": nrm(ks[14], (DEPTH, A_GROUPS, A_CHUNK, A_CHUNK), A_CHUNK ** -0.5),
        "gmlp_b_s": nrm(ks[15], (DEPTH, A_GROUPS, A_CHUNK), 0.1),
        "sb_bias": C_BIAS_INIT + nrm(ks[25], (DEPTH, C_HEADS), 0.1),
        "hgrn_lb_logits": nrm(ks[16], (DEPTH, B_FDIM)),
        "hgrn_onorm_w": 1.0 + nrm(ks[17], (DEPTH, B_WIDTH), 0.05),
        "w_branch_a": nrm(ks[18], (DEPTH, A_WIDTH, D_MODEL), A_WIDTH ** -0.5),
        "w_branch_b": nrm(ks[19], (DEPTH, B_WIDTH, D_MODEL), B_WIDTH ** -0.5),
        "w_branch_c": nrm(ks[20], (DEPTH, C_WIDTH, D_MODEL), C_WIDTH ** -0.5),
        "w_out": nrm(ks[21], (DEPTH, D_MODEL, D_MODEL), D_MODEL ** -0.5),
        "w_ff1": nrm(ks[22], (DEPTH, D_MODEL, D_FF), D_MODEL ** -0.5),
        "w_ff2": nrm(ks[23], (DEPTH, D_FF, D_MODEL), D_FF ** -0.5),
        "final_norm_w": 1.0 + nrm(ks[24], (D_MODEL,), 0.05),
    }


def reference(x_prompt, x_sample, c_prompt, c_sample, cache_k, cache_v, state_hgrn, page_table,
              w_ada, b_ada, norm1_w, norm2_w, w_in, gmlp_vnorm_w, gmlp_w_s, gmlp_b_s, sb_bias,
              hgrn_lb_logits, hgrn_onorm_w, w_branch_a, w_branch_b, w_branch_c, w_out,
              w_ff1, w_ff2, final_norm_w):
    lb_cum = jnp.cumsum(jax.nn.softmax(hgrn_lb_logits.astype(jnp.float32), axis=0), axis=0)
    lower_bounds = lb_cum - lb_cum[0:1]

    yp, ys = x_prompt, x_sample
    n_seq_dec = page_table.shape[0]
    S0_prompt = jnp.zeros((x_prompt.shape[0], B_HEADS, B_KDIM, B_VDIM), x_prompt.dtype)
    kp_l, vp_l, Sp_l, ks_l, vs_l, Ss_l, gv_l = [], [], [], [], [], [], []
    for l in range(DEPTH):
        lw = (lower_bounds[l], w_ada[l], b_ada[l], norm1_w[l], norm2_w[l], w_in[l],
              gmlp_vnorm_w[l], gmlp_w_s[l], gmlp_b_s[l], sb_bias[l], hgrn_onorm_w[l],
              w_branch_a[l], w_branch_b[l], w_branch_c[l], w_out[l], w_ff1[l], w_ff2[l])
        yp, kp, vp, Sp, _ = decoder_layer(yp, c_prompt, None, None, S0_prompt, *lw)
        past_k = cache_k[l][page_table].reshape(n_seq_dec, -1, C_HEADS, C_HEAD_DIM)
        past_v = cache_v[l][page_table].reshape(n_seq_dec, -1, C_HEADS, C_HEAD_DIM)
        ys, kS, vS, SS, gv = decoder_layer(ys, c_sample, past_k, past_v, state_hgrn[l], *lw)
        kp_l.append(kp); vp_l.append(vp); Sp_l.append(Sp)
        ks_l.append(kS); vs_l.append(vS); Ss_l.append(SS); gv_l.append(gv)

    y_prompt = rmsnorm(yp, final_norm_w)
    y_sample = rmsnorm(ys, final_norm_w)
    k_prompt = jnp.stack(kp_l)
    v_prompt = jnp.stack(vp_l)
    hgrn_state_prompt = jnp.stack(Sp_l)
    k_sample = jnp.stack(ks_l)
    v_sample = jnp.stack(vs_l)
    hgrn_state_sample = jnp.stack(Ss_l)
    gmlp_v_sample = jnp.stack(gv_l)
    return (y_prompt, y_sample, k_prompt, v_prompt, hgrn_state_prompt, k_sample, v_sample, hgrn_state_sample, gmlp_v_sample)
```

```python
import numpy as np
from contextlib import ExitStack
import concourse.bass as bass
import concourse.mybir as mybir
from concourse.bass_utils import run_bass_kernel_spmd

F32 = mybir.dt.float32
BF16 = mybir.dt.bfloat16
I32 = mybir.dt.int32
AF = mybir.ActivationFunctionType
ALU = mybir.AluOpType
AX = mybir.AxisListType
D = 1024
EPS = 1e-6
NS = 16


class Res:
    __slots__ = ("name", "w", "rs")

    def __init__(self, name):
        self.name = name
        self.w = None
        self.rs = {}


class Prog:
    def __init__(self, nc, es):
        self.nc = nc
        self.es = es
        self.eng = {"pe": nc.tensor, "act": nc.scalar, "dve": nc.vector, "pool": nc.gpsimd, "sp": nc.sync}
        self.esem = {k: es.enter_context(nc.semaphore("e_" + k)) for k in self.eng}
        self.cnt = {k: 0 for k in self.eng}
        self.seen = {k: {} for k in self.eng}
        self.dsem = {}
        self.dcnt = {}
        self.sems = {}
        self.store_sems = {}

    def _waits(self, e, reads, writes, skip=None, selfsync=True):
        need = {}

        def add(tok):
            if tok is None:
                return
            s, v = tok
            if s is skip:
                return
            if need.get(id(s), (None, 0))[1] < v:
                need[id(s)] = (s, v)

        for r in reads:
            add(r.w)
        for w in writes:
            add(w.w)
            for t in w.rs.values():
                add(t)
        eng = self.eng[e]
        for sid, (s, v) in need.items():
            if (not selfsync) and s is self.esem[e]:
                continue
            if self.seen[e].get(sid, 0) >= v:
                continue
            eng.wait_ge(s, v)
            self.seen[e][sid] = v

    def _commit(self, tok, reads, writes):
        for r in reads:
            r.rs[id(tok[0])] = tok
        for w in writes:
            w.w = tok
            w.rs = {}

    def op(self, e, fn, reads=(), writes=(), selfsync=True):
        self._waits(e, reads, writes, selfsync=selfsync)
        ins = fn(self.eng[e])
        self.cnt[e] += 1
        ins.then_inc(self.esem[e], 1)
        self._commit((self.esem[e], self.cnt[e]), reads, writes)

    def dma(self, q, fn, reads, writes, store=False):
        owner = reads[0] if store else writes[0]
        key = (id(owner), store)
        if key not in self.dsem:
            self.dsem[key] = self.es.enter_context(self.nc.semaphore(("s_" if store else "d_") + owner.name))
            self.dcnt[key] = 0
        sem = self.dsem[key]
        self._waits(q, reads, writes, skip=sem)
        ins = fn(self.eng[q])
        self.dcnt[key] += 16
        ins.then_inc(sem, 16)
        if store:
            self.store_sems[id(sem)] = (sem, self.dcnt[key])
        self._commit((sem, self.dcnt[key]), reads, writes)

    def wait_all(self, e, ress):
        self._waits(e, ress, ress)
        for sid, (sem, v) in self.store_sems.items():
            if self.seen[e].get(sid, 0) < v:
                self.eng[e].wait_ge(sem, v)
                self.seen[e][sid] = v


def build(SEQ, NPG, NPOOL, NT):
    nc = bass.Bass("TRN2", target_bir_lowering=False)
    es = ExitStack()
    P = Prog(nc, es)
    NTILES = SEQ // 128
    NG = NTILES // NT
    NP = NT * 128
    NCOL = NS * NPG

    def din(name, shape, dt=F32):
        return nc.dram_tensor(name, list(shape), dt, kind="ExternalInput")

    def dout(name, shape, dt=F32):
        return nc.dram_tensor(name, list(shape), dt, kind="ExternalOutput")

    def dscr(name, shape, dt):
        return nc.dram_tensor(name, list(shape), dt)

    def SB(name, shape, dt):
        return es.enter_context(nc.sbuf_tensor(name, list(shape), dt))

    def PS(name, shape, dt):
        return es.enter_context(nc.psum_tensor(name, list(shape), dt))

    xp = din("xp", [SEQ, D]); xs = din("xs", [128, D]); cp = din("cp", [128, D]); cs = din("cs", [128, D])
    ck = din("ck", [2, NPOOL * 128, 512]); cv = din("cv", [2, NPOOL * 128, 512])
    st = din("st", [2, NS, 4, 128, 128]); pt = din("pt", [1, NCOL], I32)
    consts = din("consts", [128, 1413])
    w_ada = din("w_ada", [2, D, 6144]); b_ada = din("b_ada", [2, 6144])
    n1w = din("n1w", [2, D]); n2w = din("n2w", [2, D]); w_in = din("w_in", [2, D, 7680])
    vnw = din("vnw", [2, 512]); w_s = din("w_s", [2, 4, 128, 128]); b_s = din("b_s", [2, 4, 128])
    sbb = din("sbb", [2, 8]); lbl = din("lbl", [2, 512]); onw = din("onw", [2, 512])
    w_ba = din("w_ba", [2, 512, D]); w_bb = din("w_bb", [2, 512, D]); w_bc = din("w_bc", [2, 512, D])
    w_out = din("w_out", [2, D, D]); w_ff1 = din("w_ff1", [2, D, 4096]); w_ff2 = din("w_ff2", [2, 4096, D])
    fnw = din("fnw", [1, D])

    yp = dout("yp", [SEQ, D]); ys = dout("ys", [NS, D])
    kp = dout("kp", [2, SEQ, 512]); vp = dout("vp", [2, SEQ, 512]); Sp = dout("Sp", [2, 4, 128, 128])
    ks = dout("ks", [2, NS, 512]); vs = dout("vs", [2, NS, 512]); Ss = dout("Ss", [2, NS, 4, 128, 128])
    gv = dout("gv", [2, NS, 512])
    outs_res = {n: Res(n) for n in ["yp", "ys", "kp", "vp", "Sp", "ks", "vs", "Ss", "gv"]}

    wsp = {"ada": (w_ada, D, 6144), "in": (w_in, D, 7680), "ba": (w_ba, 512, D), "bb": (w_bb, 512, D),
           "bc": (w_bc, 512, D), "out": (w_out, D, D), "ff1": (w_ff1, D, 4096), "ff2": (w_ff2, 4096, D)}
    wb = {}
    wres = {}
    for l in range(2):
        for k, (src, K, C) in wsp.items():
            wb[(l, k)] = dscr(f"wb_{k}{l}", [K, C], BF16)
            wres[(l, k)] = Res(f"wb_{k}{l}")
    x1 = dscr("x1", [SEQ + 128, D], F32); x1_res = [Res(f"x1_{t}") for t in range(SEQ // 128 + 1)]
    qscr = dscr("qscr", [NS, 512], F32); qscr_res = Res("qscr")
    vscr = dscr("vscr", [NS, 512], F32); vscr_res = Res("vscr")

    cst = SB("cst", [128, 1413], F32); cst_r = Res("cst")
    ident_f = cst[:, 0:128]; iota_c = cst[:, 1152:1153]
    cb = SB("cb", [128, 1152], BF16); cb_r = Res("cb")
    ident_b = cb[:, 0:128]; tri_b = cb[:, 128:256]; ones_b = cb[:, 256:384]; mstrict_b = cb[:, 384:512]
    bt32x4_b = cb[:, 512:1024]; causal_b = cb[:, 1024:1152]
    bt32_f = cst[:, 512:640]; bmid_f = cst[:, 1153:1281]; rev32_f = cst[:, 1281:1409]; submask = cst[:, 1409:1413]
    zeros_b = SB("zeros_b", [128, NP], BF16); zeros_r = Res("zeros")

    KT = SB("KT", [128, 4, SEQ], BF16); KT_r = [Res(f"KT{t}") for t in range(NTILES)]
    VH = SB("VH", [128, NTILES, 512], BF16); VH_r = [Res(f"VH{t}") for t in range(NTILES)]
    xg = SB("xg", [128, NT, D], F32); xg_r = [Res(f"xg{i}") for i in range(NT)]
    mod = SB("mod", [128, 6144], BF16); mod_r = Res("mod")
    HT = SB("HT", [128, 8, NP], BF16); HT_r = Res("HT")
    NW = 2
    wring = [SB(f"wr{i}", [128, 4096], BF16) for i in range(NW)]; wring_r = [Res(f"wr{i}") for i in range(NW)]
    wr_i = [0]
    Pb = [SB(f"Pb{i}", [128, 4, NP], BF16) for i in range(3)]; Pb_r = [Res(f"Pb{i}") for i in range(3)]
    OA = SB("OA", [128, 4, NP], BF16); OA_r = Res("OA")
    OB = SB("OB", [128, 4, NP], BF16); OB_r = Res("OB")
    OC = SB("OC", [128, 4, NP], BF16); OC_r = Res("OC")
    BIG = SB("BIG", [128, 24, NP], BF16); BIG_r = [Res(f"BIG{j}") for j in range(24)]
    FB = SB("FB", [128, NT, 512], F32); FB_r = [Res(f"FB{i}") for i in range(NT)]
    NTMP = 6
    tmpf = [SB(f"tf{i}", [128, 1024], F32) for i in range(NTMP)]; tmpf_r = [Res(f"tf{i}") for i in range(NTMP)]
    tf_i = [0]
    NTB = 8
    tmpb = [SB(f"tb{i}", [128, 1024], BF16) for i in range(NTB)]; tmpb_r = [Res(f"tb{i}") for i in range(NTB)]
    tb_i = [0]
    NSM = 8
    small = [SB(f"sm{i}", [128, 8], F32) for i in range(NSM)]; small_r = [Res(f"sm{i}") for i in range(NSM)]
    sm_i = [0]
    fT_d = SB("fT_d", [128, 1024], F32); fT_dr = Res("fT_d")
    S_f = SB("S_f", [128, 4, 128], F32); S_fr = [Res(f"S_f{h}") for h in range(4)]
    S_b = SB("S_b", [128, 4, 128], BF16); S_br = [Res(f"S_b{h}") for h in range(4)]
    Rf4 = SB("Rf4", [128, 4, NP], F32); Rf4_r = [Res(f"Rf{h}") for h in range(4)]
    Rb4 = SB("Rb4", [128, 4, NP], BF16); Rb4_r = [Res(f"Rb{h}") for h in range(4)]
    WT = SB("WT", [128, 4, 128], BF16); WT_r = Res("WT")
    bsrow = SB("bsrow", [1, 512], BF16); bsrow_r = Res("bsrow")
    lc = SB("lc", [128, 64], F32); lc_r = Res("lc")
    lbt = SB("lbt", [128, 512], F32); lbt_r = Res("lbt")
    oml = SB("oml", [128, 512], F32); oml_r = Res("oml")
    vnb = SB("vnb", [128, 512], F32); vnb_r = Res("vnb")
    IDX = SB("IDX", [128, NCOL], mybir.dt.uint32); IDX_r = Res("IDX")
    PTs = SB("PTs", [128, NCOL], I32); PTs_r = Res("PTs")
    rmask = SB("rmask", [128, 8 * NPG], F32); rmask_r = Res("rmask")
    Sb_t = [SB(f"Sbt{i}", [128, 4, 128], F32) for i in range(1)]; Sb_r = [Res(f"Sbt{i}") for i in range(1)]
    assert NTILES >= NPG
    Vpg = VH; Vpg_r = VH_r
    QB = [FB[:, i % NT, :] for i in range(2)]; QB_r = [FB_r[i % NT] for i in range(2)]
    zb = SB("zb", [128, 8, NPG], F32); zb_r = Res("zb")
    eb_s = SB("eb_s", [128, 8, NPG], F32); eb_sr = Res("eb_s")
    spb_s = SB("spb_s", [128, 8 * NPG], BF16); spb_sr = Res("spb_s")
    wb_s = SB("wb_s", [128, 8, NPG], BF16); wb_sr = Res("wb_s")

    NPS = 5
    psr = [PS(f"ps{i}", [128, 512], F32) for i in range(NPS)]; psr_r = [Res(f"ps{i}") for i in range(NPS)]
    ps_i = [0]
    psT = PS("psT", [128, 1024], BF16); psT_r = Res("psT")
    psAcc = PS("psAcc", [128, 512], F32); psAcc_r = Res("psAcc")
    psAcc2 = PS("psAcc2", [128, 512], F32); psAcc2_r = Res("psAcc2")

    def ring(lst, rl, idx):
        i = idx[0] % len(lst)
        idx[0] += 1
        return lst[i], rl[i]

    def psum():
        return ring(psr, psr_r, ps_i)

    def tf():
        return ring(tmpf, tmpf_r, tf_i)

    def tbf():
        return ring(tmpb, tmpb_r, tb_i)

    def sm():
        return ring(small, small_r, sm_i)

    def mm(out, lhsT, rhs, start, stop, R, W):
        P.op("pe", lambda e: e.matmul(out, lhsT=lhsT, rhs=rhs, start=start, stop=stop), reads=R, writes=W,
             selfsync=False)

    def tr(out, in_, ident, R, W):
        P.op("pe", lambda e: e.transpose(out=out, in_=in_, identity=ident), reads=R, writes=W, selfsync=False)

    def act(out, in_, func, R, W, bias=None, scale=None, accum=None):
        kw = {}
        if bias is not None:
            kw["bias"] = bias
        if scale is not None:
            kw["scale"] = scale
        if accum is not None:
            kw["accum_out"] = accum
        P.op("act", lambda e: e.activation(out=out, in_=in_, func=func, **kw), reads=R, writes=W)

    def tt(out, a, b, op, R, W, eng="dve"):
        P.op(eng, lambda e: e.tensor_tensor(out=out, in0=a, in1=b, op=op), reads=R, writes=W)

    def ts(out, a, s1, s2, op0, op1, R, W, eng="dve"):
        if op1 is None:
            P.op(eng, lambda e: e.tensor_scalar(out=out, in0=a, scalar1=s1, scalar2=None, op0=op0), reads=R, writes=W)
        else:
            P.op(eng, lambda e: e.tensor_scalar(out=out, in0=a, scalar1=s1, scalar2=s2, op0=op0, op1=op1),
                 reads=R, writes=W)

    def stt(out, a, s, b, op0, op1, R, W):
        P.op("dve", lambda e: e.scalar_tensor_tensor(out=out, in0=a, scalar=s, in1=b, op0=op0, op1=op1),
             reads=R, writes=W)

    def cp_(out, in_, R, W, eng="dve"):
        if eng == "act":
            P.op(eng, lambda e: e.activation(out=out, in_=in_, func=AF.Identity), reads=R, writes=W)
        else:
            P.op(eng, lambda e: e.tensor_copy(out=out, in_=in_), reads=R, writes=W)

    def recip(out, in_, R, W):
        P.op("dve", lambda e: e.reciprocal(out=out, in_=in_), reads=R, writes=W)

    def mset(ap, val, W, eng="dve"):
        P.op(eng, lambda e: e.memset(ap, val), reads=(), writes=W)

    def ld(out, in_, R, W, q="sp"):
        P.dma(q, lambda e: e.dma_start(out=out, in_=in_), reads=R, writes=W)

    ld(cst[:], consts.ap(), [], [cst_r])
    cp_(cb[:], cst[:, 0:1152], [cst_r], [cb_r])
    mset(zeros_b[:], 0.0, [zeros_r])
    for l in range(2):
        for k, (src, K, C) in wsp.items():
            for r0 in range(0, K, 128):
                P.dma("pool", lambda e, l=l, k=k, src=src, r0=r0: e.dma_start(
                    out=wb[(l, k)].ap()[r0:r0 + 128, :], in_=src.ap()[l, r0:r0 + 128, :], max_dma_last_dim=4096),
                    reads=[], writes=[wres[(l, k)]])
    ld(PTs[:], pt.ap().partition_broadcast(128), [], [PTs_r])
    ts(IDX[:], PTs[:], 128.0, iota_c, ALU.mult, ALU.add, [PTs_r, cst_r], [IDX_r])
    mset(rmask[:], 1.0, [rmask_r])
    for h in range(8):
        mset(rmask[:, h * NPG:h * NPG + 1], 0.0, [rmask_r])

    def wchunk(l, k, r0, nr, c0, ncol):
        slot, sr = ring(wring, wring_r, wr_i)
        view = slot[:, 0:nr * ncol].rearrange("p (a b) -> p a b", a=nr)
        src = wb[(l, k)].ap()[r0:r0 + nr * 128, c0:c0 + ncol].rearrange("(a p) c -> p a c", p=128)
        ld(view, src, [wres[(l, k)]], [sr])
        return view, sr

    def bcast_row(dst, dst_r, src_ap):
        ld(dst, src_ap.partition_broadcast(128), [], [dst_r])

    def layer_consts(l):
        for g in range(4):
            t1, t1r = tf()
            ld(t1[:, 0:128], w_s.ap()[l, g], [], [t1r])
            ps, pr = psum()
            tr(ps[:, 0:128], t1[:, 0:128], ident_f, [t1r, cst_r], [pr])
            tt(WT[:, g, :], ps[:, 0:128], causal_b, ALU.mult, [pr, cb_r], [WT_r])
        t1, t1r = tf()
        ld(t1[0:1, 0:512], b_s.ap()[l:l + 1].rearrange("o g t -> o (g t)"), [], [t1r])
        cp_(bsrow[:], t1[0:1, 0:512], [t1r], [bsrow_r])
        bcast_row(lc[:, 0:8], lc_r, sbb.ap()[l:l + 1, :])
        for h in range(4):
            ld(lc[:, 8 + h:9 + h], onw.ap()[l, h * 128:(h + 1) * 128].rearrange("(p o) -> p o", o=1), [], [lc_r])
            ld(lc[:, 20 + h:21 + h], w_s.ap()[l, h, 0:1, 0:1].partition_broadcast(128), [], [lc_r])
            ld(lc[:, 24 + h:25 + h], b_s.ap()[l, h:h + 1, 0:1].partition_broadcast(128), [], [lc_r])
        bcast_row(vnb[:], vnb_r, vnw.ap()[l:l + 1, :])
        if l == 0:
            mset(lbt[:], 0.0, [lbt_r]); mset(oml[:], 1.0, [oml_r])
            mset(lc[:, 12:16], 0.0, [lc_r]); mset(lc[:, 16:20], 1.0, [lc_r])
        else:
            a, ar = tf(); b, br = tf()
            bcast_row(a[:, 0:512], ar, lbl.ap()[0:1, :])
            bcast_row(b[:, 0:512], br, lbl.ap()[1:2, :])
            tt(a[:, 0:512], b[:, 0:512], a[:, 0:512], ALU.subtract, [ar, br], [ar])
            act(lbt[:], a[:, 0:512], AF.Sigmoid, [ar], [lbt_r])
            ts(oml[:], lbt[:], -1.0, 1.0, ALU.mult, ALU.add, [lbt_r], [oml_r])
            c, cr = tf()
            for h in range(4):
                ld(c[:, h:h + 1], lbl.ap()[0, h * 128:(h + 1) * 128].rearrange("(p o) -> p o", o=1), [], [cr])
                ld(c[:, 4 + h:5 + h], lbl.ap()[1, h * 128:(h + 1) * 128].rearrange("(p o) -> p o", o=1), [], [cr])
            tt(c[:, 8:12], c[:, 4:8], c[:, 0:4], ALU.subtract, [cr], [cr])
            act(lc[:, 12:16], c[:, 8:12], AF.Sigmoid, [cr], [lc_r])
            ts(lc[:, 16:20], lc[:, 12:16], -1.0, 1.0, ALU.mult, ALU.add, [lc_r], [lc_r])
        for h in range(4):
            mset(S_f[:, h, :], 0.0, [S_fr[h]])
            mset(S_b[:, h, :], 0.0, [S_br[h]])

    def compute_mod(l, csrc):
        c, cr = tf()
        ld(c[:], csrc.ap(), [], [cr])
        c2, c2r = tbf()
        act(c2[:], c[:], AF.Silu, [cr], [c2r])
        for kc in range(8):
            tr(psT[:, kc * 128:(kc + 1) * 128], c2[:, kc * 128:(kc + 1) * 128], ident_b, [c2r, cb_r], [psT_r])
        cT2, cTb_r = tbf()
        cTb = cT2[:, :].rearrange("p (a b) -> p a b", a=8)
        cp_(cTb, psT[:, :].rearrange("p (a b) -> p a b", a=8), [psT_r], [cTb_r])
        for blk in range(12):
            wv, wr = wchunk(l, "ada", 0, 8, blk * 512, 512)
            ps, pr = psum()
            for kc in range(8):
                mm(ps[:], cTb[:, kc, :], wv[:, kc, :], kc == 0, kc == 7, [cTb_r, wr], [pr])
            bb, bbr = tf()
            bcast_row(bb[:, 0:512], bbr, b_ada.ap()[l:l + 1, blk * 512:(blk + 1) * 512])
            which = blk // 2
            if which in (1, 4):
                nsrc = n1w if which == 1 else n2w
                off = (blk % 2) * 512
                bcast_row(bb[:, 512:1024], bbr, nsrc.ap()[l:l + 1, off:off + 512])
                t2, t2r = tf()
                tt(t2[:, 0:512], ps[:], bb[:, 0:512], ALU.add, [pr, bbr], [t2r])
                stt(mod[:, blk * 512:(blk + 1) * 512], t2[:, 0:512], 1.0, bb[:, 512:1024], ALU.add, ALU.mult,
                    [t2r, bbr], [mod_r])
            else:
                tt(mod[:, blk * 512:(blk + 1) * 512], ps[:], bb[:, 0:512], ALU.add, [pr, bbr], [mod_r])

    SH1, SC1, G1, SH2, SC2, G2 = 0, 1024, 2048, 3072, 4096, 5120

    def norm_to_HT(T, a_off, b_off):
        for i in range(T):
            junk, jr = tbf()
            s1, s1r = sm()
            act(junk[:], xg[:, i, :], AF.Square, [xg_r[i]], [jr, s1r], accum=s1[:, 0:1])
            act(s1[:, 1:2], s1[:, 0:1], AF.Sqrt, [s1r], [s1r], bias=EPS, scale=1.0 / D)
            recip(s1[:, 2:3], s1[:, 1:2], [s1r], [s1r])
            t1, t1r = tf()
            stt(t1[:], xg[:, i, :], s1[:, 2:3], mod[:, a_off:a_off + D], ALU.mult, ALU.mult, [xg_r[i], s1r, mod_r], [t1r])
            hb, hbr = tbf()
            tt(hb[:], t1[:], mod[:, b_off:b_off + D], ALU.add, [t1r, mod_r], [hbr])
            for kc in range(8):
                tr(psT[:, kc * 128:(kc + 1) * 128], hb[:, kc * 128:(kc + 1) * 128], ident_b, [hbr, cb_r], [psT_r])
            cp_(HT[:, :, i * 128:(i + 1) * 128], psT[:, :].rearrange("p (a b) -> p a b", a=8), [psT_r], [HT_r],
                eng="act")

    def proj_F(wv, wr, N, evac, src=None, src_r=None, nkc=8):
        src = HT if src is None else src
        src_r = HT_r if src_r is None else src_r
        for j in range(4):
            ps, pr = psum()
            for kc in range(nkc):
                mm(ps[:, 0:N], wv[:, kc, j * 128:(j + 1) * 128], src[:, kc, 0:N], kc == 0, kc == nkc - 1,
                   [wr, src_r], [pr])
            evac(j, ps, pr)

    def proj_T(wv, wr, T, evac):
        for i in range(T):
            ps, pr = psum()
            for kc in range(8):
                mm(ps[:], HT[:, kc, i * 128:(i + 1) * 128], wv[:, kc, :], kc == 0, kc == 7, [wr, HT_r], [pr])
            evac(i, ps, pr)

    def tokview(buf):
        return buf[:, :, :].rearrange("p a b -> p (a b)").rearrange("p (t c) -> p t c", c=512)

    def process_group(l, gi, sample):
        T = 1 if sample else NT
        N = T * 128
        tok0 = SEQ if sample else gi * NP
        for i in range(T):
            if l == 0:
                src = xs.ap() if sample else xp.ap()[gi * NP + i * 128: gi * NP + (i + 1) * 128, :]
                ld(xg[:, i, :], src, [], [xg_r[i]])
            else:
                ld(xg[:, i, :], x1.ap()[tok0 + i * 128: tok0 + (i + 1) * 128, :], [x1_res[tok0 // 128 + i]], [xg_r[i]])
        norm_to_HT(T, SC1, SH1)
        uT, uT_r = Pb[0], Pb_r[0]
        va, va_r = tokview(Pb[1]), Pb_r[1]

        wv, wr = wchunk(l, "in", 0, 8, 0, 512)
        proj_F(wv, wr, N, lambda j, ps, pr: act(uT[:, j, 0:N], ps[:, 0:N], AF.Gelu, [pr], [uT_r]))
        wv, wr = wchunk(l, "in", 0, 8, 512, 512)

        def ev_av(i, ps, pr):
            ga, gar = tf()
            act(ga[:, 0:512], ps[:], AF.Gelu, [pr], [gar])
            s1, s1r = sm()
            act(ga[:, 512:1024], ga[:, 0:512], AF.Square, [gar], [gar, s1r], accum=s1[:, 0:1])
            act(s1[:, 1:2], s1[:, 0:1], AF.Sqrt, [s1r], [s1r], bias=EPS, scale=1.0 / 512)
            recip(s1[:, 2:3], s1[:, 1:2], [s1r], [s1r])
            stt(ga[:, 512:1024], ga[:, 0:512], s1[:, 2:3], vnb[:], ALU.mult, ALU.mult, [gar, s1r, vnb_r], [gar])
            cp_(va[:, i, :], ga[:, 512:1024], [gar], [va_r], eng="act")
            if sample:
                P.dma("pool", lambda e: e.dma_start(out=gv.ap()[l], in_=ga[0:NS, 512:1024]), [gar], [outs_res["gv"]], store=True)
        proj_T(wv, wr, T, ev_av)
        for i in range(T):
            ps, pr = psum()
            for g in range(4):
                if sample:
                    mm(ps[:, g * 128:(g + 1) * 128], va[:, i, g * 128:(g + 1) * 128], ident_b, True, True,
                       [va_r, cb_r], [pr])
                else:
                    mm(ps[:, g * 128:(g + 1) * 128], va[:, i, g * 128:(g + 1) * 128], WT[:, g, :], True, False,
                       [va_r, WT_r], [pr])
                    mm(ps[:, g * 128:(g + 1) * 128], ones_b[0:1, :], bsrow[0:1, g * 128:(g + 1) * 128], False, True,
                       [cb_r, bsrow_r], [pr])
            if sample:
                t1, t1r = tf()
                for g in range(4):
                    ts(t1[:, g * 128:(g + 1) * 128], ps[:, g * 128:(g + 1) * 128], lc[:, 20 + g:21 + g],
                       lc[:, 24 + g:25 + g], ALU.mult, ALU.add, [pr, lc_r], [t1r])
                tt(OA[:, :, i * 128:(i + 1) * 128], t1[:, 0:512].rearrange("p (a b) -> p a b", a=4),
                   uT[:, :, i * 128:(i + 1) * 128], ALU.mult, [t1r, uT_r], [OA_r])
            else:
                tt(OA[:, :, i * 128:(i + 1) * 128], ps[:, :].rearrange("p (a b) -> p a b", a=4),
                   uT[:, :, i * 128:(i + 1) * 128], ALU.mult, [pr, uT_r], [OA_r])

        qT, qT_r = Pb[0], Pb_r[0]
        vb, vb_r = tokview(Pb[1]), Pb_r[1]
        gT, gT_r = Pb[2], Pb_r[2]
        wv, wr = wchunk(l, "in", 0, 8, 1024, 512)
        proj_F(wv, wr, N, lambda j, ps, pr: act(qT[:, j, 0:N], ps[:, 0:N], AF.Silu, [pr], [qT_r]))
        wv, wr = wchunk(l, "in", 0, 8, 1536, 512)
        fT, fT_r = fT_d, fT_dr
        if sample:
            def ev_f(j, ps, pr):
                sg, sgr = tf()
                act(sg[:, 0:128], ps[:, 0:128], AF.Sigmoid, [pr], [sgr])
                ts(fT[:, j * 128:(j + 1) * 128], sg[:, 0:128], lc[:, 16 + j:17 + j], lc[:, 12 + j:13 + j], ALU.mult,
                   ALU.add, [sgr, lc_r], [fT_r])
                ts(fT[:, 512 + j * 128:512 + (j + 1) * 128], fT[:, j * 128:(j + 1) * 128], -1.0, 1.0, ALU.mult, ALU.add,
                   [fT_r], [fT_r])
            proj_F(wv, wr, N, ev_f)
        else:
            proj_T(wv, wr, T, lambda i, ps, pr: cp_(FB[:, i, :], ps[:], [pr], [FB_r[i]], eng="act"))
        wv, wr = wchunk(l, "in", 0, 8, 2048, 512)
        if sample:
            def ev_i(i, ps, pr):
                t1, t1r = tf()
                cp_(t1[:, 0:512], ps[:], [pr], [t1r], eng="act")
                P.dma("pool", lambda e: e.dma_start(out=vscr.ap(), in_=t1[0:NS, 0:512]), [t1r], [vscr_res], store=True)
            proj_T(wv, wr, T, ev_i)
        else:
            proj_T(wv, wr, T, lambda i, ps, pr: cp_(vb[:, i, :], ps[:], [pr], [vb_r], eng="act"))
        wv, wr = wchunk(l, "in", 0, 8, 2560, 512)
        proj_F(wv, wr, N, lambda j, ps, pr: act(gT[:, j, 0:N], ps[:, 0:N], AF.Silu, [pr], [gT_r]))

        def onorm(psO, psO_r, ncols, dst, tcols):
            W4 = 4 * ncols
            o2, o2r = tbf()
            act(o2[:, 0:W4], psO[:, 0:W4], AF.Square, [psO_r], [o2r])
            pss, pssr = psum()
            mm(pss[:, 0:W4], ones_b, o2[:, 0:W4], True, True, [cb_r, o2r], [pssr])
            rs, rsr = tf()
            act(rs[:, 0:W4], pss[:, 0:W4], AF.Sqrt, [pssr], [rsr], bias=EPS, scale=1.0 / 128)
            recip(rs[:, 0:W4], rs[:, 0:W4], [rsr], [rsr])
            for h in range(4):
                stt(rs[:, 512 + h * ncols:512 + (h + 1) * ncols], psO[:, h * ncols:(h + 1) * ncols], lc[:, 8 + h:9 + h],
                    rs[:, h * ncols:(h + 1) * ncols], ALU.mult, ALU.mult, [psO_r, lc_r, rsr], [rsr])
            tt(dst, rs[:, 512:512 + W4].rearrange("p (a b) -> p a b", a=4), tcols, ALU.mult, [rsr, gT_r], [OB_r])

        if sample:
            mset(OB[:, :, :], 0.0, [OB_r])
            for b in range(NS):
                Sb, Sbr = Sb_t[0], Sb_r[0]
                ld(Sb[:], st.ap()[l, b].rearrange("h k v -> k h v"), [], [Sbr])
                vB, vBr = QB[b % 2], QB_r[b % 2]
                ld(vB, vscr.ap()[b:b + 1, :].partition_broadcast(128), [vscr_res], [vBr])
                sn, snr = tf()
                for h in range(4):
                    ts(sn[:, 512 + h * 128:512 + (h + 1) * 128], vB[:, h * 128:(h + 1) * 128],
                       fT[:, 512 + h * 128 + b:512 + h * 128 + b + 1], None, ALU.mult, None, [vBr, fT_r], [snr])
                    stt(sn[:, h * 128:(h + 1) * 128], Sb[:, h, :], fT[:, h * 128 + b:h * 128 + b + 1],
                        sn[:, 512 + h * 128:512 + (h + 1) * 128], ALU.mult, ALU.add, [Sbr, fT_r, snr], [snr])
                P.dma("pool", lambda e, b=b, sn=sn: e.dma_start(
                    out=Ss.ap()[l, b].rearrange("h k v -> k h v"),
                    in_=sn[:, 0:512].rearrange("p (a b) -> p a b", a=4)), [snr], [outs_res["Ss"]], store=True)
                snb, snbr = tbf()
                cp_(snb[:, 0:512], sn[:, 0:512], [snr], [snbr], eng="act")
                for h in range(4):
                    mm(psAcc[:, h * NS + b:h * NS + b + 1], snb[:, h * 128:(h + 1) * 128], qT[:, h, b:b + 1], True, True,
                       [snbr, qT_r], [psAcc_r])
            onorm(psAcc, psAcc_r, NS, OB[:, :, 0:NS], gT[:, :, 0:NS])
        else:
            for i in range(T):
                sg, sgr = tf()
                act(sg[:, 0:512], FB[:, i, :], AF.Sigmoid, [FB_r[i]], [sgr])
                tt(sg[:, 0:512], sg[:, 0:512], oml[:], ALU.mult, [sgr, oml_r], [sgr])
                tt(sg[:, 0:512], sg[:, 0:512], lbt[:], ALU.add, [sgr, lbt_r], [sgr])
                act(sg[:, 512:1024], sg[:, 0:512], AF.Ln, [sgr], [sgr])
                lf = sg[:, 512:1024]
                psb, psbr = psum()
                mm(psb[:], bmid_f, lf, True, True, [cst_r, sgr], [psbr])
                e1, e1r = tf()
                ts(e1[:, 0:512], psb[:], -80.0, 80.0, ALU.max, ALU.min, [psbr], [e1r])
                act(e1[:, 0:512], e1[:, 0:512], AF.Exp, [e1r], [e1r], scale=-1.0)
                ts(e1[:, 512:1024], sg[:, 0:512], -1.0, 1.0, ALU.mult, ALU.add, [sgr], [e1r])
                kh, khr = tbf()
                tt(kh[:, 0:512], e1[:, 512:1024], e1[:, 0:512], ALU.mult, [e1r], [khr])
                psr_, psrr = psum()
                mm(psr_[:], rev32_f, lf, True, True, [cst_r, sgr], [psrr])
                e2, e2r = tf()
                act(e2[:, 0:512], psr_[:], AF.Exp, [psrr], [e2r])
                kd, kdr = tbf()
                tt(kd[:, 0:512], e1[:, 512:1024], e2[:, 0:512], ALU.mult, [e1r, e2r], [kdr])
                for h in range(4):
                    tr(psT[:, h * 128:(h + 1) * 128], kh[:, h * 128:(h + 1) * 128], ident_b, [khr, cb_r], [psT_r])
                cp_(kh[:, 512:1024], psT[:, 0:512], [psT_r], [khr], eng="act")
                psbt, psbtr = psum()
                for h in range(4):
                    mm(psbt[:, h * 128:(h + 1) * 128], sg[:, 512 + h * 128:512 + (h + 1) * 128], bmid_f, True, True,
                       [sgr, cst_r], [psbtr])
                eb, ebr = tf()
                ts(eb[:, 0:512], psbt[:], -80.0, 80.0, ALU.max, ALU.min, [psbtr], [ebr])
                act(eb[:, 0:512], eb[:, 0:512], AF.Exp, [ebr], [ebr])
                psbs, psbsr = psum()
                for h in range(4):
                    mm(psbs[:, h * 128:(h + 1) * 128], sg[:, 512 + h * 128:512 + (h + 1) * 128], bt32_f, True, True,
                       [sgr, cst_r], [psbsr])
                act(eb[:, 512:1024], psbs[:], AF.Exp, [psbsr], [ebr])
                ql, qlr = tbf()
                tt(ql[:, 0:512].rearrange("p (a b) -> p a b", a=4), qT[:, :, i * 128:(i + 1) * 128],
                   eb[:, 0:512].rearrange("p (a b) -> p a b", a=4), ALU.mult, [qT_r, ebr], [qlr])
                qs, qsr = tbf()
                tt(qs[:, 0:512].rearrange("p (a b) -> p a b", a=4), qT[:, :, i * 128:(i + 1) * 128],
                   eb[:, 512:1024].rearrange("p (a b) -> p a b", a=4), ALU.mult, [qT_r, ebr], [qsr])
                psA, psAr = psum()
                for h in range(4):
                    mm(psA[:, h * 128:(h + 1) * 128], kh[:, 512 + h * 128:512 + (h + 1) * 128],
                       ql[:, h * 128:(h + 1) * 128], True, True, [khr, qlr], [psAr])
                at_, atr = tf()
                ts(at_[:, 0:512], psA[:], 1e30, -1e30, ALU.min, ALU.max, [psAr], [atr])
                tt(ql[:, 512:1024], at_[:, 0:512], bt32x4_b, ALU.mult, [atr, cb_r], [qlr])
                vm = []
                for jj in range(2):
                    vmt, vmr = tbf()
                    for j2 in range(2):
                        j = jj * 2 + j2
                        ts(vmt[:, j2 * 512:(j2 + 1) * 512], vb[:, i, :], submask[:, j:j + 1], None, ALU.mult, None,
                           [vb_r, cst_r], [vmr])
                        vm.append((vmt[:, j2 * 512:(j2 + 1) * 512], vmr))
                for h in range(4):
                    hs = slice(h * 128, (h + 1) * 128)
                    mm(psAcc[:, hs], vb[:, i, hs], ql[:, 512 + h * 128:512 + (h + 1) * 128], True, False,
                       [vb_r, qlr], [psAcc_r])
                    for j in range(4):
                        c0 = h * 128 + j * 32
                        mm(psAcc[:, c0:c0 + 32], S_b[:, h, :], qs[:, c0:c0 + 32], False, j == 3,
                           [S_br[h], qsr], [psAcc_r])
                        psd, psdr = psum()
                        mm(psd[:, 0:128], kd[:, hs], vm[j][0][:, hs], True, True, [kdr, vm[j][1]], [psdr])
                        ecol = 512 + c0 + 31
                        stt(S_f[:, h, :], S_f[:, h, :], eb[:, ecol:ecol + 1], psd[:, 0:128], ALU.mult, ALU.add,
                            [S_fr[h], ebr, psdr], [S_fr[h]])
                        cp_(S_b[:, h, :], S_f[:, h, :], [S_fr[h]], [S_br[h]], eng="act")
                onorm(psAcc, psAcc_r, 128, OB[:, :, i * 128:(i + 1) * 128], gT[:, :, i * 128:(i + 1) * 128])
            if gi == NG - 1:
                P.dma("pool", lambda e: e.dma_start(out=Sp.ap()[l].rearrange("h k v -> k h v"), in_=S_f[:]),
                      S_fr, [outs_res["Sp"]], store=True)

        cqT, cqT_r = Pb[0], Pb_r[0]
        wv, wr = wchunk(l, "in", 0, 8, 3072, 512)
        if sample:
            def ev_q(i, ps, pr):
                t1, t1r = tf()
                cp_(t1[:, 0:512], ps[:], [pr], [t1r], eng="act")
                P.dma("pool", lambda e: e.dma_start(out=qscr.ap(), in_=t1[0:NS, 0:512]), [t1r], [qscr_res], store=True)
            proj_T(wv, wr, T, ev_q)
        else:
            proj_F(wv, wr, N, lambda j, ps, pr: cp_(cqT[:, j, 0:N], ps[:, 0:N], [pr], [cqT_r], eng="act"))
        wv, wr = wchunk(l, "in", 0, 8, 3584, 512)

        def ev_k(i, ps, pr):
            kf, kfr = tf()
            cp_(kf[:, 0:512], ps[:], [pr], [kfr], eng="act")
            if sample:
                P.dma("pool", lambda e: e.dma_start(out=ks.ap()[l], in_=kf[0:NS, 0:512]), [kfr], [outs_res["ks"]], store=True)
            else:
                ti = gi * NT + i
                P.dma("pool", lambda e: e.dma_start(out=kp.ap()[l, ti * 128:(ti + 1) * 128, :], in_=kf[:, 0:512]),
                      [kfr], [outs_res["kp"]], store=True)
                ps2, p2r = psum()
                for j in range(4):
                    tr(ps2[:, j * 128:(j + 1) * 128], kf[:, j * 128:(j + 1) * 128], ident_f, [kfr, cst_r], [p2r])
                cp_(KT[:, :, ti * 128:(ti + 1) * 128], ps2[:, :].rearrange("p (a b) -> p a b", a=4), [p2r], [KT_r[ti]])
        proj_T(wv, wr, T, ev_k)
        wv, wr = wchunk(l, "in", 0, 8, 4096, 512)

        def ev_v(i, ps, pr):
            vf, vfr = tf()
            cp_(vf[:, 0:512], ps[:], [pr], [vfr], eng="act")
            if sample:
                P.dma("pool", lambda e: e.dma_start(out=vs.ap()[l], in_=vf[0:NS, 0:512]), [vfr], [outs_res["vs"]], store=True)
            else:
                ti = gi * NT + i
                P.dma("pool", lambda e: e.dma_start(out=vp.ap()[l, ti * 128:(ti + 1) * 128, :], in_=vf[:, 0:512]),
                      [vfr], [outs_res["vp"]], store=True)
                cp_(VH[:, ti, :], vf[:, 0:512], [vfr], [VH_r[ti]])
        proj_T(wv, wr, T, ev_v)

        if sample:
            mset(OC[:, :, :], 0.0, [OC_r])
            for b in range(NS):
                qb, qbr = QB[b % 2], QB_r[b % 2]
                ld(qb, qscr.ap()[b:b + 1, :].partition_broadcast(128), [qscr_res], [qbr])
                for p in range(NPG):
                    col = b * NPG + p
                    r = NPG - 1 - p
                    kpg_t, kpr = tf()
                    kpg = kpg_t[:, 0:512]
                    P.dma("pool", lambda e, kpg=kpg, col=col: e.indirect_dma_start(
                        out=kpg, out_offset=None, in_=ck.ap().rearrange("l r c -> (l r) c"),
                        in_offset=bass.IndirectOffsetOnAxis(ap=IDX[:, col:col + 1], axis=0),
                        element_offset=l * NPOOL * 128 * 512), [IDX_r], [kpr])
                    pr_, prr = tf()
                    tt(pr_[:, 0:512], kpg, qb, ALU.mult, [kpr, qbr], [prr])
                    P.op("dve", lambda e, pr_=pr_, r=r: e.tensor_reduce(
                        out=zb[:, :, r], in_=pr_[:, 0:512].rearrange("p (a b) -> p a b", a=8), axis=AX.X, op=ALU.add),
                        reads=[prr], writes=[zb_r])
                    P.dma("pool", lambda e, p=p, col=col: e.indirect_dma_start(
                        out=Vpg[:, p, :], out_offset=None, in_=cv.ap().rearrange("l r c -> (l r) c"),
                        in_offset=bass.IndirectOffsetOnAxis(ap=IDX[:, col:col + 1], axis=0),
                        element_offset=l * NPOOL * 128 * 512), [IDX_r], [Vpg_r[p]])
                for h in range(8):
                    act(eb_s[:, h, :], zb[:, h, :], AF.Exp, [zb_r, lc_r], [eb_sr], bias=lc[:, h:h + 1], scale=0.125)
                act(spb_s[:], eb_s[:, :, :].rearrange("p a b -> p (a b)"), AF.Ln, [eb_sr], [spb_sr], bias=1.0)
                psg, psgr = psum()
                mm(psg[:, 0:8 * NPG], tri_b, spb_s[:], True, True, [cb_r, spb_sr], [psgr])
                pst, pstr = psum()
                mm(pst[:, 0:8 * NPG], ones_b, spb_s[:], True, True, [cb_r, spb_sr], [pstr])
                t1, t1r = tf()
                W8 = 8 * NPG
                P.op("dve", lambda e, t1=t1, pst=pst: e.tensor_tensor_scan(
                    out=t1[:, 0:W8], data0=rmask[:], data1=pst[:, 0:W8], initial=0.0, op0=ALU.mult, op1=ALU.add),
                    reads=[rmask_r, pstr], writes=[t1r])
                tt(t1[:, 0:W8], t1[:, 0:W8], pst[:, 0:W8], ALU.subtract, [t1r, pstr], [t1r])
                tt(t1[:, 0:W8], t1[:, 0:W8], psg[:, 0:W8], ALU.add, [t1r, psgr], [t1r])
                act(t1[:, 0:W8], t1[:, 0:W8], AF.Exp, [t1r], [t1r], scale=-1.0)
                tt(wb_s[:, :, :].rearrange("p a b -> p (a b)"), t1[:, 0:W8],
                   eb_s[:, :, :].rearrange("p a b -> p (a b)"), ALU.mult, [t1r, eb_sr], [wb_sr])
                for h in range(8):
                    hp, po = h // 2, (h % 2) * 64
                    for p in range(NPG):
                        r = NPG - 1 - p
                        mm(psAcc2[po:po + 64, hp * NS + b:hp * NS + b + 1], Vpg[:, p, h * 64:(h + 1) * 64],
                           wb_s[:, h, r:r + 1], p == 0, p == NPG - 1, [Vpg_r[p], wb_sr], [psAcc2_r])
            cp_(OC[:, :, 0:NS], psAcc2[:, 0:4 * NS].rearrange("p (a b) -> p a b", a=4), [psAcc2_r], [OC_r], eng="act")
        else:
            last = gi * NT + NT - 1
            pack = N <= 256
            steps = list(range(last, -1, -1))
            for q in range(4):
                heads = [2 * q, 2 * q + 1]
                accs = {}
                for h in heads:
                    hp, po = h // 2, (h % 2) * 64
                    acc, accr = (psAcc2, psAcc2_r) if (q % 2) == 0 else (psAcc, psAcc_r)
                    accs[h] = (acc, accr)
                    mset(Rf4[:, h % 4, :], 0.0, [Rf4_r[h % 4]], eng="pool")
                    mset(Rb4[:, h % 4, :], 0.0, [Rb4_r[h % 4]], eng="pool")
                    mm(acc[po:po + 64, 0:N], zeros_b[:, 0:64], zeros_b[:, 0:N], True, False, [zeros_r], [accr])
                ctx = {}

                def geom(kt):
                    jq = max(0, kt - gi * NT)
                    return jq * 128, kt >= gi * NT, kt == last

                def S1(kt):
                    c0, diag, first = geom(kt)
                    stg = {}
                    for h in heads:
                        hp, po = h // 2, (h % 2) * 64
                        psz, pszr = psum()
                        if pack:
                            psg, psgr, go = psz, pszr, 256
                        else:
                            psg, psgr = psum()
                            go = 0
                        mm(psz[:, c0:N], KT[po:po + 64, hp, kt * 128:(kt + 1) * 128], cqT[po:po + 64, hp, c0:N],
                           True, True, [KT_r[kt], cqT_r], [pszr])
                        e_, er = tf()
                        act(e_[:, c0:N], psz[:, c0:N], AF.Exp, [pszr, lc_r], [er], bias=lc[:, h:h + 1], scale=0.125)
                        sp_, spr = tbf()
                        act(sp_[:, c0:N], e_[:, c0:N], AF.Ln, [er], [spr], bias=1.0)
                        if diag:
                            tt(sp_[:, c0:c0 + 128], sp_[:, c0:c0 + 128], mstrict_b, ALU.mult, [spr, cb_r], [spr])
                        stg[h] = (psg, psgr, go, e_, er, sp_, spr)
                    ctx[kt] = stg

                def S2(kt):
                    c0, diag, first = geom(kt)
                    for h in heads:
                        psg, psgr, go, e_, er, sp_, spr = ctx[kt][h]
                        Rb, Rb_r = Rb4[:, h % 4, :], Rb4_r[h % 4]
                        mm(psg[:, go + c0:go + N], tri_b, sp_[:, c0:N], True, first, [cb_r, spr], [psgr])
                        if not first:
                            mm(psg[:, go + c0:go + N], ones_b, Rb[:, c0:N], False, True, [cb_r, Rb_r], [psgr])
                        act(psg[:, go + c0:go + N], psg[:, go + c0:go + N], AF.Exp, [psgr], [psgr], scale=-1.0)

                def S3(kt):
                    c0, diag, first = geom(kt)
                    for h in heads:
                        hp, po = h // 2, (h % 2) * 64
                        acc, accr = accs[h]
                        psg, psgr, go, e_, er, sp_, spr = ctx[kt][h]
                        Rf, Rf_r, Rb, Rb_r = Rf4[:, h % 4, :], Rf4_r[h % 4], Rb4[:, h % 4, :], Rb4_r[h % 4]
                        tt(sp_[:, 512 + c0:512 + N], e_[:, c0:N], psg[:, go + c0:go + N], ALU.mult, [er, psgr], [spr])
                        if diag:
                            tt(sp_[:, 512 + c0:512 + c0 + 128], sp_[:, 512 + c0:512 + c0 + 128], mstrict_b, ALU.mult,
                               [spr, cb_r], [spr])
                        mm(acc[po:po + 64, c0:N], VH[:, kt, h * 64:(h + 1) * 64], sp_[:, 512 + c0:512 + N], False,
                           kt == 0, [VH_r[kt], spr], [accr])
                        if kt > 0:
                            tt(Rf[:, c0:N], Rf[:, c0:N], sp_[:, c0:N], ALU.add, [Rf_r, spr], [Rf_r])
                            cp_(Rb[:, c0:N], Rf[:, c0:N], [Rf_r], [Rb_r])
                    del ctx[kt]

                S1(steps[0])
                for i_, kt in enumerate(steps):
                    if i_ + 1 < len(steps):
                        S1(steps[i_ + 1])
                    S2(kt)
                    S3(kt)
                for h in heads:
                    hp, po = h // 2, (h % 2) * 64
                    acc, accr = accs[h]
                    cp_(OC[po:po + 64, hp, 0:N], acc[po:po + 64, 0:N], [accr], [OC_r], eng="act")

        for c in range(6):
            wv, wr = wchunk(l, "in", 0, 8, 4608 + c * 512, 512)
            proj_F(wv, wr, N, lambda j, ps, pr, c=c: act(BIG[:, c * 4 + j, 0:N], ps[:, 0:N], AF.Sigmoid, [pr],
                                                        [BIG_r[c * 4 + j]]))
        for xi, (wk, osrc, osr) in enumerate([("ba", OA, OA_r), ("bb", OB, OB_r), ("bc", OC, OC_r)]):
            for half in range(2):
                wv, wr = wchunk(l, wk, 0, 4, half * 512, 512)

                def ev_b(j, ps, pr, xi=xi, half=half):
                    jj = half * 4 + j
                    gate = BIG[:, xi * 8 + jj, 0:N]
                    if xi == 0:
                        tt(HT[:, jj, 0:N], ps[:, 0:N], gate, ALU.mult, [pr, BIG_r[xi * 8 + jj]], [HT_r])
                    else:
                        t1, t1r = tf()
                        tt(t1[:, 0:N], ps[:, 0:N], gate, ALU.mult, [pr, BIG_r[xi * 8 + jj]], [t1r])
                        tt(HT[:, jj, 0:N], HT[:, jj, 0:N], t1[:, 0:N], ALU.add, [HT_r, t1r], [HT_r])
                proj_F(wv, wr, N, ev_b, src=osrc, src_r=osr, nkc=4)
        for half in range(2):
            wv, wr = wchunk(l, "out", 0, 8, half * 512, 512)

            def ev_o(i, ps, pr, half=half):
                t1, t1r = tf()
                tt(t1[:, 0:512], ps[:], mod[:, G1 + half * 512:G1 + (half + 1) * 512], ALU.mult, [pr, mod_r], [t1r])
                tt(xg[:, i, half * 512:(half + 1) * 512], xg[:, i, half * 512:(half + 1) * 512], t1[:, 0:512], ALU.add,
                   [xg_r[i], t1r], [xg_r[i]])
            proj_T(wv, wr, T, ev_o)
        norm_to_HT(T, SC2, SH2)
        for half in range(2):
            for c in range(4):
                wv, wr = wchunk(l, "ff1", 0, 8, half * 2048 + c * 512, 512)

                def ev_f1(j, ps, pr, c=c):
                    r_, rr = tbf()
                    act(r_[:, 0:N], ps[:, 0:N], AF.Relu, [pr], [rr])
                    tt(BIG[:, c * 4 + j, 0:N], r_[:, 0:N], r_[:, 0:N], ALU.mult, [rr], [BIG_r[c * 4 + j]], eng="pool")
                proj_F(wv, wr, N, ev_f1)
            for c in range(4):
                slot, sr = ring(wring, wring_r, wr_i)
                view = slot[:, :].rearrange("p (a b) -> p a b", a=16)
                src = wb[(l, "ff2")].ap()[half * 2048:(half + 1) * 2048, c * 256:(c + 1) * 256].rearrange(
                    "(a p) c -> p a c", p=128)
                ld(view, src, [wres[(l, "ff2")]], [sr])
                for i in range(T):
                    ps, pr = psum()
                    for kc in range(16):
                        mm(ps[:, 0:256], BIG[:, kc, i * 128:(i + 1) * 128], view[:, kc, :], kc == 0, kc == 15,
                           [BIG_r[kc], sr], [pr])
                    t1, t1r = tf()
                    tt(t1[:, 0:256], ps[:, 0:256], mod[:, G2 + c * 256:G2 + (c + 1) * 256], ALU.mult, [pr, mod_r], [t1r])
                    tt(xg[:, i, c * 256:(c + 1) * 256], xg[:, i, c * 256:(c + 1) * 256], t1[:, 0:256], ALU.add,
                       [xg_r[i], t1r], [xg_r[i]])
        for i in range(T):
            if l == 0:
                P.dma("pool", lambda e, i=i: e.dma_start(out=x1.ap()[tok0 + i * 128: tok0 + (i + 1) * 128, :],
                                                         in_=xg[:, i, :]), [xg_r[i]], [x1_res[tok0 // 128 + i]], store=True)
            else:
                junk, jr = tbf()
                s1, s1r = sm()
                act(junk[:], xg[:, i, :], AF.Square, [xg_r[i]], [jr, s1r], accum=s1[:, 0:1])
                act(s1[:, 1:2], s1[:, 0:1], AF.Sqrt, [s1r], [s1r], bias=EPS, scale=1.0 / D)
                recip(s1[:, 2:3], s1[:, 1:2], [s1r], [s1r])
                fb_, fbr = tf()
                bcast_row(fb_[:], fbr, fnw.ap())
                y_, yr = tf()
                stt(y_[:], xg[:, i, :], s1[:, 2:3], fb_[:], ALU.mult, ALU.mult, [xg_r[i], s1r, fbr], [yr])
                if sample:
                    P.dma("pool", lambda e, y_=y_: e.dma_start(out=ys.ap(), in_=y_[0:NS, :]), [yr], [outs_res["ys"]], store=True)
                else:
                    P.dma("pool", lambda e, y_=y_, i=i: e.dma_start(
                        out=yp.ap()[gi * NP + i * 128: gi * NP + (i + 1) * 128, :], in_=y_[:]), [yr], [outs_res["yp"]], store=True)

    for l in range(2):
        layer_consts(l)
        compute_mod(l, cp)
        for gi in range(NG):
            process_group(l, gi, False)
        compute_mod(l, cs)
        process_group(l, 0, True)
    P.wait_all("pool", list(outs_res.values()))
    es.close()
    return nc


_CACHE = {}


def _consts():
    c = np.zeros((128, 1413), np.float32)
    i = np.arange(128)
    s, t = i[:, None], i[None, :]
    c[:, 0:128] = np.eye(128)
    c[:, 128:256] = (s >= t)
    c[:, 256:384] = 1.0
    c[:, 384:512] = (s < t)
    same = (s // 32) == (t // 32)
    bt = ((s <= t) & same).astype(np.float32)
    mid = (t // 32) * 32 + 15
    c[:, 1153:1281] = (((s <= t) & same).astype(np.float32) - ((s <= mid) & same).astype(np.float32))
    c[:, 1281:1409] = ((s > t) & same)
    for j in range(4):
        c[:, 1409 + j] = ((i // 32) == j)
    c[:, 512:1024] = np.tile(bt, (1, 4))
    c[:, 1024:1152] = (s <= t)
    c[:, 1152] = i
    return c


def kernel(x_prompt, x_sample, c_prompt, c_sample, cache_k, cache_v, state_hgrn, page_table,
           w_ada, b_ada, norm1_w, norm2_w, w_in, gmlp_vnorm_w, gmlp_w_s, gmlp_b_s, sb_bias,
           hgrn_lb_logits, hgrn_onorm_w, w_branch_a, w_branch_b, w_branch_c, w_out,
           w_ff1, w_ff2, final_norm_w, NT=2):
    f = lambda a: np.ascontiguousarray(np.asarray(a, dtype=np.float32))
    x_prompt = f(x_prompt); x_sample = f(x_sample)
    B, SEQ, _ = x_prompt.shape
    DB = x_sample.shape[0]
    NPG = page_table.shape[1]
    NPOOL = cache_k.shape[1]
    assert DB == 8 * NS and B == 4
    key = (SEQ, NPG, NPOOL, NT)
    if key not in _CACHE:
        _CACHE[key] = build(SEQ, NPG, NPOOL, NT)
    nc = _CACHE[key]
    ckf = f(cache_k).reshape(2, NPOOL * 128, 512)
    cvf = f(cache_v).reshape(2, NPOOL * 128, 512)
    consts = _consts()
    shared = dict(ck=ckf, cv=cvf, consts=consts, w_ada=f(w_ada), b_ada=f(b_ada), n1w=f(norm1_w), n2w=f(norm2_w),
                  w_in=f(w_in), vnw=f(gmlp_vnorm_w), w_s=f(gmlp_w_s), b_s=f(gmlp_b_s), sbb=f(sb_bias),
                  lbl=f(hgrn_lb_logits), onw=f(hgrn_onorm_w), w_ba=f(w_branch_a), w_bb=f(w_branch_b),
                  w_bc=f(w_branch_c), w_out=f(w_out), w_ff1=f(w_ff1), w_ff2=f(w_ff2),
                  fnw=f(final_norm_w).reshape(1, D))
    cpf = f(c_prompt); csf = f(c_sample); stf = f(state_hgrn)
    pti = np.ascontiguousarray(np.asarray(page_table, dtype=np.int32))
    in_maps = []
    for c in range(8):
        b = c % 4
        xs_t = np.zeros((128, D), np.float32); xs_t[:NS] = x_sample[c * NS:(c + 1) * NS, 0, :]
        cs_t = np.zeros((128, D), np.float32); cs_t[:NS] = csf[c * NS:(c + 1) * NS]
        m = dict(shared)
        m.update(xp=x_prompt[b], xs=xs_t, cp=np.ascontiguousarray(np.broadcast_to(cpf[b], (128, D))), cs=cs_t,
                 st=np.ascontiguousarray(stf[:, c * NS:(c + 1) * NS]),
                 pt=np.ascontiguousarray(pti[c * NS:(c + 1) * NS].reshape(1, NS * NPG)))
        in_maps.append(m)
    res = run_bass_kernel_spmd(nc, in_maps, core_ids=list(range(8))).results
    y_prompt = np.stack([res[b]["yp"] for b in range(4)])
    y_sample = np.concatenate([res[c]["ys"] for c in range(8)])[:, None, :]
    k_prompt = np.stack([res[b]["kp"] for b in range(4)], axis=1).reshape(2, 4, SEQ, 8, 64)
    v_prompt = np.stack([res[b]["vp"] for b in range(4)], axis=1).reshape(2, 4, SEQ, 8, 64)
    S_prompt = np.stack([res[b]["Sp"] for b in range(4)], axis=1)
    k_sample = np.concatenate([res[c]["ks"] for c in range(8)], axis=1).reshape(2, DB, 1, 8, 64)
    v_sample = np.concatenate([res[c]["vs"] for c in range(8)], axis=1).reshape(2, DB, 1, 8, 64)
    S_sample = np.concatenate([res[c]["Ss"] for c in range(8)], axis=1)
    g_sample = np.concatenate([res[c]["gv"] for c in range(8)], axis=1)[:, :, None, :]
    return (y_prompt, y_sample, k_prompt, v_prompt, S_prompt, k_sample, v_sample, S_sample, g_sample)
```

```python
import numpy as np
from contextlib import ExitStack
import concourse.bass as bass
import concourse.mybir as mybir
from concourse.bass_utils import run_bass_kernel_spmd

F32 = mybir.dt.float32
BF16 = mybir.dt.bfloat16
I32 = mybir.dt.int32
AF = mybir.ActivationFunctionType
ALU = mybir.AluOpType
AX = mybir.AxisListType
D = 1024
EPS = 1e-6
NS = 16


class Res:
    __slots__ = ("name", "w", "rs")

    def __init__(self, name):
        self.name = name
        self.w = None
        self.rs = {}


class Prog:
    def __init__(self, nc, es):
        self.nc = nc
        self.es = es
        self.eng = {"pe": nc.tensor, "act": nc.scalar, "dve": nc.vector, "pool": nc.gpsimd, "sp": nc.sync}
        self.esem = {k: es.enter_context(nc.semaphore("e_" + k)) for k in self.eng}
        self.cnt = {k: 0 for k in self.eng}
        self.seen = {k: {} for k in self.eng}
        self.dsem = {}
        self.dcnt = {}
        self.sems = {}
        self.store_sems = {}

    def _waits(self, e, reads, writes, skip=None, selfsync=True):
        need = {}

        def add(tok):
            if tok is None:
                return
            s, v = tok
            if s is skip:
                return
            if need.get(id(s), (None, 0))[1] < v:
                need[id(s)] = (s, v)

        for r in reads:
            add(r.w)
        for w in writes:
            add(w.w)
            for t in w.rs.values():
                add(t)
        eng = self.eng[e]
        for sid, (s, v) in need.items():
            if (not selfsync) and s is self.esem[e]:
                continue
            if self.seen[e].get(sid, 0) >= v:
                continue
            eng.wait_ge(s, v)
            self.seen[e][sid] = v

    def _commit(self, tok, reads, writes):
        for r in reads:
            r.rs[id(tok[0])] = tok
        for w in writes:
            w.w = tok
            w.rs = {}

    def op(self, e, fn, reads=(), writes=(), selfsync=True):
        self._waits(e, reads, writes, selfsync=selfsync)
        ins = fn(self.eng[e])
        self.cnt[e] += 1
        ins.then_inc(self.esem[e], 1)
        self._commit((self.esem[e], self.cnt[e]), reads, writes)

    def dma(self, q, fn, reads, writes, store=False):
        owner = reads[0] if store else writes[0]
        key = (id(owner), store)
        if key not in self.dsem:
            self.dsem[key] = self.es.enter_context(self.nc.semaphore(("s_" if store else "d_") + owner.name))
            self.dcnt[key] = 0
        sem = self.dsem[key]
        self._waits(q, reads, writes, skip=sem)
        ins = fn(self.eng[q])
        self.dcnt[key] += 16
        ins.then_inc(sem, 16)
        if store:
            self.store_sems[id(sem)] = (sem, self.dcnt[key])
        self._commit((sem, self.dcnt[key]), reads, writes)

    def wait_all(self, e, ress):
        self._waits(e, ress, ress)
        for sid, (sem, v) in self.store_sems.items():
            if self.seen[e].get(sid, 0) < v:
                self.eng[e].wait_ge(sem, v)
                self.seen[e][sid] = v


def build(SEQ, NPG, NPOOL, NT):
    nc = bass.Bass("TRN2", target_bir_lowering=False)
    es = ExitStack()
    P = Prog(nc, es)
    NTILES = SEQ // 128
    NG = NTILES // NT
    NP = NT * 128
    NCOL = NS * NPG

    def din(name, shape, dt=F32):
        return nc.dram_tensor(name, list(shape), dt, kind="ExternalInput")

    def dout(name, shape, dt=F32):
        return nc.dram_tensor(name, list(shape), dt, kind="ExternalOutput")

    def dscr(name, shape, dt):
        return nc.dram_tensor(name, list(shape), dt)

    def SB(name, shape, dt):
        return es.enter_context(nc.sbuf_tensor(name, list(shape), dt))

    def PS(name, shape, dt):
        return es.enter_context(nc.psum_tensor(name, list(shape), dt))

    xp = din("xp", [SEQ, D]); xs = din("xs", [128, D]); cp = din("cp", [128, D]); cs = din("cs", [128, D])
    ck = din("ck", [2, NPOOL * 128, 512]); cv = din("cv", [2, NPOOL * 128, 512])
    st = din("st", [2, NS, 4, 128, 128]); pt = din("pt", [1, NCOL], I32)
    consts = din("consts", [128, 1413])
    w_ada = din("w_ada", [2, D, 6144]); b_ada = din("b_ada", [2, 6144])
    n1w = din("n1w", [2, D]); n2w = din("n2w", [2, D]); w_in = din("w_in", [2, D, 7680])
    vnw = din("vnw", [2, 512]); w_s = din("w_s", [2, 4, 128, 128]); b_s = din("b_s", [2, 4, 128])
    sbb = din("sbb", [2, 8]); lbl = din("lbl", [2, 512]); onw = din("onw", [2, 512])
    w_ba = din("w_ba", [2, 512, D]); w_bb = din("w_bb", [2, 512, D]); w_bc = din("w_bc", [2, 512, D])
    w_out = din("w_out", [2, D, D]); w_ff1 = din("w_ff1", [2, D, 4096]); w_ff2 = din("w_ff2", [2, 4096, D])
    fnw = din("fnw", [1, D])

    yp = dout("yp", [SEQ, D]); ys = dout("ys", [NS, D])
    kp = dout("kp", [2, SEQ, 512]); vp = dout("vp", [2, SEQ, 512]); Sp = dout("Sp", [2, 4, 128, 128])
    ks = dout("ks", [2, NS, 512]); vs = dout("vs", [2, NS, 512]); Ss = dout("Ss", [2, NS, 4, 128, 128])
    gv = dout("gv", [2, NS, 512])
    outs_res = {n: Res(n) for n in ["yp", "ys", "kp", "vp", "Sp", "ks", "vs", "Ss", "gv"]}

    wsp = {"ada": (w_ada, D, 6144), "in": (w_in, D, 7680), "ba": (w_ba, 512, D), "bb": (w_bb, 512, D),
           "bc": (w_bc, 512, D), "out": (w_out, D, D), "ff1": (w_ff1, D, 4096), "ff2": (w_ff2, 4096, D)}
    wb = {}
    wres = {}
    for l in range(2):
        for k, (src, K, C) in wsp.items():
            wb[(l, k)] = dscr(f"wb_{k}{l}", [K, C], BF16)
            wres[(l, k)] = Res(f"wb_{k}{l}")
    x1 = dscr("x1", [SEQ + 128, D], F32); x1_res = [Res(f"x1_{t}") for t in range(SEQ // 128 + 1)]
    qscr = dscr("qscr", [NS, 512], F32); qscr_res = Res("qscr")
    vscr = dscr("vscr", [NS, 512], F32); vscr_res = Res("vscr")

    cst = SB("cst", [128, 1413], F32); cst_r = Res("cst")
    ident_f = cst[:, 0:128]; iota_c = cst[:, 1152:1153]
    cb = SB("cb", [128, 1152], BF16); cb_r = Res("cb")
    ident_b = cb[:, 0:128]; tri_b = cb[:, 128:256]; ones_b = cb[:, 256:384]; mstrict_b = cb[:, 384:512]
    bt32x4_b = cb[:, 512:1024]; causal_b = cb[:, 1024:1152]
    bt32_f = cst[:, 512:640]; bmid_f = cst[:, 1153:1281]; rev32_f = cst[:, 1281:1409]; submask = cst[:, 1409:1413]
    zeros_b = SB("zeros_b", [128, NP], BF16); zeros_r = Res("zeros")

    KT = SB("KT", [128, 4, SEQ], BF16); KT_r = [Res(f"KT{t}") for t in range(NTILES)]
    VH = SB("VH", [128, NTILES, 512], BF16); VH_r = [Res(f"VH{t}") for t in range(NTILES)]
    xg = SB("xg", [128, NT, D], F32); xg_r = [Res(f"xg{i}") for i in range(NT)]
    mod = SB("mod", [128, 6144], BF16); mod_r = Res("mod")
    HT = SB("HT", [128, 8, NP], BF16); HT_r = Res("HT")
    NW = 2
    wring = [SB(f"wr{i}", [128, 4096], BF16) for i in range(NW)]; wring_r = [Res(f"wr{i}") for i in range(NW)]
    wr_i = [0]
    Pb = [SB(f"Pb{i}", [128, 4, NP], BF16) for i in range(3)]; Pb_r = [Res(f"Pb{i}") for i in range(3)]
    OA = SB("OA", [128, 4, NP], BF16); OA_r = Res("OA")
    OB = SB("OB", [128, 4, NP], BF16); OB_r = Res("OB")
    OC = SB("OC", [128, 4, NP], BF16); OC_r = Res("OC")
    BIG = SB("BIG", [128, 24, NP], BF16); BIG_r = [Res(f"BIG{j}") for j in range(24)]
    FB = SB("FB", [128, NT, 512], F32); FB_r = [Res(f"FB{i}") for i in range(NT)]
    NTMP = 6
    tmpf = [SB(f"tf{i}", [128, 1024], F32) for i in range(NTMP)]; tmpf_r = [Res(f"tf{i}") for i in range(NTMP)]
    tf_i = [0]
    NTB = 8
    tmpb = [SB(f"tb{i}", [128, 1024], BF16) for i in range(NTB)]; tmpb_r = [Res(f"tb{i}") for i in range(NTB)]
    tb_i = [0]
    NSM = 8
    small = [SB(f"sm{i}", [128, 8], F32) for i in range(NSM)]; small_r = [Res(f"sm{i}") for i in range(NSM)]
    sm_i = [0]
    fT_d = SB("fT_d", [128, 1024], F32); fT_dr = Res("fT_d")
    S_f = SB("S_f", [128, 4, 128], F32); S_fr = [Res(f"S_f{h}") for h in range(4)]
    S_b = SB("S_b", [128, 4, 128], BF16); S_br = [Res(f"S_b{h}") for h in range(4)]
    Rf4 = SB("Rf4", [128, 4, NP], F32); Rf4_r = [Res(f"Rf{h}") for h in range(4)]
    Rb4 = SB("Rb4", [128, 4, NP], BF16); Rb4_r = [Res(f"Rb{h}") for h in range(4)]
    WT = SB("WT", [128, 4, 128], BF16); WT_r = Res("WT")
    bsrow = SB("bsrow", [1, 512], BF16); bsrow_r = Res("bsrow")
    lc = SB("lc", [128, 64], F32); lc_r = Res("lc")
    lbt = SB("lbt", [128, 512], F32); lbt_r = Res("lbt")
    oml = SB("oml", [128, 512], F32); oml_r = Res("oml")
    vnb = SB("vnb", [128, 512], F32); vnb_r = Res("vnb")
    IDX = SB("IDX", [128, NCOL], mybir.dt.uint32); IDX_r = Res("IDX")
    PTs = SB("PTs", [128, NCOL], I32); PTs_r = Res("PTs")
    rmask = SB("rmask", [128, 8 * NPG], F32); rmask_r = Res("rmask")
    Sb_t = [SB(f"Sbt{i}", [128, 4, 128], F32) for i in range(1)]; Sb_r = [Res(f"Sbt{i}") for i in range(1)]
    assert NTILES >= NPG
    Vpg = VH; Vpg_r = VH_r
    QB = [FB[:, i % NT, :] for i in range(2)]; QB_r = [FB_r[i % NT] for i in range(2)]
    zb = SB("zb", [128, 8, NPG], F32); zb_r = Res("zb")
    eb_s = SB("eb_s", [128, 8, NPG], F32); eb_sr = Res("eb_s")
    spb_s = SB("spb_s", [128, 8 * NPG], BF16); spb_sr = Res("spb_s")
    wb_s = SB("wb_s", [128, 8, NPG], BF16); wb_sr = Res("wb_s")

    NPS = 5
    psr = [PS(f"ps{i}", [128, 512], F32) for i in range(NPS)]; psr_r = [Res(f"ps{i}") for i in range(NPS)]
    ps_i = [0]
    psT = PS("psT", [128, 1024], BF16); psT_r = Res("psT")
    psAcc = PS("psAcc", [128, 512], F32); psAcc_r = Res("psAcc")
    psAcc2 = PS("psAcc2", [128, 512], F32); psAcc2_r = Res("psAcc2")

    def ring(lst, rl, idx):
        i = idx[0] % len(lst)
        idx[0] += 1
        return lst[i], rl[i]

    def psum():
        return ring(psr, psr_r, ps_i)

    def tf():
        return ring(tmpf, tmpf_r, tf_i)

    def tbf():
        return ring(tmpb, tmpb_r, tb_i)

    def sm():
        return ring(small, small_r, sm_i)

    def mm(out, lhsT, rhs, start, stop, R, W):
        P.op("pe", lambda e: e.matmul(out, lhsT=lhsT, rhs=rhs, start=start, stop=stop), reads=R, writes=W,
             selfsync=False)

    def tr(out, in_, ident, R, W):
        P.op("pe", lambda e: e.transpose(out=out, in_=in_, identity=ident), reads=R, writes=W, selfsync=False)

    def act(out, in_, func, R, W, bias=None, scale=None, accum=None):
        kw = {}
        if bias is not None:
            kw["bias"] = bias
        if scale is not None:
            kw["scale"] = scale
        if accum is not None:
            kw["accum_out"] = accum
        P.op("act", lambda e: e.activation(out=out, in_=in_, func=func, **kw), reads=R, writes=W)

    def tt(out, a, b, op, R, W, eng="dve"):
        P.op(eng, lambda e: e.tensor_tensor(out=out, in0=a, in1=b, op=op), reads=R, writes=W)

    def ts(out, a, s1, s2, op0, op1, R, W, eng="dve"):
        if op1 is None:
            P.op(eng, lambda e: e.tensor_scalar(out=out, in0=a, scalar1=s1, scalar2=None, op0=op0), reads=R, writes=W)
        else:
            P.op(eng, lambda e: e.tensor_scalar(out=out, in0=a, scalar1=s1, scalar2=s2, op0=op0, op1=op1),
                 reads=R, writes=W)

    def stt(out, a, s, b, op0, op1, R, W):
        P.op("dve", lambda e: e.scalar_tensor_tensor(out=out, in0=a, scalar=s, in1=b, op0=op0, op1=op1),
             reads=R, writes=W)

    def cp_(out, in_, R, W, eng="dve"):
        if eng == "act":
            P.op(eng, lambda e: e.activation(out=out, in_=in_, func=AF.Identity), reads=R, writes=W)
        else:
            P.op(eng, lambda e: e.tensor_copy(out=out, in_=in_), reads=R, writes=W)

    def recip(out, in_, R, W):
        P.op("dve", lambda e: e.reciprocal(out=out, in_=in_), reads=R, writes=W)

    def mset(ap, val, W, eng="dve"):
        P.op(eng, lambda e: e.memset(ap, val), reads=(), writes=W)

    def ld(out, in_, R, W, q="sp"):
        P.dma(q, lambda e: e.dma_start(out=out, in_=in_), reads=R, writes=W)

    ld(cst[:], consts.ap(), [], [cst_r])
    cp_(cb[:], cst[:, 0:1152], [cst_r], [cb_r])
    mset(zeros_b[:], 0.0, [zeros_r])
    for l in range(2):
        for k, (src, K, C) in wsp.items():
            for r0 in range(0, K, 128):
                P.dma("pool", lambda e, l=l, k=k, src=src, r0=r0: e.dma_start(
                    out=wb[(l, k)].ap()[r0:r0 + 128, :], in_=src.ap()[l, r0:r0 + 128, :], max_dma_last_dim=4096),
                    reads=[], writes=[wres[(l, k)]])
    ld(PTs[:], pt.ap().partition_broadcast(128), [], [PTs_r])
    ts(IDX[:], PTs[:], 128.0, iota_c, ALU.mult, ALU.add, [PTs_r, cst_r], [IDX_r])
    mset(rmask[:], 1.0, [rmask_r])
    for h in range(8):
        mset(rmask[:, h * NPG:h * NPG + 1], 0.0, [rmask_r])

    def wchunk(l, k, r0, nr, c0, ncol):
        slot, sr = ring(wring, wring_r, wr_i)
        view = slot[:, 0:nr * ncol].rearrange("p (a b) -> p a b", a=nr)
        src = wb[(l, k)].ap()[r0:r0 + nr * 128, c0:c0 + ncol].rearrange("(a p) c -> p a c", p=128)
        ld(view, src, [wres[(l, k)]], [sr])
        return view, sr

    def bcast_row(dst, dst_r, src_ap):
        ld(dst, src_ap.partition_broadcast(128), [], [dst_r])

    def layer_consts(l):
        for g in range(4):
            t1, t1r = tf()
            ld(t1[:, 0:128], w_s.ap()[l, g], [], [t1r])
            ps, pr = psum()
            tr(ps[:, 0:128], t1[:, 0:128], ident_f, [t1r, cst_r], [pr])
            tt(WT[:, g, :], ps[:, 0:128], causal_b, ALU.mult, [pr, cb_r], [WT_r])
        t1, t1r = tf()
        ld(t1[0:1, 0:512], b_s.ap()[l:l + 1].rearrange("o g t -> o (g t)"), [], [t1r])
        cp_(bsrow[:], t1[0:1, 0:512], [t1r], [bsrow_r])
        bcast_row(lc[:, 0:8], lc_r, sbb.ap()[l:l + 1, :])
        for h in range(4):
            ld(lc[:, 8 + h:9 + h], onw.ap()[l, h * 128:(h + 1) * 128].rearrange("(p o) -> p o", o=1), [], [lc_r])
            ld(lc[:, 20 + h:21 + h], w_s.ap()[l, h, 0:1, 0:1].partition_broadcast(128), [], [lc_r])
            ld(lc[:, 24 + h:25 + h], b_s.ap()[l, h:h + 1, 0:1].partition_broadcast(128), [], [lc_r])
        bcast_row(vnb[:], vnb_r, vnw.ap()[l:l + 1, :])
        if l == 0:
            mset(lbt[:], 0.0, [lbt_r]); mset(oml[:], 1.0, [oml_r])
            mset(lc[:, 12:16], 0.0, [lc_r]); mset(lc[:, 16:20], 1.0, [lc_r])
        else:
            a, ar = tf(); b, br = tf()
            bcast_row(a[:, 0:512], ar, lbl.ap()[0:1, :])
            bcast_row(b[:, 0:512], br, lbl.ap()[1:2, :])
            tt(a[:, 0:512], b[:, 0:512], a[:, 0:512], ALU.subtract, [ar, br], [ar])
            act(lbt[:], a[:, 0:512], AF.Sigmoid, [ar], [lbt_r])
            ts(oml[:], lbt[:], -1.0, 1.0, ALU.mult, ALU.add, [lbt_r], [oml_r])
            c, cr = tf()
            for h in range(4):
                ld(c[:, h:h + 1], lbl.ap()[0, h * 128:(h + 1) * 128].rearrange("(p o) -> p o", o=1), [], [cr])
                ld(c[:, 4 + h:5 + h], lbl.ap()[1, h * 128:(h + 1) * 128].rearrange("(p o) -> p o", o=1), [], [cr])
            tt(c[:, 8:12], c[:, 4:8], c[:, 0:4], ALU.subtract, [cr], [cr])
            act(lc[:, 12:16], c[:, 8:12], AF.Sigmoid, [cr], [lc_r])
            ts(lc[:, 16:20], lc[:, 12:16], -1.0, 1.0, ALU.mult, ALU.add, [lc_r], [lc_r])
        for h in range(4):
            mset(S_f[:, h, :], 0.0, [S_fr[h]])
            mset(S_b[:, h, :], 0.0, [S_br[h]])

    def compute_mod(l, csrc):
        c, cr = tf()
        ld(c[:], csrc.ap(), [], [cr])
        c2, c2r = tbf()
        act(c2[:], c[:], AF.Silu, [cr], [c2r])
        for kc in range(8):
            tr(psT[:, kc * 128:(kc + 1) * 128], c2[:, kc * 128:(kc + 1) * 128], ident_b, [c2r, cb_r], [psT_r])
        cT2, cTb_r = tbf()
        cTb = cT2[:, :].rearrange("p (a b) -> p a b", a=8)
        cp_(cTb, psT[:, :].rearrange("p (a b) -> p a b", a=8), [psT_r], [cTb_r])
        for blk in range(12):
            wv, wr = wchunk(l, "ada", 0, 8, blk * 512, 512)
            ps, pr = psum()
            for kc in range(8):
                mm(ps[:], cTb[:, kc, :], wv[:, kc, :], kc == 0, kc == 7, [cTb_r, wr], [pr])
            bb, bbr = tf()
            bcast_row(bb[:, 0:512], bbr, b_ada.ap()[l:l + 1, blk * 512:(blk + 1) * 512])
            which = blk // 2
            if which in (1, 4):
                nsrc = n1w if which == 1 else n2w
                off = (blk % 2) * 512
                bcast_row(bb[:, 512:1024], bbr, nsrc.ap()[l:l + 1, off:off + 512])
                t2, t2r = tf()
                tt(t2[:, 0:512], ps[:], bb[:, 0:512], ALU.add, [pr, bbr], [t2r])
                stt(mod[:, blk * 512:(blk + 1) * 512], t2[:, 0:512], 1.0, bb[:, 512:1024], ALU.add, ALU.mult,
                    [t2r, bbr], [mod_r])
            else:
                tt(mod[:, blk * 512:(blk + 1) * 512], ps[:], bb[:, 0:512], ALU.add, [pr, bbr], [mod_r])

    SH1, SC1, G1, SH2, SC2, G2 = 0, 1024, 2048, 3072, 4096, 5120

    def norm_to_HT(T, a_off, b_off):
        for i in range(T):
            junk, jr = tbf()
            s1, s1r = sm()
            act(junk[:], xg[:, i, :], AF.Square, [xg_r[i]], [jr, s1r], accum=s1[:, 0:1])
            act(s1[:, 1:2], s1[:, 0:1], AF.Sqrt, [s1r], [s1r], bias=EPS, scale=1.0 / D)
            recip(s1[:, 2:3], s1[:, 1:2], [s1r], [s1r])
            t1, t1r = tf()
            stt(t1[:], xg[:, i, :], s1[:, 2:3], mod[:, a_off:a_off + D], ALU.mult, ALU.mult, [xg_r[i], s1r, mod_r], [t1r])
            hb, hbr = tbf()
            tt(hb[:], t1[:], mod[:, b_off:b_off + D], ALU.add, [t1r, mod_r], [hbr])
            for kc in range(8):
                tr(psT[:, kc * 128:(kc + 1) * 128], hb[:, kc * 128:(kc + 1) * 128], ident_b, [hbr, cb_r], [psT_r])
            cp_(HT[:, :, i * 128:(i + 1) * 128], psT[:, :].rearrange("p (a b) -> p a b", a=8), [psT_r], [HT_r],
                eng="act")

    def proj_F(wv, wr, N, evac, src=None, src_r=None, nkc=8):
        src = HT if src is None else src
        src_r = HT_r if src_r is None else src_r
        for j in range(4):
            ps, pr = psum()
            for kc in range(nkc):
                mm(ps[:, 0:N], wv[:, kc, j * 128:(j + 1) * 128], src[:, kc, 0:N], kc == 0, kc == nkc - 1,
                   [wr, src_r], [pr])
            evac(j, ps, pr)

    def proj_T(wv, wr, T, evac):
        for i in range(T):
            ps, pr = psum()
            for kc in range(8):
                mm(ps[:], HT[:, kc, i * 128:(i + 1) * 128], wv[:, kc, :], kc == 0, kc == 7, [wr, HT_r], [pr])
            evac(i, ps, pr)

    def tokview(buf):
        return buf[:, :, :].rearrange("p a b -> p (a b)").rearrange("p (t c) -> p t c", c=512)

    def process_group(l, gi, sample):
        T = 1 if sample else NT
        N = T * 128
        tok0 = SEQ if sample else gi * NP
        for i in range(T):
            if l == 0:
                src = xs.ap() if sample else xp.ap()[gi * NP + i * 128: gi * NP + (i + 1) * 128, :]
                ld(xg[:, i, :], src, [], [xg_r[i]])
            else:
                ld(xg[:, i, :], x1.ap()[tok0 + i * 128: tok0 + (i + 1) * 128, :], [x1_res[tok0 // 128 + i]], [xg_r[i]])
        norm_to_HT(T, SC1, SH1)
        uT, uT_r = Pb[0], Pb_r[0]
        va, va_r = tokview(Pb[1]), Pb_r[1]

        wv, wr = wchunk(l, "in", 0, 8, 0, 512)
        proj_F(wv, wr, N, lambda j, ps, pr: act(uT[:, j, 0:N], ps[:, 0:N], AF.Gelu, [pr], [uT_r]))
        wv, wr = wchunk(l, "in", 0, 8, 512, 512)

        def ev_av(i, ps, pr):
            ga, gar = tf()
            act(ga[:, 0:512], ps[:], AF.Gelu, [pr], [gar])
            s1, s1r = sm()
            act(ga[:, 512:1024], ga[:, 0:512], AF.Square, [gar], [gar, s1r], accum=s1[:, 0:1])
            act(s1[:, 1:2], s1[:, 0:1], AF.Sqrt, [s1r], [s1r], bias=EPS, scale=1.0 / 512)
            recip(s1[:, 2:3], s1[:, 1:2], [s1r], [s1r])
            stt(ga[:, 512:1024], ga[:, 0:512], s1[:, 2:3], vnb[:], ALU.mult, ALU.mult, [gar, s1r, vnb_r], [gar])
            cp_(va[:, i, :], ga[:, 512:1024], [gar], [va_r], eng="act")
            if sample:
                P.dma("pool", lambda e: e.dma_start(out=gv.ap()[l], in_=ga[0:NS, 512:1024]), [gar], [outs_res["gv"]], store=True)
        proj_T(wv, wr, T, ev_av)
        for i in range(T):
            ps, pr = psum()
            for g in range(4):
                if sample:
                    mm(ps[:, g * 128:(g + 1) * 128], va[:, i, g * 128:(g + 1) * 128], ident_b, True, True,
                       [va_r, cb_r], [pr])
                else:
                    mm(ps[:, g * 128:(g + 1) * 128], va[:, i, g * 128:(g + 1) * 128], WT[:, g, :], True, False,
                       [va_r, WT_r], [pr])
                    mm(ps[:, g * 128:(g + 1) * 128], ones_b[0:1, :], bsrow[0:1, g * 128:(g + 1) * 128], False, True,
                       [cb_r, bsrow_r], [pr])
            if sample:
                t1, t1r = tf()
                for g in range(4):
                    ts(t1[:, g * 128:(g + 1) * 128], ps[:, g * 128:(g + 1) * 128], lc[:, 20 + g:21 + g],
                       lc[:, 24 + g:25 + g], ALU.mult, ALU.add, [pr, lc_r], [t1r])
                tt(OA[:, :, i * 128:(i + 1) * 128], t1[:, 0:512].rearrange("p (a b) -> p a b", a=4),
                   uT[:, :, i * 128:(i + 1) * 128], ALU.mult, [t1r, uT_r], [OA_r])
            else:
                tt(OA[:, :, i * 128:(i + 1) * 128], ps[:, :].rearrange("p (a b) -> p a b", a=4),
                   uT[:, :, i * 128:(i + 1) * 128], ALU.mult, [pr, uT_r], [OA_r])

        qT, qT_r = Pb[0], Pb_r[0]
        vb, vb_r = tokview(Pb[1]), Pb_r[1]
        gT, gT_r = Pb[2], Pb_r[2]
        wv, wr = wchunk(l, "in", 0, 8, 1024, 512)
        proj_F(wv, wr, N, lambda j, ps, pr: act(qT[:, j, 0:N], ps[:, 0:N], AF.Silu, [pr], [qT_r]))
        wv, wr = wchunk(l, "in", 0, 8, 1536, 512)
        fT, fT_r = fT_d, fT_dr
        if sample:
            def ev_f(j, ps, pr):
                sg, sgr = tf()
                act(sg[:, 0:128], ps[:, 0:128], AF.Sigmoid, [pr], [sgr])
                ts(fT[:, j * 128:(j + 1) * 128], sg[:, 0:128], lc[:, 16 + j:17 + j], lc[:, 12 + j:13 + j], ALU.mult,
                   ALU.add, [sgr, lc_r], [fT_r])
                ts(fT[:, 512 + j * 128:512 + (j + 1) * 128], fT[:, j * 128:(j + 1) * 128], -1.0, 1.0, ALU.mult, ALU.add,
                   [fT_r], [fT_r])
            proj_F(wv, wr, N, ev_f)
        else:
            proj_T(wv, wr, T, lambda i, ps, pr: cp_(FB[:, i, :], ps[:], [pr], [FB_r[i]], eng="act"))
        wv, wr = wchunk(l, "in", 0, 8, 2048, 512)
        if sample:
            def ev_i(i, ps, pr):
                t1, t1r = tf()
                cp_(t1[:, 0:512], ps[:], [pr], [t1r], eng="act")
                P.dma("pool", lambda e: e.dma_start(out=vscr.ap(), in_=t1[0:NS, 0:512]), [t1r], [vscr_res], store=True)
            proj_T(wv, wr, T, ev_i)
        else:
            proj_T(wv, wr, T, lambda i, ps, pr: cp_(vb[:, i, :], ps[:], [pr], [vb_r], eng="act"))
        wv, wr = wchunk(l, "in", 0, 8, 2560, 512)
        proj_F(wv, wr, N, lambda j, ps, pr: act(gT[:, j, 0:N], ps[:, 0:N], AF.Silu, [pr], [gT_r]))

        def onorm(psO, psO_r, ncols, dst, tcols):
            W4 = 4 * ncols
            o2, o2r = tbf()
            act(o2[:, 0:W4], psO[:, 0:W4], AF.Square, [psO_r], [o2r])
            pss, pssr = psum()
            mm(pss[:, 0:W4], ones_b, o2[:, 0:W4], True, True, [cb_r, o2r], [pssr])
            rs, rsr = tf()
            act(rs[:, 0:W4], pss[:, 0:W4], AF.Sqrt, [pssr], [rsr], bias=EPS, scale=1.0 / 128)
            recip(rs[:, 0:W4], rs[:, 0:W4], [rsr], [rsr])
            for h in range(4):
                stt(rs[:, 512 + h * ncols:512 + (h + 1) * ncols], psO[:, h * ncols:(h + 1) * ncols], lc[:, 8 + h:9 + h],
                    rs[:, h * ncols:(h + 1) * ncols], ALU.mult, ALU.mult, [psO_r, lc_r, rsr], [rsr])
            tt(dst, rs[:, 512:512 + W4].rearrange("p (a b) -> p a b", a=4), tcols, ALU.mult, [rsr, gT_r], [OB_r])

        if sample:
            mset(OB[:, :, :], 0.0, [OB_r])
            for b in range(NS):
                Sb, Sbr = Sb_t[0], Sb_r[0]
                ld(Sb[:], st.ap()[l, b].rearrange("h k v -> k h v"), [], [Sbr])
                vB, vBr = QB[b % 2], QB_r[b % 2]
                ld(vB, vscr.ap()[b:b + 1, :].partition_broadcast(128), [vscr_res], [vBr])
                sn, snr = tf()
                for h in range(4):
                    ts(sn[:, 512 + h * 128:512 + (h + 1) * 128], vB[:, h * 128:(h + 1) * 128],
                       fT[:, 512 + h * 128 + b:512 + h * 128 + b + 1], None, ALU.mult, None, [vBr, fT_r], [snr])
                    stt(sn[:, h * 128:(h + 1) * 128], Sb[:, h, :], fT[:, h * 128 + b:h * 128 + b + 1],
                        sn[:, 512 + h * 128:512 + (h + 1) * 128], ALU.mult, ALU.add, [Sbr, fT_r, snr], [snr])
                P.dma("pool", lambda e, b=b, sn=sn: e.dma_start(
                    out=Ss.ap()[l, b].rearrange("h k v -> k h v"),
                    in_=sn[:, 0:512].rearrange("p (a b) -> p a b", a=4)), [snr], [outs_res["Ss"]], store=True)
                snb, snbr = tbf()
                cp_(snb[:, 0:512], sn[:, 0:512], [snr], [snbr], eng="act")
                for h in range(4):
                    mm(psAcc[:, h * NS + b:h * NS + b + 1], snb[:, h * 128:(h + 1) * 128], qT[:, h, b:b + 1], True, True,
                       [snbr, qT_r], [psAcc_r])
            onorm(psAcc, psAcc_r, NS, OB[:, :, 0:NS], gT[:, :, 0:NS])
        else:
            for i in range(T):
                sg, sgr = tf()
                act(sg[:, 0:512], FB[:, i, :], AF.Sigmoid, [FB_r[i]], [sgr])
                tt(sg[:, 0:512], sg[:, 0:512], oml[:], ALU.mult, [sgr, oml_r], [sgr])
                tt(sg[:, 0:512], sg[:, 0:512], lbt[:], ALU.add, [sgr, lbt_r], [sgr])
                act(sg[:, 512:1024], sg[:, 0:512], AF.Ln, [sgr], [sgr])
                lf = sg[:, 512:1024]
                psb, psbr = psum()
                mm(psb[:], bmid_f, lf, True, True, [cst_r, sgr], [psbr])
                e1, e1r = tf()
                ts(e1[:, 0:512], psb[:], -80.0, 80.0, ALU.max, ALU.min, [psbr], [e1r])
                act(e1[:, 0:512], e1[:, 0:512], AF.Exp, [e1r], [e1r], scale=-1.0)
                ts(e1[:, 512:1024], sg[:, 0:512], -1.0, 1.0, ALU.mult, ALU.add, [sgr], [e1r])
                kh, khr = tbf()
                tt(kh[:, 0:512], e1[:, 512:1024], e1[:, 0:512], ALU.mult, [e1r], [khr])
                psr_, psrr = psum()
                mm(psr_[:], rev32_f, lf, True, True, [cst_r, sgr], [psrr])
                e2, e2r = tf()
                act(e2[:, 0:512], psr_[:], AF.Exp, [psrr], [e2r])
                kd, kdr = tbf()
                tt(kd[:, 0:512], e1[:, 512:1024], e2[:, 0:512], ALU.mult, [e1r, e2r], [kdr])
                for h in range(4):
                    tr(psT[:, h * 128:(h + 1) * 128], kh[:, h * 128:(h + 1) * 128], ident_b, [khr, cb_r], [psT_r])
                cp_(kh[:, 512:1024], psT[:, 0:512], [psT_r], [khr], eng="act")
                psbt, psbtr = psum()
                for h in range(4):
                    mm(psbt[:, h * 128:(h + 1) * 128], sg[:, 512 + h * 128:512 + (h + 1) * 128], bmid_f, True, True,
                       [sgr, cst_r], [psbtr])
                eb, ebr = tf()
                ts(eb[:, 0:512], psbt[:], -80.0, 80.0, ALU.max, ALU.min, [psbtr], [ebr])
                act(eb[:, 0:512], eb[:, 0:512], AF.Exp, [ebr], [ebr])
                psbs, psbsr = psum()
                for h in range(4):
                    mm(psbs[:, h * 128:(h + 1) * 128], sg[:, 512 + h * 128:512 + (h + 1) * 128], bt32_f, True, True,
                       [sgr, cst_r], [psbsr])
                act(eb[:, 512:1024], psbs[:], AF.Exp, [psbsr], [ebr])
                ql, qlr = tbf()
                tt(ql[:, 0:512].rearrange("p (a b) -> p a b", a=4), qT[:, :, i * 128:(i + 1) * 128],
                   eb[:, 0:512].rearrange("p (a b) -> p a b", a=4), ALU.mult, [qT_r, ebr], [qlr])
                qs, qsr = tbf()
                tt(qs[:, 0:512].rearrange("p (a b) -> p a b", a=4), qT[:, :, i * 128:(i + 1) * 128],
                   eb[:, 512:1024].rearrange("p (a b) -> p a b", a=4), ALU.mult, [qT_r, ebr], [qsr])
                psA, psAr = psum()
                for h in range(4):
                    mm(psA[:, h * 128:(h + 1) * 128], kh[:, 512 + h * 128:512 + (h + 1) * 128],
                       ql[:, h * 128:(h + 1) * 128], True, True, [khr, qlr], [psAr])
                at_, atr = tf()
                ts(at_[:, 0:512], psA[:], 1e30, -1e30, ALU.min, ALU.max, [psAr], [atr])
                tt(ql[:, 512:1024], at_[:, 0:512], bt32x4_b, ALU.mult, [atr, cb_r], [qlr])
                vm = []
                for jj in range(2):
                    vmt, vmr = tbf()
                    for j2 in range(2):
                        j = jj * 2 + j2
                        ts(vmt[:, j2 * 512:(j2 + 1) * 512], vb[:, i, :], submask[:, j:j + 1], None, ALU.mult, None,
                           [vb_r, cst_r], [vmr])
                        vm.append((vmt[:, j2 * 512:(j2 + 1) * 512], vmr))
                for h in range(4):
                    hs = slice(h * 128, (h + 1) * 128)
                    mm(psAcc[:, hs], vb[:, i, hs], ql[:, 512 + h * 128:512 + (h + 1) * 128], True, False,
                       [vb_r, qlr], [psAcc_r])
                    for j in range(4):
                        c0 = h * 128 + j * 32
                        mm(psAcc[:, c0:c0 + 32], S_b[:, h, :], qs[:, c0:c0 + 32], False, j == 3,
                           [S_br[h], qsr], [psAcc_r])
                        psd, psdr = psum()
                        mm(psd[:, 0:128], kd[:, hs], vm[j][0][:, hs], True, True, [kdr, vm[j][1]], [psdr])
                        ecol = 512 + c0 + 31
                        stt(S_f[:, h, :], S_f[:, h, :], eb[:, ecol:ecol + 1], psd[:, 0:128], ALU.mult, ALU.add,
                            [S_fr[h], ebr, psdr], [S_fr[h]])
                        cp_(S_b[:, h, :], S_f[:, h, :], [S_fr[h]], [S_br[h]], eng="act")
                onorm(psAcc, psAcc_r, 128, OB[:, :, i * 128:(i + 1) * 128], gT[:, :, i * 128:(i + 1) * 128])
            if gi == NG - 1:
                P.dma("pool", lambda e: e.dma_start(out=Sp.ap()[l].rearrange("h k v -> k h v"), in_=S_f[:]),
                      S_fr, [outs_res["Sp"]], store=True)

        cqT, cqT_r = Pb[0], Pb_r[0]
        wv, wr = wchunk(l, "in", 0, 8, 3072, 512)
        if sample:
            def ev_q(i, ps, pr):
                t1, t1r = tf()
                cp_(t1[:, 0:512], ps[:], [pr], [t1r], eng="act")
                P.dma("pool", lambda e: e.dma_start(out=qscr.ap(), in_=t1[0:NS, 0:512]), [t1r], [qscr_res], store=True)
            proj_T(wv, wr, T, ev_q)
        else:
            proj_F(wv, wr, N, lambda j, ps, pr: cp_(cqT[:, j, 0:N], ps[:, 0:N], [pr], [cqT_r], eng="act"))
        wv, wr = wchunk(l, "in", 0, 8, 3584, 512)

        def ev_k(i, ps, pr):
            kf, kfr = tf()
            cp_(kf[:, 0:512], ps[:], [pr], [kfr], eng="act")
            if sample:
                P.dma("pool", lambda e: e.dma_start(out=ks.ap()[l], in_=kf[0:NS, 0:512]), [kfr], [outs_res["ks"]], store=True)
            else:
                ti = gi * NT + i
                P.dma("pool", lambda e: e.dma_start(out=kp.ap()[l, ti * 128:(ti + 1) * 128, :], in_=kf[:, 0:512]),
                      [kfr], [outs_res["kp"]], store=True)
                ps2, p2r = psum()
                for j in range(4):
                    tr(ps2[:, j * 128:(j + 1) * 128], kf[:, j * 128:(j + 1) * 128], ident_f, [kfr, cst_r], [p2r])
                cp_(KT[:, :, ti * 128:(ti + 1) * 128], ps2[:, :].rearrange("p (a b) -> p a b", a=4), [p2r], [KT_r[ti]])
        proj_T(wv, wr, T, ev_k)
        wv, wr = wchunk(l, "in", 0, 8, 4096, 512)

        def ev_v(i, ps, pr):
            vf, vfr = tf()
            cp_(vf[:, 0:512], ps[:], [pr], [vfr], eng="act")
            if sample:
                P.dma("pool", lambda e: e.dma_start(out=vs.ap()[l], in_=vf[0:NS, 0:512]), [vfr], [outs_res["vs"]], store=True)
            else:
                ti = gi * NT + i
                P.dma("pool", lambda e: e.dma_start(out=vp.ap()[l, ti * 128:(ti + 1) * 128, :], in_=vf[:, 0:512]),
                      [vfr], [outs_res["vp"]], store=True)
                cp_(VH[:, ti, :], vf[:, 0:512], [vfr], [VH_r[ti]])
        proj_T(wv, wr, T, ev_v)

        if sample:
            mset(OC[:, :, :], 0.0, [OC_r])
            for b in range(NS):
                qb, qbr = QB[b % 2], QB_r[b % 2]
                ld(qb, qscr.ap()[b:b + 1, :].partition_broadcast(128), [qscr_res], [qbr])
                for p in range(NPG):
                    col = b * NPG + p
                    r = NPG - 1 - p
                    kpg_t, kpr = tf()
                    kpg = kpg_t[:, 0:512]
                    P.dma("pool", lambda e, kpg=kpg, col=col: e.indirect_dma_start(
                        out=kpg, out_offset=None, in_=ck.ap().rearrange("l r c -> (l r) c"),
                        in_offset=bass.IndirectOffsetOnAxis(ap=IDX[:, col:col + 1], axis=0),
                        element_offset=l * NPOOL * 128 * 512), [IDX_r], [kpr])
                    pr_, prr = tf()
                    tt(pr_[:, 0:512], kpg, qb, ALU.mult, [kpr, qbr], [prr])
                    P.op("dve", lambda e, pr_=pr_, r=r: e.tensor_reduce(
                        out=zb[:, :, r], in_=pr_[:, 0:512].rearrange("p (a b) -> p a b", a=8), axis=AX.X, op=ALU.add),
                        reads=[prr], writes=[zb_r])
                vbase = (b % 2) * NPG if NTILES >= 2 * NPG else 0
                for p in range(NPG):
                    col = b * NPG + p
                    P.dma("pool", lambda e, p=p, col=col, vbase=vbase: e.indirect_dma_start(
                        out=Vpg[:, vbase + p, :], out_offset=None, in_=cv.ap().rearrange("l r c -> (l r) c"),
                        in_offset=bass.IndirectOffsetOnAxis(ap=IDX[:, col:col + 1], axis=0),
                        element_offset=l * NPOOL * 128 * 512), [IDX_r], [Vpg_r[vbase + p]])
                for h in range(8):
                    act(eb_s[:, h, :], zb[:, h, :], AF.Exp, [zb_r, lc_r], [eb_sr], bias=lc[:, h:h + 1], scale=0.125)
                act(spb_s[:], eb_s[:, :, :].rearrange("p a b -> p (a b)"), AF.Ln, [eb_sr], [spb_sr], bias=1.0)
                psg, psgr = psum()
                mm(psg[:, 0:8 * NPG], tri_b, spb_s[:], True, True, [cb_r, spb_sr], [psgr])
                pst, pstr = psum()
                mm(pst[:, 0:8 * NPG], ones_b, spb_s[:], True, True, [cb_r, spb_sr], [pstr])
                t1, t1r = tf()
                W8 = 8 * NPG
                P.op("dve", lambda e, t1=t1, pst=pst: e.tensor_tensor_scan(
                    out=t1[:, 0:W8], data0=rmask[:], data1=pst[:, 0:W8], initial=0.0, op0=ALU.mult, op1=ALU.add),
                    reads=[rmask_r, pstr], writes=[t1r])
                tt(t1[:, 0:W8], t1[:, 0:W8], pst[:, 0:W8], ALU.subtract, [t1r, pstr], [t1r])
                tt(t1[:, 0:W8], t1[:, 0:W8], psg[:, 0:W8], ALU.add, [t1r, psgr], [t1r])
                act(t1[:, 0:W8], t1[:, 0:W8], AF.Exp, [t1r], [t1r], scale=-1.0)
                tt(wb_s[:, :, :].rearrange("p a b -> p (a b)"), t1[:, 0:W8],
                   eb_s[:, :, :].rearrange("p a b -> p (a b)"), ALU.mult, [t1r, eb_sr], [wb_sr])
                for h in range(8):
                    hp, po = h // 2, (h % 2) * 64
                    for p in range(NPG):
                        r = NPG - 1 - p
                        mm(psAcc2[po:po + 64, hp * NS + b:hp * NS + b + 1], Vpg[:, vbase + p, h * 64:(h + 1) * 64],
                           wb_s[:, h, r:r + 1], p == 0, p == NPG - 1, [Vpg_r[vbase + p], wb_sr], [psAcc2_r])
            cp_(OC[:, :, 0:NS], psAcc2[:, 0:4 * NS].rearrange("p (a b) -> p a b", a=4), [psAcc2_r], [OC_r], eng="act")
        else:
            last = gi * NT + NT - 1
            pack = N <= 256
            for q in range(2):
                heads = list(range(4 * q, 4 * q + 4))
                accs = {}
                for h in heads:
                    hp, po = h // 2, (h % 2) * 64
                    acc, accr = (psAcc2, psAcc2_r) if (h % 4) < 2 else (psAcc, psAcc_r)
                    accs[h] = (acc, accr)
                    mset(Rf4[:, h % 4, :], 0.0, [Rf4_r[h % 4]], eng="pool")
                    mset(Rb4[:, h % 4, :], 0.0, [Rb4_r[h % 4]], eng="pool")
                    mm(acc[po:po + 64, 0:N], zeros_b[:, 0:64], zeros_b[:, 0:N], True, False, [zeros_r], [accr])
                for kt in range(last, -1, -1):
                    jq = max(0, kt - gi * NT)
                    c0 = jq * 128
                    diag = kt >= gi * NT
                    first = kt == last
                    stg = {}
                    for h in heads:
                        hp, po = h // 2, (h % 2) * 64
                        psz, pszr = psum()
                        if pack:
                            psg, psgr, go = psz, pszr, 256
                        else:
                            psg, psgr = psum()
                            go = 0
                        mm(psz[:, c0:N], KT[po:po + 64, hp, kt * 128:(kt + 1) * 128], cqT[po:po + 64, hp, c0:N],
                           True, True, [KT_r[kt], cqT_r], [pszr])
                        e_, er = tf()
                        act(e_[:, c0:N], psz[:, c0:N], AF.Exp, [pszr, lc_r], [er], bias=lc[:, h:h + 1], scale=0.125)
                        sp_, spr = tbf()
                        act(sp_[:, c0:N], e_[:, c0:N], AF.Ln, [er], [spr], bias=1.0)
                        if diag:
                            tt(sp_[:, c0:c0 + 128], sp_[:, c0:c0 + 128], mstrict_b, ALU.mult, [spr, cb_r], [spr])
                        stg[h] = (psg, psgr, go, e_, er, sp_, spr)
                    for h in heads:
                        psg, psgr, go, e_, er, sp_, spr = stg[h]
                        Rb, Rb_r = Rb4[:, h % 4, :], Rb4_r[h % 4]
                        mm(psg[:, go + c0:go + N], tri_b, sp_[:, c0:N], True, first, [cb_r, spr], [psgr])
                        if not first:
                            mm(psg[:, go + c0:go + N], ones_b, Rb[:, c0:N], False, True, [cb_r, Rb_r], [psgr])
                        act(psg[:, go + c0:go + N], psg[:, go + c0:go + N], AF.Exp, [psgr], [psgr], scale=-1.0)
                    for h in heads:
                        hp, po = h // 2, (h % 2) * 64
                        acc, accr = accs[h]
                        psg, psgr, go, e_, er, sp_, spr = stg[h]
                        Rf, Rf_r, Rb, Rb_r = Rf4[:, h % 4, :], Rf4_r[h % 4], Rb4[:, h % 4, :], Rb4_r[h % 4]
                        tt(sp_[:, 512 + c0:512 + N], e_[:, c0:N], psg[:, go + c0:go + N], ALU.mult, [er, psgr], [spr])
                        if diag:
                            tt(sp_[:, 512 + c0:512 + c0 + 128], sp_[:, 512 + c0:512 + c0 + 128], mstrict_b, ALU.mult,
                               [spr, cb_r], [spr])
                        mm(acc[po:po + 64, c0:N], VH[:, kt, h * 64:(h + 1) * 64], sp_[:, 512 + c0:512 + N], False,
                           kt == 0, [VH_r[kt], spr], [accr])
                        if kt > 0:
                            tt(Rf[:, c0:N], Rf[:, c0:N], sp_[:, c0:N], ALU.add, [Rf_r, spr], [Rf_r])
                            cp_(Rb[:, c0:N], Rf[:, c0:N], [Rf_r], [Rb_r])
                for h in heads:
                    hp, po = h // 2, (h % 2) * 64
                    acc, accr = accs[h]
                    cp_(OC[po:po + 64, hp, 0:N], acc[po:po + 64, 0:N], [accr], [OC_r], eng="act")

        for c in range(6):
            wv, wr = wchunk(l, "in", 0, 8, 4608 + c * 512, 512)
            proj_F(wv, wr, N, lambda j, ps, pr, c=c: act(BIG[:, c * 4 + j, 0:N], ps[:, 0:N], AF.Sigmoid, [pr],
                                                        [BIG_r[c * 4 + j]]))
        for xi, (wk, osrc, osr) in enumerate([("ba", OA, OA_r), ("bb", OB, OB_r), ("bc", OC, OC_r)]):
            for half in range(2):
                wv, wr = wchunk(l, wk, 0, 4, half * 512, 512)

                def ev_b(j, ps, pr, xi=xi, half=half):
                    jj = half * 4 + j
                    gate = BIG[:, xi * 8 + jj, 0:N]
                    if xi == 0:
                        tt(HT[:, jj, 0:N], ps[:, 0:N], gate, ALU.mult, [pr, BIG_r[xi * 8 + jj]], [HT_r])
                    else:
                        t1, t1r = tf()
                        tt(t1[:, 0:N], ps[:, 0:N], gate, ALU.mult, [pr, BIG_r[xi * 8 + jj]], [t1r])
                        tt(HT[:, jj, 0:N], HT[:, jj, 0:N], t1[:, 0:N], ALU.add, [HT_r, t1r], [HT_r])
                proj_F(wv, wr, N, ev_b, src=osrc, src_r=osr, nkc=4)
        for half in range(2):
            wv, wr = wchunk(l, "out", 0, 8, half * 512, 512)

            def ev_o(i, ps, pr, half=half):
                t1, t1r = tf()
                tt(t1[:, 0:512], ps[:], mod[:, G1 + half * 512:G1 + (half + 1) * 512], ALU.mult, [pr, mod_r], [t1r])
                tt(xg[:, i, half * 512:(half + 1) * 512], xg[:, i, half * 512:(half + 1) * 512], t1[:, 0:512], ALU.add,
                   [xg_r[i], t1r], [xg_r[i]])
            proj_T(wv, wr, T, ev_o)
        norm_to_HT(T, SC2, SH2)
        for half in range(2):
            for c in range(4):
                wv, wr = wchunk(l, "ff1", 0, 8, half * 2048 + c * 512, 512)

                def ev_f1(j, ps, pr, c=c):
                    r_, rr = tbf()
                    act(r_[:, 0:N], ps[:, 0:N], AF.Relu, [pr], [rr])
                    tt(BIG[:, c * 4 + j, 0:N], r_[:, 0:N], r_[:, 0:N], ALU.mult, [rr], [BIG_r[c * 4 + j]], eng="pool")
                proj_F(wv, wr, N, ev_f1)
            for c in range(4):
                slot, sr = ring(wring, wring_r, wr_i)
                view = slot[:, :].rearrange("p (a b) -> p a b", a=16)
                src = wb[(l, "ff2")].ap()[half * 2048:(half + 1) * 2048, c * 256:(c + 1) * 256].rearrange(
                    "(a p) c -> p a c", p=128)
                ld(view, src, [wres[(l, "ff2")]], [sr])
                for i in range(T):
                    ps, pr = psum()
                    for kc in range(16):
                        mm(ps[:, 0:256], BIG[:, kc, i * 128:(i + 1) * 128], view[:, kc, :], kc == 0, kc == 15,
                           [BIG_r[kc], sr], [pr])
                    t1, t1r = tf()
                    tt(t1[:, 0:256], ps[:, 0:256], mod[:, G2 + c * 256:G2 + (c + 1) * 256], ALU.mult, [pr, mod_r], [t1r])
                    tt(xg[:, i, c * 256:(c + 1) * 256], xg[:, i, c * 256:(c + 1) * 256], t1[:, 0:256], ALU.add,
                       [xg_r[i], t1r], [xg_r[i]])
        for i in range(T):
            if l == 0:
                P.dma("pool", lambda e, i=i: e.dma_start(out=x1.ap()[tok0 + i * 128: tok0 + (i + 1) * 128, :],
                                                         in_=xg[:, i, :]), [xg_r[i]], [x1_res[tok0 // 128 + i]], store=True)
            else:
                junk, jr = tbf()
                s1, s1r = sm()
                act(junk[:], xg[:, i, :], AF.Square, [xg_r[i]], [jr, s1r], accum=s1[:, 0:1])
                act(s1[:, 1:2], s1[:, 0:1], AF.Sqrt, [s1r], [s1r], bias=EPS, scale=1.0 / D)
                recip(s1[:, 2:3], s1[:, 1:2], [s1r], [s1r])
                fb_, fbr = tf()
                bcast_row(fb_[:], fbr, fnw.ap())
                y_, yr = tf()
                stt(y_[:], xg[:, i, :], s1[:, 2:3], fb_[:], ALU.mult, ALU.mult, [xg_r[i], s1r, fbr], [yr])
                if sample:
                    P.dma("pool", lambda e, y_=y_: e.dma_start(out=ys.ap(), in_=y_[0:NS, :]), [yr], [outs_res["ys"]], store=True)
                else:
                    P.dma("pool", lambda e, y_=y_, i=i: e.dma_start(
                        out=yp.ap()[gi * NP + i * 128: gi * NP + (i + 1) * 128, :], in_=y_[:]), [yr], [outs_res["yp"]], store=True)

    for l in range(2):
        layer_consts(l)
        compute_mod(l, cp)
        for gi in range(NG):
            process_group(l, gi, False)
        compute_mod(l, cs)
        process_group(l, 0, True)
    P.wait_all("pool", list(outs_res.values()))
    es.close()
    return nc


_CACHE = {}


def _consts():
    c = np.zeros((128, 1413), np.float32)
    i = np.arange(128)
    s, t = i[:, None], i[None, :]
    c[:, 0:128] = np.eye(128)
    c[:, 128:256] = (s >= t)
    c[:, 256:384] = 1.0
    c[:, 384:512] = (s < t)
    same = (s // 32) == (t // 32)
    bt = ((s <= t) & same).astype(np.float32)
    mid = (t // 32) * 32 + 15
    c[:, 1153:1281] = (((s <= t) & same).astype(np.float32) - ((s <= mid) & same).astype(np.float32))
    c[:, 1281:1409] = ((s > t) & same)
    for j in range(4):
        c[:, 1409 + j] = ((i // 32) == j)
    c[:, 512:1024] = np.tile(bt, (1, 4))
    c[:, 1024:1152] = (s <= t)
    c[:, 1152] = i
    return c


def kernel(x_prompt, x_sample, c_prompt, c_sample, cache_k, cache_v, state_hgrn, page_table,
           w_ada, b_ada, norm1_w, norm2_w, w_in, gmlp_vnorm_w, gmlp_w_s, gmlp_b_s, sb_bias,
           hgrn_lb_logits, hgrn_onorm_w, w_branch_a, w_branch_b, w_branch_c, w_out,
           w_ff1, w_ff2, final_norm_w, NT=2):
    f = lambda a: np.ascontiguousarray(np.asarray(a, dtype=np.float32))
    x_prompt = f(x_prompt); x_sample = f(x_sample)
    B, SEQ, _ = x_prompt.shape
    DB = x_sample.shape[0]
    NPG = page_table.shape[1]
    NPOOL = cache_k.shape[1]
    assert DB == 8 * NS and B == 4
    key = (SEQ, NPG, NPOOL, NT)
    if key not in _CACHE:
        _CACHE[key] = build(SEQ, NPG, NPOOL, NT)
    nc = _CACHE[key]
    ckf = f(cache_k).reshape(2, NPOOL * 128, 512)
    cvf = f(cache_v).reshape(2, NPOOL * 128, 512)
    consts = _consts()
    shared = dict(ck=ckf, cv=cvf, consts=consts, w_ada=f(w_ada), b_ada=f(b_ada), n1w=f(norm1_w), n2w=f(norm2_w),
                  w_in=f(w_in), vnw=f(gmlp_vnorm_w), w_s=f(gmlp_w_s), b_s=f(gmlp_b_s), sbb=f(sb_bias),
                  lbl=f(hgrn_lb_logits), onw=f(hgrn_onorm_w), w_ba=f(w_branch_a), w_bb=f(w_branch_b),
                  w_bc=f(w_branch_c), w_out=f(w_out), w_ff1=f(w_ff1), w_ff2=f(w_ff2),
                  fnw=f(final_norm_w).reshape(1, D))
    cpf = f(c_prompt); csf = f(c_sample); stf = f(state_hgrn)
    pti = np.ascontiguousarray(np.asarray(page_table, dtype=np.int32))
    in_maps = []
    for c in range(8):
        b = c % 4
        xs_t = np.zeros((128, D), np.float32); xs_t[:NS] = x_sample[c * NS:(c + 1) * NS, 0, :]
        cs_t = np.zeros((128, D), np.float32); cs_t[:NS] = csf[c * NS:(c + 1) * NS]
        m = dict(shared)
        m.update(xp=x_prompt[b], xs=xs_t, cp=np.ascontiguousarray(np.broadcast_to(cpf[b], (128, D))), cs=cs_t,
                 st=np.ascontiguousarray(stf[:, c * NS:(c + 1) * NS]),
                 pt=np.ascontiguousarray(pti[c * NS:(c + 1) * NS].reshape(1, NS * NPG)))
        in_maps.append(m)
    res = run_bass_kernel_spmd(nc, in_maps, core_ids=list(range(8))).results
    y_prompt = np.stack([res[b]["yp"] for b in range(4)])
    y_sample = np.concatenate([res[c]["ys"] for c in range(8)])[:, None, :]
    k_prompt = np.stack([res[b]["kp"] for b in range(4)], axis=1).reshape(2, 4, SEQ, 8, 64)
    v_prompt = np.stack([res[b]["vp"] for b in range(4)], axis=1).reshape(2, 4, SEQ, 8, 64)
    S_prompt = np.stack([res[b]["Sp"] for b in range(4)], axis=1)
    k_sample = np.concatenate([res[c]["ks"] for c in range(8)], axis=1).reshape(2, DB, 1, 8, 64)
    v_sample = np.concatenate([res[c]["vs"] for c in range(8)], axis=1).reshape(2, DB, 1, 8, 64)
    S_sample = np.concatenate([res[c]["Ss"] for c in range(8)], axis=1)
    g_sample = np.concatenate([res[c]["gv"] for c in range(8)], axis=1)[:, :, None, :]
    return (y_prompt, y_sample, k_prompt, v_prompt, S_prompt, k_sample, v_sample, S_sample, g_sample)
```

```python
import numpy as np
from contextlib import ExitStack
import concourse.bass as bass
import concourse.mybir as mybir
from concourse.bass_utils import run_bass_kernel_spmd

F32 = mybir.dt.float32
BF16 = mybir.dt.bfloat16
I32 = mybir.dt.int32
AF = mybir.ActivationFunctionType
ALU = mybir.AluOpType
AX = mybir.AxisListType
D = 1024
EPS = 1e-6
NS = 16


class Res:
    __slots__ = ("name", "w", "rs")

    def __init__(self, name):
        self.name = name
        self.w = None
        self.rs = {}


class Prog:
    def __init__(self, nc, es):
        self.nc = nc
        self.es = es
        self.eng = {"pe": nc.tensor, "act": nc.scalar, "dve": nc.vector, "pool": nc.gpsimd, "sp": nc.sync}
        self.esem = {k: es.enter_context(nc.semaphore("e_" + k)) for k in self.eng}
        self.cnt = {k: 0 for k in self.eng}
        self.seen = {k: {} for k in self.eng}
        self.dsem = {}
        self.dcnt = {}
        self.sems = {}
        self.store_sems = {}

    def _waits(self, e, reads, writes, skip=None, selfsync=True):
        need = {}

        def add(tok):
            if tok is None:
                return
            s, v = tok
            if s is skip:
                return
            if need.get(id(s), (None, 0))[1] < v:
                need[id(s)] = (s, v)

        for r in reads:
            add(r.w)
        for w in writes:
            add(w.w)
            for t in w.rs.values():
                add(t)
        eng = self.eng[e]
        for sid, (s, v) in need.items():
            if (not selfsync) and s is self.esem[e]:
                continue
            if self.seen[e].get(sid, 0) >= v:
                continue
            eng.wait_ge(s, v)
            self.seen[e][sid] = v

    def _commit(self, tok, reads, writes):
        for r in reads:
            r.rs[id(tok[0])] = tok
        for w in writes:
            w.w = tok
            w.rs = {}

    def op(self, e, fn, reads=(), writes=(), selfsync=True):
        self._waits(e, reads, writes, selfsync=selfsync)
        ins = fn(self.eng[e])
        self.cnt[e] += 1
        ins.then_inc(self.esem[e], 1)
        self._commit((self.esem[e], self.cnt[e]), reads, writes)

    def dma(self, q, fn, reads, writes, store=False):
        owner = reads[0] if store else writes[0]
        key = (id(owner), store)
        if key not in self.dsem:
            self.dsem[key] = self.es.enter_context(self.nc.semaphore(("s_" if store else "d_") + owner.name))
            self.dcnt[key] = 0
        sem = self.dsem[key]
        self._waits(q, reads, writes, skip=sem)
        ins = fn(self.eng[q])
        self.dcnt[key] += 16
        ins.then_inc(sem, 16)
        if store:
            self.store_sems[id(sem)] = (sem, self.dcnt[key])
        self._commit((sem, self.dcnt[key]), reads, writes)

    def wait_all(self, e, ress):
        self._waits(e, ress, ress)
        for sid, (sem, v) in self.store_sems.items():
            if self.seen[e].get(sid, 0) < v:
                self.eng[e].wait_ge(sem, v)
                self.seen[e][sid] = v


def build(SEQ, NPG, NPOOL, NT):
    nc = bass.Bass("TRN2", target_bir_lowering=False)
    es = ExitStack()
    P = Prog(nc, es)
    NTILES = SEQ // 128
    NG = NTILES // NT
    NP = NT * 128
    NCOL = NS * NPG

    def din(name, shape, dt=F32):
        return nc.dram_tensor(name, list(shape), dt, kind="ExternalInput")

    def dout(name, shape, dt=F32):
        return nc.dram_tensor(name, list(shape), dt, kind="ExternalOutput")

    def dscr(name, shape, dt):
        return nc.dram_tensor(name, list(shape), dt)

    def SB(name, shape, dt):
        return es.enter_context(nc.sbuf_tensor(name, list(shape), dt))

    def PS(name, shape, dt):
        return es.enter_context(nc.psum_tensor(name, list(shape), dt))

    xp = din("xp", [SEQ, D]); xs = din("xs", [128, D]); cp = din("cp", [128, D]); cs = din("cs", [128, D])
    ck = din("ck", [2, NPOOL * 128, 512]); cv = din("cv", [2, NPOOL * 128, 512])
    st = din("st", [2, NS, 4, 128, 128]); pt = din("pt", [1, NCOL], I32)
    consts = din("consts", [128, 1413])
    w_ada = din("w_ada", [2, D, 6144]); b_ada = din("b_ada", [2, 6144])
    n1w = din("n1w", [2, D]); n2w = din("n2w", [2, D]); w_in = din("w_in", [2, D, 7680])
    vnw = din("vnw", [2, 512]); w_s = din("w_s", [2, 4, 128, 128]); b_s = din("b_s", [2, 4, 128])
    sbb = din("sbb", [2, 8]); lbl = din("lbl", [2, 512]); onw = din("onw", [2, 512])
    w_ba = din("w_ba", [2, 512, D]); w_bb = din("w_bb", [2, 512, D]); w_bc = din("w_bc", [2, 512, D])
    w_out = din("w_out", [2, D, D]); w_ff1 = din("w_ff1", [2, D, 4096]); w_ff2 = din("w_ff2", [2, 4096, D])
    fnw = din("fnw", [1, D])

    yp = dout("yp", [SEQ, D]); ys = dout("ys", [NS, D])
    kp = dout("kp", [2, SEQ, 512]); vp = dout("vp", [2, SEQ, 512]); Sp = dout("Sp", [2, 4, 128, 128])
    ks = dout("ks", [2, NS, 512]); vs = dout("vs", [2, NS, 512]); Ss = dout("Ss", [2, NS, 4, 128, 128])
    gv = dout("gv", [2, NS, 512])
    outs_res = {n: Res(n) for n in ["yp", "ys", "kp", "vp", "Sp", "ks", "vs", "Ss", "gv"]}

    wsp = {"ada": (w_ada, D, 6144), "in": (w_in, D, 7680), "ba": (w_ba, 512, D), "bb": (w_bb, 512, D),
           "bc": (w_bc, 512, D), "out": (w_out, D, D), "ff1": (w_ff1, D, 4096), "ff2": (w_ff2, 4096, D)}
    wb = {}
    wres = {}
    for l in range(2):
        for k, (src, K, C) in wsp.items():
            wb[(l, k)] = dscr(f"wb_{k}{l}", [K, C], BF16)
            wres[(l, k)] = Res(f"wb_{k}{l}")
    x1 = dscr("x1", [SEQ + 128, D], F32); x1_res = [Res(f"x1_{t}") for t in range(SEQ // 128 + 1)]
    qscr = dscr("qscr", [NS, 512], F32); qscr_res = Res("qscr")
    vscr = dscr("vscr", [NS, 512], F32); vscr_res = Res("vscr")

    cst = SB("cst", [128, 1413], F32); cst_r = Res("cst")
    ident_f = cst[:, 0:128]; iota_c = cst[:, 1152:1153]
    cb = SB("cb", [128, 1152], BF16); cb_r = Res("cb")
    ident_b = cb[:, 0:128]; tri_b = cb[:, 128:256]; ones_b = cb[:, 256:384]; mstrict_b = cb[:, 384:512]
    bt32x4_b = cb[:, 512:1024]; causal_b = cb[:, 1024:1152]
    bt32_f = cst[:, 512:640]; bmid_f = cst[:, 1153:1281]; rev32_f = cst[:, 1281:1409]; submask = cst[:, 1409:1413]
    zeros_b = SB("zeros_b", [128, NP], BF16); zeros_r = Res("zeros")

    KT = SB("KT", [128, 4, SEQ], BF16); KT_r = [Res(f"KT{t}") for t in range(NTILES)]
    VH = SB("VH", [128, NTILES, 512], BF16); VH_r = [Res(f"VH{t}") for t in range(NTILES)]
    xg = SB("xg", [128, NT, D], F32); xg_r = [Res(f"xg{i}") for i in range(NT)]
    mod = SB("mod", [128, 6144], BF16); mod_r = Res("mod")
    HT = SB("HT", [128, 8, NP], BF16); HT_r = Res("HT")
    NW = 2
    wring = [SB(f"wr{i}", [128, 4096], BF16) for i in range(NW)]; wring_r = [Res(f"wr{i}") for i in range(NW)]
    wr_i = [0]
    Pb = [SB(f"Pb{i}", [128, 4, NP], BF16) for i in range(3)]; Pb_r = [Res(f"Pb{i}") for i in range(3)]
    OA = SB("OA", [128, 4, NP], BF16); OA_r = Res("OA")
    OB = SB("OB", [128, 4, NP], BF16); OB_r = Res("OB")
    OC = SB("OC", [128, 4, NP], BF16); OC_r = Res("OC")
    BIG = SB("BIG", [128, 24, NP], BF16); BIG_r = [Res(f"BIG{j}") for j in range(24)]
    FB = SB("FB", [128, NT, 512], F32); FB_r = [Res(f"FB{i}") for i in range(NT)]
    NTMP = 6
    tmpf = [SB(f"tf{i}", [128, 1024], F32) for i in range(NTMP)]; tmpf_r = [Res(f"tf{i}") for i in range(NTMP)]
    tf_i = [0]
    NTB = 8
    tmpb = [SB(f"tb{i}", [128, 1024], BF16) for i in range(NTB)]; tmpb_r = [Res(f"tb{i}") for i in range(NTB)]
    tb_i = [0]
    NSM = 8
    small = [SB(f"sm{i}", [128, 8], F32) for i in range(NSM)]; small_r = [Res(f"sm{i}") for i in range(NSM)]
    sm_i = [0]
    fT_d = SB("fT_d", [128, 1024], F32); fT_dr = Res("fT_d")
    S_f = SB("S_f", [128, 4, 128], F32); S_fr = [Res(f"S_f{h}") for h in range(4)]
    S_b = SB("S_b", [128, 4, 128], BF16); S_br = [Res(f"S_b{h}") for h in range(4)]
    Rf4 = SB("Rf4", [128, 4, NP], F32); Rf4_r = [Res(f"Rf{h}") for h in range(4)]
    Rb4 = SB("Rb4", [128, 4, NP], BF16); Rb4_r = [Res(f"Rb{h}") for h in range(4)]
    WT = SB("WT", [128, 4, 128], BF16); WT_r = Res("WT")
    bsrow = SB("bsrow", [1, 512], BF16); bsrow_r = Res("bsrow")
    lc = SB("lc", [128, 64], F32); lc_r = Res("lc")
    lbt = SB("lbt", [128, 512], F32); lbt_r = Res("lbt")
    oml = SB("oml", [128, 512], F32); oml_r = Res("oml")
    vnb = SB("vnb", [128, 512], F32); vnb_r = Res("vnb")
    IDX = SB("IDX", [128, NCOL], mybir.dt.uint32); IDX_r = Res("IDX")
    PTs = SB("PTs", [128, NCOL], I32); PTs_r = Res("PTs")
    rmask = SB("rmask", [128, 8 * NPG], F32); rmask_r = Res("rmask")
    Sb_t = [SB(f"Sbt{i}", [128, 4, 128], F32) for i in range(1)]; Sb_r = [Res(f"Sbt{i}") for i in range(1)]
    assert NTILES >= NPG
    Vpg = VH; Vpg_r = VH_r
    QB = [FB[:, i % NT, :] for i in range(2)]; QB_r = [FB_r[i % NT] for i in range(2)]
    zb = SB("zb", [128, 8, NPG], F32); zb_r = Res("zb")
    eb_s = SB("eb_s", [128, 8, NPG], F32); eb_sr = Res("eb_s")
    spb_s = SB("spb_s", [128, 8 * NPG], BF16); spb_sr = Res("spb_s")
    wb_s = SB("wb_s", [128, 8, NPG], BF16); wb_sr = Res("wb_s")

    NPS = 5
    psr = [PS(f"ps{i}", [128, 512], F32) for i in range(NPS)]; psr_r = [Res(f"ps{i}") for i in range(NPS)]
    ps_i = [0]
    psT = PS("psT", [128, 1024], BF16); psT_r = Res("psT")
    psAcc = PS("psAcc", [128, 512], F32); psAcc_r = Res("psAcc")
    psAcc2 = PS("psAcc2", [128, 512], F32); psAcc2_r = Res("psAcc2")

    def ring(lst, rl, idx):
        i = idx[0] % len(lst)
        idx[0] += 1
        return lst[i], rl[i]

    def psum():
        return ring(psr, psr_r, ps_i)

    def tf():
        return ring(tmpf, tmpf_r, tf_i)

    def tbf():
        return ring(tmpb, tmpb_r, tb_i)

    def sm():
        return ring(small, small_r, sm_i)

    def mm(out, lhsT, rhs, start, stop, R, W):
        P.op("pe", lambda e: e.matmul(out, lhsT=lhsT, rhs=rhs, start=start, stop=stop), reads=R, writes=W,
             selfsync=False)

    def tr(out, in_, ident, R, W):
        P.op("pe", lambda e: e.transpose(out=out, in_=in_, identity=ident), reads=R, writes=W, selfsync=False)

    def act(out, in_, func, R, W, bias=None, scale=None, accum=None):
        kw = {}
        if bias is not None:
            kw["bias"] = bias
        if scale is not None:
            kw["scale"] = scale
        if accum is not None:
            kw["accum_out"] = accum
        P.op("act", lambda e: e.activation(out=out, in_=in_, func=func, **kw), reads=R, writes=W)

    def tt(out, a, b, op, R, W, eng="dve"):
        P.op(eng, lambda e: e.tensor_tensor(out=out, in0=a, in1=b, op=op), reads=R, writes=W)

    def ts(out, a, s1, s2, op0, op1, R, W, eng="dve"):
        if op1 is None:
            P.op(eng, lambda e: e.tensor_scalar(out=out, in0=a, scalar1=s1, scalar2=None, op0=op0), reads=R, writes=W)
        else:
            P.op(eng, lambda e: e.tensor_scalar(out=out, in0=a, scalar1=s1, scalar2=s2, op0=op0, op1=op1),
                 reads=R, writes=W)

    def stt(out, a, s, b, op0, op1, R, W):
        P.op("dve", lambda e: e.scalar_tensor_tensor(out=out, in0=a, scalar=s, in1=b, op0=op0, op1=op1),
             reads=R, writes=W)

    def cp_(out, in_, R, W, eng="dve"):
        if eng == "act":
            P.op(eng, lambda e: e.activation(out=out, in_=in_, func=AF.Identity), reads=R, writes=W)
        else:
            P.op(eng, lambda e: e.tensor_copy(out=out, in_=in_), reads=R, writes=W)

    def recip(out, in_, R, W):
        P.op("dve", lambda e: e.reciprocal(out=out, in_=in_), reads=R, writes=W)

    def mset(ap, val, W, eng="dve"):
        P.op(eng, lambda e: e.memset(ap, val), reads=(), writes=W)

    def ld(out, in_, R, W, q="sp"):
        P.dma(q, lambda e: e.dma_start(out=out, in_=in_), reads=R, writes=W)

    ld(cst[:], consts.ap(), [], [cst_r])
    cp_(cb[:], cst[:, 0:1152], [cst_r], [cb_r])
    mset(zeros_b[:], 0.0, [zeros_r])
    for l in range(2):
        for k, (src, K, C) in wsp.items():
            for r0 in range(0, K, 128):
                P.dma("pool", lambda e, l=l, k=k, src=src, r0=r0: e.dma_start(
                    out=wb[(l, k)].ap()[r0:r0 + 128, :], in_=src.ap()[l, r0:r0 + 128, :], max_dma_last_dim=4096),
                    reads=[], writes=[wres[(l, k)]])
    ld(PTs[:], pt.ap().partition_broadcast(128), [], [PTs_r])
    ts(IDX[:], PTs[:], 128.0, iota_c, ALU.mult, ALU.add, [PTs_r, cst_r], [IDX_r])
    mset(rmask[:], 1.0, [rmask_r])
    for h in range(8):
        mset(rmask[:, h * NPG:h * NPG + 1], 0.0, [rmask_r])

    def wchunk(l, k, r0, nr, c0, ncol):
        slot, sr = ring(wring, wring_r, wr_i)
        view = slot[:, 0:nr * ncol].rearrange("p (a b) -> p a b", a=nr)
        src = wb[(l, k)].ap()[r0:r0 + nr * 128, c0:c0 + ncol].rearrange("(a p) c -> p a c", p=128)
        ld(view, src, [wres[(l, k)]], [sr])
        return view, sr

    def bcast_row(dst, dst_r, src_ap):
        ld(dst, src_ap.partition_broadcast(128), [], [dst_r])

    def layer_consts(l):
        for g in range(4):
            t1, t1r = tf()
            ld(t1[:, 0:128], w_s.ap()[l, g], [], [t1r])
            ps, pr = psum()
            tr(ps[:, 0:128], t1[:, 0:128], ident_f, [t1r, cst_r], [pr])
            tt(WT[:, g, :], ps[:, 0:128], causal_b, ALU.mult, [pr, cb_r], [WT_r])
        t1, t1r = tf()
        ld(t1[0:1, 0:512], b_s.ap()[l:l + 1].rearrange("o g t -> o (g t)"), [], [t1r])
        cp_(bsrow[:], t1[0:1, 0:512], [t1r], [bsrow_r])
        bcast_row(lc[:, 0:8], lc_r, sbb.ap()[l:l + 1, :])
        for h in range(4):
            ld(lc[:, 8 + h:9 + h], onw.ap()[l, h * 128:(h + 1) * 128].rearrange("(p o) -> p o", o=1), [], [lc_r])
            ld(lc[:, 20 + h:21 + h], w_s.ap()[l, h, 0:1, 0:1].partition_broadcast(128), [], [lc_r])
            ld(lc[:, 24 + h:25 + h], b_s.ap()[l, h:h + 1, 0:1].partition_broadcast(128), [], [lc_r])
        bcast_row(vnb[:], vnb_r, vnw.ap()[l:l + 1, :])
        if l == 0:
            mset(lbt[:], 0.0, [lbt_r]); mset(oml[:], 1.0, [oml_r])
            mset(lc[:, 12:16], 0.0, [lc_r]); mset(lc[:, 16:20], 1.0, [lc_r])
        else:
            a, ar = tf(); b, br = tf()
            bcast_row(a[:, 0:512], ar, lbl.ap()[0:1, :])
            bcast_row(b[:, 0:512], br, lbl.ap()[1:2, :])
            tt(a[:, 0:512], b[:, 0:512], a[:, 0:512], ALU.subtract, [ar, br], [ar])
            act(lbt[:], a[:, 0:512], AF.Sigmoid, [ar], [lbt_r])
            ts(oml[:], lbt[:], -1.0, 1.0, ALU.mult, ALU.add, [lbt_r], [oml_r])
            c, cr = tf()
            for h in range(4):
                ld(c[:, h:h + 1], lbl.ap()[0, h * 128:(h + 1) * 128].rearrange("(p o) -> p o", o=1), [], [cr])
                ld(c[:, 4 + h:5 + h], lbl.ap()[1, h * 128:(h + 1) * 128].rearrange("(p o) -> p o", o=1), [], [cr])
            tt(c[:, 8:12], c[:, 4:8], c[:, 0:4], ALU.subtract, [cr], [cr])
            act(lc[:, 12:16], c[:, 8:12], AF.Sigmoid, [cr], [lc_r])
            ts(lc[:, 16:20], lc[:, 12:16], -1.0, 1.0, ALU.mult, ALU.add, [lc_r], [lc_r])
        for h in range(4):
            mset(S_f[:, h, :], 0.0, [S_fr[h]])
            mset(S_b[:, h, :], 0.0, [S_br[h]])

    def compute_mod(l, csrc):
        c, cr = tf()
        ld(c[:], csrc.ap(), [], [cr])
        c2, c2r = tbf()
        act(c2[:], c[:], AF.Silu, [cr], [c2r])
        for kc in range(8):
            tr(psT[:, kc * 128:(kc + 1) * 128], c2[:, kc * 128:(kc + 1) * 128], ident_b, [c2r, cb_r], [psT_r])
        cT2, cTb_r = tbf()
        cTb = cT2[:, :].rearrange("p (a b) -> p a b", a=8)
        cp_(cTb, psT[:, :].rearrange("p (a b) -> p a b", a=8), [psT_r], [cTb_r])
        for blk in range(12):
            wv, wr = wchunk(l, "ada", 0, 8, blk * 512, 512)
            ps, pr = psum()
            for kc in range(8):
                mm(ps[:], cTb[:, kc, :], wv[:, kc, :], kc == 0, kc == 7, [cTb_r, wr], [pr])
            bb, bbr = tf()
            bcast_row(bb[:, 0:512], bbr, b_ada.ap()[l:l + 1, blk * 512:(blk + 1) * 512])
            which = blk // 2
            if which in (1, 4):
                nsrc = n1w if which == 1 else n2w
                off = (blk % 2) * 512
                bcast_row(bb[:, 512:1024], bbr, nsrc.ap()[l:l + 1, off:off + 512])
                t2, t2r = tf()
                tt(t2[:, 0:512], ps[:], bb[:, 0:512], ALU.add, [pr, bbr], [t2r])
                stt(mod[:, blk * 512:(blk + 1) * 512], t2[:, 0:512], 1.0, bb[:, 512:1024], ALU.add, ALU.mult,
                    [t2r, bbr], [mod_r])
            else:
                tt(mod[:, blk * 512:(blk + 1) * 512], ps[:], bb[:, 0:512], ALU.add, [pr, bbr], [mod_r])

    SH1, SC1, G1, SH2, SC2, G2 = 0, 1024, 2048, 3072, 4096, 5120

    def norm_to_HT(T, a_off, b_off):
        for i in range(T):
            junk, jr = tbf()
            s1, s1r = sm()
            act(junk[:], xg[:, i, :], AF.Square, [xg_r[i]], [jr, s1r], accum=s1[:, 0:1])
            act(s1[:, 1:2], s1[:, 0:1], AF.Sqrt, [s1r], [s1r], bias=EPS, scale=1.0 / D)
            recip(s1[:, 2:3], s1[:, 1:2], [s1r], [s1r])
            t1, t1r = tf()
            stt(t1[:], xg[:, i, :], s1[:, 2:3], mod[:, a_off:a_off + D], ALU.mult, ALU.mult, [xg_r[i], s1r, mod_r], [t1r])
            hb, hbr = tbf()
            tt(hb[:], t1[:], mod[:, b_off:b_off + D], ALU.add, [t1r, mod_r], [hbr])
            for kc in range(8):
                tr(psT[:, kc * 128:(kc + 1) * 128], hb[:, kc * 128:(kc + 1) * 128], ident_b, [hbr, cb_r], [psT_r])
            cp_(HT[:, :, i * 128:(i + 1) * 128], psT[:, :].rearrange("p (a b) -> p a b", a=8), [psT_r], [HT_r],
                eng="act")

    def proj_F(wv, wr, N, evac, src=None, src_r=None, nkc=8):
        src = HT if src is None else src
        src_r = HT_r if src_r is None else src_r
        for j in range(4):
            ps, pr = psum()
            for kc in range(nkc):
                mm(ps[:, 0:N], wv[:, kc, j * 128:(j + 1) * 128], src[:, kc, 0:N], kc == 0, kc == nkc - 1,
                   [wr, src_r], [pr])
            evac(j, ps, pr)

    def proj_T(wv, wr, T, evac):
        for i in range(T):
            ps, pr = psum()
            for kc in range(8):
                mm(ps[:], HT[:, kc, i * 128:(i + 1) * 128], wv[:, kc, :], kc == 0, kc == 7, [wr, HT_r], [pr])
            evac(i, ps, pr)

    def tokview(buf):
        return buf[:, :, :].rearrange("p a b -> p (a b)").rearrange("p (t c) -> p t c", c=512)

    def process_group(l, gi, sample):
        T = 1 if sample else NT
        N = T * 128
        tok0 = SEQ if sample else gi * NP
        for i in range(T):
            if l == 0:
                src = xs.ap() if sample else xp.ap()[gi * NP + i * 128: gi * NP + (i + 1) * 128, :]
                ld(xg[:, i, :], src, [], [xg_r[i]])
            else:
                ld(xg[:, i, :], x1.ap()[tok0 + i * 128: tok0 + (i + 1) * 128, :], [x1_res[tok0 // 128 + i]], [xg_r[i]])
        norm_to_HT(T, SC1, SH1)
        uT, uT_r = Pb[0], Pb_r[0]
        va, va_r = tokview(Pb[1]), Pb_r[1]

        wv, wr = wchunk(l, "in", 0, 8, 0, 512)
        proj_F(wv, wr, N, lambda j, ps, pr: act(uT[:, j, 0:N], ps[:, 0:N], AF.Gelu, [pr], [uT_r]))
        wv, wr = wchunk(l, "in", 0, 8, 512, 512)

        def ev_av(i, ps, pr):
            ga, gar = tf()
            act(ga[:, 0:512], ps[:], AF.Gelu, [pr], [gar])
            s1, s1r = sm()
            act(ga[:, 512:1024], ga[:, 0:512], AF.Square, [gar], [gar, s1r], accum=s1[:, 0:1])
            act(s1[:, 1:2], s1[:, 0:1], AF.Sqrt, [s1r], [s1r], bias=EPS, scale=1.0 / 512)
            recip(s1[:, 2:3], s1[:, 1:2], [s1r], [s1r])
            stt(ga[:, 512:1024], ga[:, 0:512], s1[:, 2:3], vnb[:], ALU.mult, ALU.mult, [gar, s1r, vnb_r], [gar])
            cp_(va[:, i, :], ga[:, 512:1024], [gar], [va_r], eng="act")
            if sample:
                P.dma("pool", lambda e: e.dma_start(out=gv.ap()[l], in_=ga[0:NS, 512:1024]), [gar], [outs_res["gv"]], store=True)
        proj_T(wv, wr, T, ev_av)
        for i in range(T):
            ps, pr = psum()
            for g in range(4):
                if sample:
                    mm(ps[:, g * 128:(g + 1) * 128], va[:, i, g * 128:(g + 1) * 128], ident_b, True, True,
                       [va_r, cb_r], [pr])
                else:
                    mm(ps[:, g * 128:(g + 1) * 128], va[:, i, g * 128:(g + 1) * 128], WT[:, g, :], True, False,
                       [va_r, WT_r], [pr])
                    mm(ps[:, g * 128:(g + 1) * 128], ones_b[0:1, :], bsrow[0:1, g * 128:(g + 1) * 128], False, True,
                       [cb_r, bsrow_r], [pr])
            if sample:
                t1, t1r = tf()
                for g in range(4):
                    ts(t1[:, g * 128:(g + 1) * 128], ps[:, g * 128:(g + 1) * 128], lc[:, 20 + g:21 + g],
                       lc[:, 24 + g:25 + g], ALU.mult, ALU.add, [pr, lc_r], [t1r])
                tt(OA[:, :, i * 128:(i + 1) * 128], t1[:, 0:512].rearrange("p (a b) -> p a b", a=4),
                   uT[:, :, i * 128:(i + 1) * 128], ALU.mult, [t1r, uT_r], [OA_r])
            else:
                tt(OA[:, :, i * 128:(i + 1) * 128], ps[:, :].rearrange("p (a b) -> p a b", a=4),
                   uT[:, :, i * 128:(i + 1) * 128], ALU.mult, [pr, uT_r], [OA_r])

        qT, qT_r = Pb[0], Pb_r[0]
        vb, vb_r = tokview(Pb[1]), Pb_r[1]
        gT, gT_r = Pb[2], Pb_r[2]
        wv, wr = wchunk(l, "in", 0, 8, 1024, 512)
        proj_F(wv, wr, N, lambda j, ps, pr: act(qT[:, j, 0:N], ps[:, 0:N], AF.Silu, [pr], [qT_r]))
        wv, wr = wchunk(l, "in", 0, 8, 1536, 512)
        fT, fT_r = fT_d, fT_dr
        if sample:
            def ev_f(j, ps, pr):
                sg, sgr = tf()
                act(sg[:, 0:128], ps[:, 0:128], AF.Sigmoid, [pr], [sgr])
                ts(fT[:, j * 128:(j + 1) * 128], sg[:, 0:128], lc[:, 16 + j:17 + j], lc[:, 12 + j:13 + j], ALU.mult,
                   ALU.add, [sgr, lc_r], [fT_r])
                ts(fT[:, 512 + j * 128:512 + (j + 1) * 128], fT[:, j * 128:(j + 1) * 128], -1.0, 1.0, ALU.mult, ALU.add,
                   [fT_r], [fT_r])
            proj_F(wv, wr, N, ev_f)
        else:
            proj_T(wv, wr, T, lambda i, ps, pr: cp_(FB[:, i, :], ps[:], [pr], [FB_r[i]], eng="act"))
        wv, wr = wchunk(l, "in", 0, 8, 2048, 512)
        if sample:
            def ev_i(i, ps, pr):
                t1, t1r = tf()
                cp_(t1[:, 0:512], ps[:], [pr], [t1r], eng="act")
                P.dma("pool", lambda e: e.dma_start(out=vscr.ap(), in_=t1[0:NS, 0:512]), [t1r], [vscr_res], store=True)
            proj_T(wv, wr, T, ev_i)
        else:
            proj_T(wv, wr, T, lambda i, ps, pr: cp_(vb[:, i, :], ps[:], [pr], [vb_r], eng="act"))
        wv, wr = wchunk(l, "in", 0, 8, 2560, 512)
        proj_F(wv, wr, N, lambda j, ps, pr: act(gT[:, j, 0:N], ps[:, 0:N], AF.Silu, [pr], [gT_r]))

        def onorm(psO, psO_r, ncols, dst, tcols):
            W4 = 4 * ncols
            o2, o2r = tbf()
            act(o2[:, 0:W4], psO[:, 0:W4], AF.Square, [psO_r], [o2r])
            pss, pssr = psum()
            mm(pss[:, 0:W4], ones_b, o2[:, 0:W4], True, True, [cb_r, o2r], [pssr])
            rs, rsr = tf()
            act(rs[:, 0:W4], pss[:, 0:W4], AF.Sqrt, [pssr], [rsr], bias=EPS, scale=1.0 / 128)
            recip(rs[:, 0:W4], rs[:, 0:W4], [rsr], [rsr])
            for h in range(4):
                stt(rs[:, 512 + h * ncols:512 + (h + 1) * ncols], psO[:, h * ncols:(h + 1) * ncols], lc[:, 8 + h:9 + h],
                    rs[:, h * ncols:(h + 1) * ncols], ALU.mult, ALU.mult, [psO_r, lc_r, rsr], [rsr])
            tt(dst, rs[:, 512:512 + W4].rearrange("p (a b) -> p a b", a=4), tcols, ALU.mult, [rsr, gT_r], [OB_r])

        if sample:
            mset(OB[:, :, :], 0.0, [OB_r])
            for b in range(NS):
                Sb, Sbr = Sb_t[0], Sb_r[0]
                ld(Sb[:], st.ap()[l, b].rearrange("h k v -> k h v"), [], [Sbr])
                vB, vBr = QB[b % 2], QB_r[b % 2]
                ld(vB, vscr.ap()[b:b + 1, :].partition_broadcast(128), [vscr_res], [vBr])
                sn, snr = tf()
                for h in range(4):
                    ts(sn[:, 512 + h * 128:512 + (h + 1) * 128], vB[:, h * 128:(h + 1) * 128],
                       fT[:, 512 + h * 128 + b:512 + h * 128 + b + 1], None, ALU.mult, None, [vBr, fT_r], [snr])
                    stt(sn[:, h * 128:(h + 1) * 128], Sb[:, h, :], fT[:, h * 128 + b:h * 128 + b + 1],
                        sn[:, 512 + h * 128:512 + (h + 1) * 128], ALU.mult, ALU.add, [Sbr, fT_r, snr], [snr])
                P.dma("pool", lambda e, b=b, sn=sn: e.dma_start(
                    out=Ss.ap()[l, b].rearrange("h k v -> k h v"),
                    in_=sn[:, 0:512].rearrange("p (a b) -> p a b", a=4)), [snr], [outs_res["Ss"]], store=True)
                snb, snbr = tbf()
                cp_(snb[:, 0:512], sn[:, 0:512], [snr], [snbr], eng="act")
                for h in range(4):
                    mm(psAcc[:, h * NS + b:h * NS + b + 1], snb[:, h * 128:(h + 1) * 128], qT[:, h, b:b + 1], True, True,
                       [snbr, qT_r], [psAcc_r])
            onorm(psAcc, psAcc_r, NS, OB[:, :, 0:NS], gT[:, :, 0:NS])
        else:
            for i in range(T):
                sg, sgr = tf()
                act(sg[:, 0:512], FB[:, i, :], AF.Sigmoid, [FB_r[i]], [sgr])
                tt(sg[:, 0:512], sg[:, 0:512], oml[:], ALU.mult, [sgr, oml_r], [sgr])
                tt(sg[:, 0:512], sg[:, 0:512], lbt[:], ALU.add, [sgr, lbt_r], [sgr])
                act(sg[:, 512:1024], sg[:, 0:512], AF.Ln, [sgr], [sgr])
                lf = sg[:, 512:1024]
                psb, psbr = psum()
                mm(psb[:], bmid_f, lf, True, True, [cst_r, sgr], [psbr])
                e1, e1r = tf()
                ts(e1[:, 0:512], psb[:], -80.0, 80.0, ALU.max, ALU.min, [psbr], [e1r])
                act(e1[:, 0:512], e1[:, 0:512], AF.Exp, [e1r], [e1r], scale=-1.0)
                ts(e1[:, 512:1024], sg[:, 0:512], -1.0, 1.0, ALU.mult, ALU.add, [sgr], [e1r])
                kh, khr = tbf()
                tt(kh[:, 0:512], e1[:, 512:1024], e1[:, 0:512], ALU.mult, [e1r], [khr])
                psr_, psrr = psum()
                mm(psr_[:], rev32_f, lf, True, True, [cst_r, sgr], [psrr])
                e2, e2r = tf()
                act(e2[:, 0:512], psr_[:], AF.Exp, [psrr], [e2r])
                kd, kdr = tbf()
                tt(kd[:, 0:512], e1[:, 512:1024], e2[:, 0:512], ALU.mult, [e1r, e2r], [kdr])
                for h in range(4):
                    tr(psT[:, h * 128:(h + 1) * 128], kh[:, h * 128:(h + 1) * 128], ident_b, [khr, cb_r], [psT_r])
                cp_(kh[:, 512:1024], psT[:, 0:512], [psT_r], [khr], eng="act")
                psbt, psbtr = psum()
                for h in range(4):
                    mm(psbt[:, h * 128:(h + 1) * 128], sg[:, 512 + h * 128:512 + (h + 1) * 128], bmid_f, True, True,
                       [sgr, cst_r], [psbtr])
                eb, ebr = tf()
                ts(eb[:, 0:512], psbt[:], -80.0, 80.0, ALU.max, ALU.min, [psbtr], [ebr])
                act(eb[:, 0:512], eb[:, 0:512], AF.Exp, [ebr], [ebr])
                psbs, psbsr = psum()
                for h in range(4):
                    mm(psbs[:, h * 128:(h + 1) * 128], sg[:, 512 + h * 128:512 + (h + 1) * 128], bt32_f, True, True,
                       [sgr, cst_r], [psbsr])
                act(eb[:, 512:1024], psbs[:], AF.Exp, [psbsr], [ebr])
                ql, qlr = tbf()
                tt(ql[:, 0:512].rearrange("p (a b) -> p a b", a=4), qT[:, :, i * 128:(i + 1) * 128],
                   eb[:, 0:512].rearrange("p (a b) -> p a b", a=4), ALU.mult, [qT_r, ebr], [qlr])
                qs, qsr = tbf()
                tt(qs[:, 0:512].rearrange("p (a b) -> p a b", a=4), qT[:, :, i * 128:(i + 1) * 128],
                   eb[:, 512:1024].rearrange("p (a b) -> p a b", a=4), ALU.mult, [qT_r, ebr], [qsr])
                psA, psAr = psum()
                for h in range(4):
                    mm(psA[:, h * 128:(h + 1) * 128], kh[:, 512 + h * 128:512 + (h + 1) * 128],
                       ql[:, h * 128:(h + 1) * 128], True, True, [khr, qlr], [psAr])
                at_, atr = tf()
                ts(at_[:, 0:512], psA[:], 1e30, -1e30, ALU.min, ALU.max, [psAr], [atr])
                tt(ql[:, 512:1024], at_[:, 0:512], bt32x4_b, ALU.mult, [atr, cb_r], [qlr])
                vm = []
                for jj in range(2):
                    vmt, vmr = tbf()
                    for j2 in range(2):
                        j = jj * 2 + j2
                        ts(vmt[:, j2 * 512:(j2 + 1) * 512], vb[:, i, :], submask[:, j:j + 1], None, ALU.mult, None,
                           [vb_r, cst_r], [vmr])
                        vm.append((vmt[:, j2 * 512:(j2 + 1) * 512], vmr))
                for h in range(4):
                    hs = slice(h * 128, (h + 1) * 128)
                    mm(psAcc[:, hs], vb[:, i, hs], ql[:, 512 + h * 128:512 + (h + 1) * 128], True, False,
                       [vb_r, qlr], [psAcc_r])
                    for j in range(4):
                        c0 = h * 128 + j * 32
                        mm(psAcc[:, c0:c0 + 32], S_b[:, h, :], qs[:, c0:c0 + 32], False, j == 3,
                           [S_br[h], qsr], [psAcc_r])
                        psd, psdr = psum()
                        mm(psd[:, 0:128], kd[:, hs], vm[j][0][:, hs], True, True, [kdr, vm[j][1]], [psdr])
                        ecol = 512 + c0 + 31
                        stt(S_f[:, h, :], S_f[:, h, :], eb[:, ecol:ecol + 1], psd[:, 0:128], ALU.mult, ALU.add,
                            [S_fr[h], ebr, psdr], [S_fr[h]])
                        cp_(S_b[:, h, :], S_f[:, h, :], [S_fr[h]], [S_br[h]], eng="act")
                onorm(psAcc, psAcc_r, 128, OB[:, :, i * 128:(i + 1) * 128], gT[:, :, i * 128:(i + 1) * 128])
            if gi == NG - 1:
                P.dma("pool", lambda e: e.dma_start(out=Sp.ap()[l].rearrange("h k v -> k h v"), in_=S_f[:]),
                      S_fr, [outs_res["Sp"]], store=True)

        cqT, cqT_r = Pb[0], Pb_r[0]
        wv, wr = wchunk(l, "in", 0, 8, 3072, 512)
        if sample:
            def ev_q(i, ps, pr):
                t1, t1r = tf()
                cp_(t1[:, 0:512], ps[:], [pr], [t1r], eng="act")
                P.dma("pool", lambda e: e.dma_start(out=qscr.ap(), in_=t1[0:NS, 0:512]), [t1r], [qscr_res], store=True)
            proj_T(wv, wr, T, ev_q)
        else:
            proj_F(wv, wr, N, lambda j, ps, pr: cp_(cqT[:, j, 0:N], ps[:, 0:N], [pr], [cqT_r], eng="act"))
        wv, wr = wchunk(l, "in", 0, 8, 3584, 512)

        def ev_k(i, ps, pr):
            kf, kfr = tf()
            cp_(kf[:, 0:512], ps[:], [pr], [kfr], eng="act")
            if sample:
                P.dma("pool", lambda e: e.dma_start(out=ks.ap()[l], in_=kf[0:NS, 0:512]), [kfr], [outs_res["ks"]], store=True)
            else:
                ti = gi * NT + i
                P.dma("pool", lambda e: e.dma_start(out=kp.ap()[l, ti * 128:(ti + 1) * 128, :], in_=kf[:, 0:512]),
                      [kfr], [outs_res["kp"]], store=True)
                ps2, p2r = psum()
                for j in range(4):
                    tr(ps2[:, j * 128:(j + 1) * 128], kf[:, j * 128:(j + 1) * 128], ident_f, [kfr, cst_r], [p2r])
                cp_(KT[:, :, ti * 128:(ti + 1) * 128], ps2[:, :].rearrange("p (a b) -> p a b", a=4), [p2r], [KT_r[ti]])
        proj_T(wv, wr, T, ev_k)
        wv, wr = wchunk(l, "in", 0, 8, 4096, 512)

        def ev_v(i, ps, pr):
            vf, vfr = tf()
            cp_(vf[:, 0:512], ps[:], [pr], [vfr], eng="act")
            if sample:
                P.dma("pool", lambda e: e.dma_start(out=vs.ap()[l], in_=vf[0:NS, 0:512]), [vfr], [outs_res["vs"]], store=True)
            else:
                ti = gi * NT + i
                P.dma("pool", lambda e: e.dma_start(out=vp.ap()[l, ti * 128:(ti + 1) * 128, :], in_=vf[:, 0:512]),
                      [vfr], [outs_res["vp"]], store=True)
                cp_(VH[:, ti, :], vf[:, 0:512], [vfr], [VH_r[ti]])
        proj_T(wv, wr, T, ev_v)

        if sample:
            mset(OC[:, :, :], 0.0, [OC_r])
            for b in range(NS):
                qb, qbr = QB[b % 2], QB_r[b % 2]
                ld(qb, qscr.ap()[b:b + 1, :].partition_broadcast(128), [qscr_res], [qbr])
                for p in range(NPG):
                    col = b * NPG + p
                    r = NPG - 1 - p
                    kpg_t, kpr = tf()
                    kpg = kpg_t[:, 0:512]
                    P.dma("pool", lambda e, kpg=kpg, col=col: e.indirect_dma_start(
                        out=kpg, out_offset=None, in_=ck.ap().rearrange("l r c -> (l r) c"),
                        in_offset=bass.IndirectOffsetOnAxis(ap=IDX[:, col:col + 1], axis=0),
                        element_offset=l * NPOOL * 128 * 512), [IDX_r], [kpr])
                    pr_, prr = tf()
                    tt(pr_[:, 0:512], kpg, qb, ALU.mult, [kpr, qbr], [prr])
                    P.op("dve", lambda e, pr_=pr_, r=r: e.tensor_reduce(
                        out=zb[:, :, r], in_=pr_[:, 0:512].rearrange("p (a b) -> p a b", a=8), axis=AX.X, op=ALU.add),
                        reads=[prr], writes=[zb_r])
                vbase = (b % 2) * NPG if NTILES >= 2 * NPG else 0
                for p in range(NPG):
                    col = b * NPG + p
                    P.dma("pool", lambda e, p=p, col=col, vbase=vbase: e.indirect_dma_start(
                        out=Vpg[:, vbase + p, :], out_offset=None, in_=cv.ap().rearrange("l r c -> (l r) c"),
                        in_offset=bass.IndirectOffsetOnAxis(ap=IDX[:, col:col + 1], axis=0),
                        element_offset=l * NPOOL * 128 * 512), [IDX_r], [Vpg_r[vbase + p]])
                for h in range(8):
                    act(eb_s[:, h, :], zb[:, h, :], AF.Exp, [zb_r, lc_r], [eb_sr], bias=lc[:, h:h + 1], scale=0.125)
                act(spb_s[:], eb_s[:, :, :].rearrange("p a b -> p (a b)"), AF.Ln, [eb_sr], [spb_sr], bias=1.0)
                psg, psgr = psum()
                mm(psg[:, 0:8 * NPG], tri_b, spb_s[:], True, True, [cb_r, spb_sr], [psgr])
                pst, pstr = psum()
                mm(pst[:, 0:8 * NPG], ones_b, spb_s[:], True, True, [cb_r, spb_sr], [pstr])
                t1, t1r = tf()
                W8 = 8 * NPG
                P.op("dve", lambda e, t1=t1, pst=pst: e.tensor_tensor_scan(
                    out=t1[:, 0:W8], data0=rmask[:], data1=pst[:, 0:W8], initial=0.0, op0=ALU.mult, op1=ALU.add),
                    reads=[rmask_r, pstr], writes=[t1r])
                tt(t1[:, 0:W8], t1[:, 0:W8], pst[:, 0:W8], ALU.subtract, [t1r, pstr], [t1r])
                tt(t1[:, 0:W8], t1[:, 0:W8], psg[:, 0:W8], ALU.add, [t1r, psgr], [t1r])
                act(t1[:, 0:W8], t1[:, 0:W8], AF.Exp, [t1r], [t1r], scale=-1.0)
                tt(wb_s[:, :, :].rearrange("p a b -> p (a b)"), t1[:, 0:W8],
                   eb_s[:, :, :].rearrange("p a b -> p (a b)"), ALU.mult, [t1r, eb_sr], [wb_sr])
                for h in range(8):
                    hp, po = h // 2, (h % 2) * 64
                    for p in range(NPG):
                        r = NPG - 1 - p
                        mm(psAcc2[po:po + 64, hp * NS + b:hp * NS + b + 1], Vpg[:, vbase + p, h * 64:(h + 1) * 64],
                           wb_s[:, h, r:r + 1], p == 0, p == NPG - 1, [Vpg_r[vbase + p], wb_sr], [psAcc2_r])
            cp_(OC[:, :, 0:NS], psAcc2[:, 0:4 * NS].rearrange("p (a b) -> p a b", a=4), [psAcc2_r], [OC_r], eng="act")
        else:
            last = gi * NT + NT - 1
            pack = N <= 256
            for q in range(2):
                heads = list(range(4 * q, 4 * q + 4))
                accs = {}
                for h in heads:
                    hp, po = h // 2, (h % 2) * 64
                    acc, accr = (psAcc2, psAcc2_r) if (h % 4) < 2 else (psAcc, psAcc_r)
                    accs[h] = (acc, accr)
                    mset(Rf4[:, h % 4, :], 0.0, [Rf4_r[h % 4]], eng="pool")
                    mset(Rb4[:, h % 4, :], 0.0, [Rb4_r[h % 4]], eng="pool")
                    mm(acc[po:po + 64, 0:N], zeros_b[:, 0:64], zeros_b[:, 0:N], True, False, [zeros_r], [accr])
                for kt in range(last, -1, -1):
                    jq = max(0, kt - gi * NT)
                    c0 = jq * 128
                    diag = kt >= gi * NT
                    first = kt == last
                    stg = {}
                    for h in heads:
                        hp, po = h // 2, (h % 2) * 64
                        psz, pszr = psum()
                        if pack:
                            psg, psgr, go = psz, pszr, 256
                        else:
                            psg, psgr = psum()
                            go = 0
                        mm(psz[:, c0:N], KT[po:po + 64, hp, kt * 128:(kt + 1) * 128], cqT[po:po + 64, hp, c0:N],
                           True, True, [KT_r[kt], cqT_r], [pszr])
                        e_, er = tf()
                        act(e_[:, c0:N], psz[:, c0:N], AF.Exp, [pszr, lc_r], [er], bias=lc[:, h:h + 1], scale=0.125)
                        sp_, spr = tbf()
                        act(sp_[:, c0:N], e_[:, c0:N], AF.Ln, [er], [spr], bias=1.0)
                        if diag:
                            tt(sp_[:, c0:c0 + 128], sp_[:, c0:c0 + 128], mstrict_b, ALU.mult, [spr, cb_r], [spr])
                        stg[h] = (psg, psgr, go, e_, er, sp_, spr)
                    for h in heads:
                        psg, psgr, go, e_, er, sp_, spr = stg[h]
                        Rb, Rb_r = Rb4[:, h % 4, :], Rb4_r[h % 4]
                        mm(psg[:, go + c0:go + N], tri_b, sp_[:, c0:N], True, first, [cb_r, spr], [psgr])
                        if not first:
                            mm(psg[:, go + c0:go + N], ones_b, Rb[:, c0:N], False, True, [cb_r, Rb_r], [psgr])
                        act(psg[:, go + c0:go + N], psg[:, go + c0:go + N], AF.Exp, [psgr], [psgr], scale=-1.0)
                    for h in heads:
                        hp, po = h // 2, (h % 2) * 64
                        acc, accr = accs[h]
                        psg, psgr, go, e_, er, sp_, spr = stg[h]
                        Rf, Rf_r, Rb, Rb_r = Rf4[:, h % 4, :], Rf4_r[h % 4], Rb4[:, h % 4, :], Rb4_r[h % 4]
                        tt(sp_[:, 512 + c0:512 + N], e_[:, c0:N], psg[:, go + c0:go + N], ALU.mult, [er, psgr], [spr])
                        if diag:
                            tt(sp_[:, 512 + c0:512 + c0 + 128], sp_[:, 512 + c0:512 + c0 + 128], mstrict_b, ALU.mult,
                               [spr, cb_r], [spr])
                        mm(acc[po:po + 64, c0:N], VH[:, kt, h * 64:(h + 1) * 64], sp_[:, 512 + c0:512 + N], False,
                           kt == 0, [VH_r[kt], spr], [accr])
                    for h in heads:
                        psg, psgr, go, e_, er, sp_, spr = stg[h]
                        Rf, Rf_r, Rb, Rb_r = Rf4[:, h % 4, :], Rf4_r[h % 4], Rb4[:, h % 4, :], Rb4_r[h % 4]
                        if kt > 0:
                            tt(Rf[:, c0:N], Rf[:, c0:N], sp_[:, c0:N], ALU.add, [Rf_r, spr], [Rf_r])
                            cp_(Rb[:, c0:N], Rf[:, c0:N], [Rf_r], [Rb_r])
                for h in heads:
                    hp, po = h // 2, (h % 2) * 64
                    acc, accr = accs[h]
                    cp_(OC[po:po + 64, hp, 0:N], acc[po:po + 64, 0:N], [accr], [OC_r], eng="act")

        for c in range(6):
            wv, wr = wchunk(l, "in", 0, 8, 4608 + c * 512, 512)
            proj_F(wv, wr, N, lambda j, ps, pr, c=c: act(BIG[:, c * 4 + j, 0:N], ps[:, 0:N], AF.Sigmoid, [pr],
                                                        [BIG_r[c * 4 + j]]))
        for xi, (wk, osrc, osr) in enumerate([("ba", OA, OA_r), ("bb", OB, OB_r), ("bc", OC, OC_r)]):
            for half in range(2):
                wv, wr = wchunk(l, wk, 0, 4, half * 512, 512)

                def ev_b(j, ps, pr, xi=xi, half=half):
                    jj = half * 4 + j
                    gate = BIG[:, xi * 8 + jj, 0:N]
                    if xi == 0:
                        tt(HT[:, jj, 0:N], ps[:, 0:N], gate, ALU.mult, [pr, BIG_r[xi * 8 + jj]], [HT_r])
                    else:
                        t1, t1r = tf()
                        tt(t1[:, 0:N], ps[:, 0:N], gate, ALU.mult, [pr, BIG_r[xi * 8 + jj]], [t1r])
                        tt(HT[:, jj, 0:N], HT[:, jj, 0:N], t1[:, 0:N], ALU.add, [HT_r, t1r], [HT_r])
                proj_F(wv, wr, N, ev_b, src=osrc, src_r=osr, nkc=4)
        for half in range(2):
            wv, wr = wchunk(l, "out", 0, 8, half * 512, 512)

            def ev_o(i, ps, pr, half=half):
                t1, t1r = tf()
                tt(t1[:, 0:512], ps[:], mod[:, G1 + half * 512:G1 + (half + 1) * 512], ALU.mult, [pr, mod_r], [t1r])
                tt(xg[:, i, half * 512:(half + 1) * 512], xg[:, i, half * 512:(half + 1) * 512], t1[:, 0:512], ALU.add,
                   [xg_r[i], t1r], [xg_r[i]])
            proj_T(wv, wr, T, ev_o)
        norm_to_HT(T, SC2, SH2)
        for half in range(2):
            for c in range(4):
                wv, wr = wchunk(l, "ff1", 0, 8, half * 2048 + c * 512, 512)

                def ev_f1(j, ps, pr, c=c):
                    r_, rr = tbf()
                    act(r_[:, 0:N], ps[:, 0:N], AF.Relu, [pr], [rr])
                    tt(BIG[:, c * 4 + j, 0:N], r_[:, 0:N], r_[:, 0:N], ALU.mult, [rr], [BIG_r[c * 4 + j]], eng="pool")
                proj_F(wv, wr, N, ev_f1)
            for c in range(4):
                slot, sr = ring(wring, wring_r, wr_i)
                view = slot[:, :].rearrange("p (a b) -> p a b", a=16)
                src = wb[(l, "ff2")].ap()[half * 2048:(half + 1) * 2048, c * 256:(c + 1) * 256].rearrange(
                    "(a p) c -> p a c", p=128)
                ld(view, src, [wres[(l, "ff2")]], [sr])
                for i in range(T):
                    ps, pr = psum()
                    for kc in range(16):
                        mm(ps[:, 0:256], BIG[:, kc, i * 128:(i + 1) * 128], view[:, kc, :], kc == 0, kc == 15,
                           [BIG_r[kc], sr], [pr])
                    t1, t1r = tf()
                    tt(t1[:, 0:256], ps[:, 0:256], mod[:, G2 + c * 256:G2 + (c + 1) * 256], ALU.mult, [pr, mod_r], [t1r])
                    tt(xg[:, i, c * 256:(c + 1) * 256], xg[:, i, c * 256:(c + 1) * 256], t1[:, 0:256], ALU.add,
                       [xg_r[i], t1r], [xg_r[i]])
        for i in range(T):
            if l == 0:
                P.dma("pool", lambda e, i=i: e.dma_start(out=x1.ap()[tok0 + i * 128: tok0 + (i + 1) * 128, :],
                                                         in_=xg[:, i, :]), [xg_r[i]], [x1_res[tok0 // 128 + i]], store=True)
            else:
                junk, jr = tbf()
                s1, s1r = sm()
                act(junk[:], xg[:, i, :], AF.Square, [xg_r[i]], [jr, s1r], accum=s1[:, 0:1])
                act(s1[:, 1:2], s1[:, 0:1], AF.Sqrt, [s1r], [s1r], bias=EPS, scale=1.0 / D)
                recip(s1[:, 2:3], s1[:, 1:2], [s1r], [s1r])
                fb_, fbr = tf()
                bcast_row(fb_[:], fbr, fnw.ap())
                y_, yr = tf()
                stt(y_[:], xg[:, i, :], s1[:, 2:3], fb_[:], ALU.mult, ALU.mult, [xg_r[i], s1r, fbr], [yr])
                if sample:
                    P.dma("pool", lambda e, y_=y_: e.dma_start(out=ys.ap(), in_=y_[0:NS, :]), [yr], [outs_res["ys"]], store=True)
                else:
                    P.dma("pool", lambda e, y_=y_, i=i: e.dma_start(
                        out=yp.ap()[gi * NP + i * 128: gi * NP + (i + 1) * 128, :], in_=y_[:]), [yr], [outs_res["yp"]], store=True)

    for l in range(2):
        layer_consts(l)
        compute_mod(l, cp)
        for gi in range(NG):
            process_group(l, gi, False)
        compute_mod(l, cs)
        process_group(l, 0, True)
    P.wait_all("pool", list(outs_res.values()))
    es.close()
    return nc


_CACHE = {}


def _consts():
    c = np.zeros((128, 1413), np.float32)
    i = np.arange(128)
    s, t = i[:, None], i[None, :]
    c[:, 0:128] = np.eye(128)
    c[:, 128:256] = (s >= t)
    c[:, 256:384] = 1.0
    c[:, 384:512] = (s < t)
    same = (s // 32) == (t // 32)
    bt = ((s <= t) & same).astype(np.float32)
    mid = (t // 32) * 32 + 15
    c[:, 1153:1281] = (((s <= t) & same).astype(np.float32) - ((s <= mid) & same).astype(np.float32))
    c[:, 1281:1409] = ((s > t) & same)
    for j in range(4):
        c[:, 1409 + j] = ((i // 32) == j)
    c[:, 512:1024] = np.tile(bt, (1, 4))
    c[:, 1024:1152] = (s <= t)
    c[:, 1152] = i
    return c


def kernel(x_prompt, x_sample, c_prompt, c_sample, cache_k, cache_v, state_hgrn, page_table,
           w_ada, b_ada, norm1_w, norm2_w, w_in, gmlp_vnorm_w, gmlp_w_s, gmlp_b_s, sb_bias,
           hgrn_lb_logits, hgrn_onorm_w, w_branch_a, w_branch_b, w_branch_c, w_out,
           w_ff1, w_ff2, final_norm_w, NT=2):
    f = lambda a: np.ascontiguousarray(np.asarray(a, dtype=np.float32))
    x_prompt = f(x_prompt); x_sample = f(x_sample)
    B, SEQ, _ = x_prompt.shape
    DB = x_sample.shape[0]
    NPG = page_table.shape[1]
    NPOOL = cache_k.shape[1]
    assert DB == 8 * NS and B == 4
    key = (SEQ, NPG, NPOOL, NT)
    if key not in _CACHE:
        _CACHE[key] = build(SEQ, NPG, NPOOL, NT)
    nc = _CACHE[key]
    ckf = f(cache_k).reshape(2, NPOOL * 128, 512)
    cvf = f(cache_v).reshape(2, NPOOL * 128, 512)
    consts = _consts()
    shared = dict(ck=ckf, cv=cvf, consts=consts, w_ada=f(w_ada), b_ada=f(b_ada), n1w=f(norm1_w), n2w=f(norm2_w),
                  w_in=f(w_in), vnw=f(gmlp_vnorm_w), w_s=f(gmlp_w_s), b_s=f(gmlp_b_s), sbb=f(sb_bias),
                  lbl=f(hgrn_lb_logits), onw=f(hgrn_onorm_w), w_ba=f(w_branch_a), w_bb=f(w_branch_b),
                  w_bc=f(w_branch_c), w_out=f(w_out), w_ff1=f(w_ff1), w_ff2=f(w_ff2),
                  fnw=f(final_norm_w).reshape(1, D))
    cpf = f(c_prompt); csf = f(c_sample); stf = f(state_hgrn)
    pti = np.ascontiguousarray(np.asarray(page_table, dtype=np.int32))
    in_maps = []
    for c in range(8):
        b = c % 4
        xs_t = np.zeros((128, D), np.float32); xs_t[:NS] = x_sample[c * NS:(c + 1) * NS, 0, :]
        cs_t = np.zeros((128, D), np.float32); cs_t[:NS] = csf[c * NS:(c + 1) * NS]
        m = dict(shared)
        m.update(xp=x_prompt[b], xs=xs_t, cp=np.ascontiguousarray(np.broadcast_to(cpf[b], (128, D))), cs=cs_t,
                 st=np.ascontiguousarray(stf[:, c * NS:(c + 1) * NS]),
                 pt=np.ascontiguousarray(pti[c * NS:(c + 1) * NS].reshape(1, NS * NPG)))
        in_maps.append(m)
    res = run_bass_kernel_spmd(nc, in_maps, core_ids=list(range(8))).results
    y_prompt = np.stack([res[b]["yp"] for b in range(4)])
    y_sample = np.concatenate([res[c]["ys"] for c in range(8)])[:, None, :]
    k_prompt = np.stack([res[b]["kp"] for b in range(4)], axis=1).reshape(2, 4, SEQ, 8, 64)
    v_prompt = np.stack([res[b]["vp"] for b in range(4)], axis=1).reshape(2, 4, SEQ, 8, 64)
    S_prompt = np.stack([res[b]["Sp"] for b in range(4)], axis=1)
    k_sample = np.concatenate([res[c]["ks"] for c in range(8)], axis=1).reshape(2, DB, 1, 8, 64)
    v_sample = np.concatenate([res[c]["vs"] for c in range(8)], axis=1).reshape(2, DB, 1, 8, 64)
    S_sample = np.concatenate([res[c]["Ss"] for c in range(8)], axis=1)
    g_sample = np.concatenate([res[c]["gv"] for c in range(8)], axis=1)[:, :, None, :]
    return (y_prompt, y_sample, k_prompt, v_prompt, S_prompt, k_sample, v_sample, S_sample, g_sample)
```

```python
import numpy as np
from contextlib import ExitStack
import concourse.bass as bass
import concourse.mybir as mybir
from concourse.bass_utils import run_bass_kernel_spmd

F32 = mybir.dt.float32
BF16 = mybir.dt.bfloat16
I32 = mybir.dt.int32
AF = mybir.ActivationFunctionType
ALU = mybir.AluOpType
AX = mybir.AxisListType
D = 1024
EPS = 1e-6
NS = 16


class Res:
    __slots__ = ("name", "w", "rs")

    def __init__(self, name):
        self.name = name
        self.w = None
        self.rs = {}


class Prog:
    def __init__(self, nc, es):
        self.nc = nc
        self.es = es
        self.eng = {"pe": nc.tensor, "act": nc.scalar, "dve": nc.vector, "pool": nc.gpsimd, "sp": nc.sync}
        self.esem = {k: es.enter_context(nc.semaphore("e_" + k)) for k in self.eng}
        self.cnt = {k: 0 for k in self.eng}
        self.seen = {k: {} for k in self.eng}
        self.dsem = {}
        self.dcnt = {}
        self.sems = {}
        self.store_sems = {}

    def _waits(self, e, reads, writes, skip=None, selfsync=True):
        need = {}

        def add(tok):
            if tok is None:
                return
            s, v = tok
            if s is skip:
                return
            if need.get(id(s), (None, 0))[1] < v:
                need[id(s)] = (s, v)

        for r in reads:
            add(r.w)
        for w in writes:
            add(w.w)
            for t in w.rs.values():
                add(t)
        eng = self.eng[e]
        for sid, (s, v) in need.items():
            if (not selfsync) and s is self.esem[e]:
                continue
            if self.seen[e].get(sid, 0) >= v:
                continue
            eng.wait_ge(s, v)
            self.seen[e][sid] = v

    def _commit(self, tok, reads, writes):
        for r in reads:
            r.rs[id(tok[0])] = tok
        for w in writes:
            w.w = tok
            w.rs = {}

    def op(self, e, fn, reads=(), writes=(), selfsync=True):
        self._waits(e, reads, writes, selfsync=selfsync)
        ins = fn(self.eng[e])
        self.cnt[e] += 1
        ins.then_inc(self.esem[e], 1)
        self._commit((self.esem[e], self.cnt[e]), reads, writes)

    def dma(self, q, fn, reads, writes, store=False):
        owner = reads[0] if store else writes[0]
        key = (id(owner), store)
        if key not in self.dsem:
            self.dsem[key] = self.es.enter_context(self.nc.semaphore(("s_" if store else "d_") + owner.name))
            self.dcnt[key] = 0
        sem = self.dsem[key]
        self._waits(q, reads, writes, skip=sem)
        ins = fn(self.eng[q])
        self.dcnt[key] += 16
        ins.then_inc(sem, 16)
        if store:
            self.store_sems[id(sem)] = (sem, self.dcnt[key])
        self._commit((sem, self.dcnt[key]), reads, writes)

    def wait_all(self, e, ress):
        self._waits(e, ress, ress)
        for sid, (sem, v) in self.store_sems.items():
            if self.seen[e].get(sid, 0) < v:
                self.eng[e].wait_ge(sem, v)
                self.seen[e][sid] = v


def build(SEQ, NPG, NPOOL, NT):
    nc = bass.Bass("TRN2", target_bir_lowering=False)
    es = ExitStack()
    P = Prog(nc, es)
    NTILES = SEQ // 128
    NG = NTILES // NT
    NP = NT * 128
    NCOL = NS * NPG

    def din(name, shape, dt=F32):
        return nc.dram_tensor(name, list(shape), dt, kind="ExternalInput")

    def dout(name, shape, dt=F32):
        return nc.dram_tensor(name, list(shape), dt, kind="ExternalOutput")

    def dscr(name, shape, dt):
        return nc.dram_tensor(name, list(shape), dt)

    def SB(name, shape, dt):
        return es.enter_context(nc.sbuf_tensor(name, list(shape), dt))

    def PS(name, shape, dt):
        return es.enter_context(nc.psum_tensor(name, list(shape), dt))

    xp = din("xp", [SEQ, D]); xs = din("xs", [128, D]); cp = din("cp", [128, D]); cs = din("cs", [128, D])
    ck = din("ck", [2, NPOOL * 128, 512]); cv = din("cv", [2, NPOOL * 128, 512])
    st = din("st", [2, NS, 4, 128, 128]); pt = din("pt", [1, NCOL], I32)
    consts = din("consts", [128, 1413])
    w_ada = din("w_ada", [2, D, 6144]); b_ada = din("b_ada", [2, 6144])
    n1w = din("n1w", [2, D]); n2w = din("n2w", [2, D]); w_in = din("w_in", [2, D, 7680])
    vnw = din("vnw", [2, 512]); w_s = din("w_s", [2, 4, 128, 128]); b_s = din("b_s", [2, 4, 128])
    sbb = din("sbb", [2, 8]); lbl = din("lbl", [2, 512]); onw = din("onw", [2, 512])
    w_ba = din("w_ba", [2, 512, D]); w_bb = din("w_bb", [2, 512, D]); w_bc = din("w_bc", [2, 512, D])
    w_out = din("w_out", [2, D, D]); w_ff1 = din("w_ff1", [2, D, 4096]); w_ff2 = din("w_ff2", [2, 4096, D])
    fnw = din("fnw", [1, D])

    yp = dout("yp", [SEQ, D]); ys = dout("ys", [NS, D])
    kp = dout("kp", [2, SEQ, 512]); vp = dout("vp", [2, SEQ, 512]); Sp = dout("Sp", [2, 4, 128, 128])
    ks = dout("ks", [2, NS, 512]); vs = dout("vs", [2, NS, 512]); Ss = dout("Ss", [2, NS, 4, 128, 128])
    gv = dout("gv", [2, NS, 512])
    outs_res = {n: Res(n) for n in ["yp", "ys", "kp", "vp", "Sp", "ks", "vs", "Ss", "gv"]}

    wsp = {"ada": (w_ada, D, 6144), "in": (w_in, D, 7680), "ba": (w_ba, 512, D), "bb": (w_bb, 512, D),
           "bc": (w_bc, 512, D), "out": (w_out, D, D), "ff1": (w_ff1, D, 4096), "ff2": (w_ff2, 4096, D)}
    wb = {}
    wres = {}
    for l in range(2):
        for k, (src, K, C) in wsp.items():
            wb[(l, k)] = dscr(f"wb_{k}{l}", [K, C], BF16)
            wres[(l, k)] = Res(f"wb_{k}{l}")
    x1 = dscr("x1", [SEQ + 128, D], F32); x1_res = [Res(f"x1_{t}") for t in range(SEQ // 128 + 1)]
    qscr = dscr("qscr", [NS, 512], F32); qscr_res = Res("qscr")
    vscr = dscr("vscr", [NS, 512], F32); vscr_res = Res("vscr")

    cst = SB("cst", [128, 1413], F32); cst_r = Res("cst")
    ident_f = cst[:, 0:128]; iota_c = cst[:, 1152:1153]
    cb = SB("cb", [128, 1152], BF16); cb_r = Res("cb")
    ident_b = cb[:, 0:128]; tri_b = cb[:, 128:256]; ones_b = cb[:, 256:384]; mstrict_b = cb[:, 384:512]
    bt32x4_b = cb[:, 512:1024]; causal_b = cb[:, 1024:1152]
    bt32_f = cst[:, 512:640]; bmid_f = cst[:, 1153:1281]; rev32_f = cst[:, 1281:1409]; submask = cst[:, 1409:1413]
    zeros_b = SB("zeros_b", [128, NP], BF16); zeros_r = Res("zeros")

    KT = SB("KT", [128, 4, SEQ], BF16); KT_r = [Res(f"KT{t}") for t in range(NTILES)]
    VH = SB("VH", [128, NTILES, 512], BF16); VH_r = [Res(f"VH{t}") for t in range(NTILES)]
    xg = SB("xg", [128, NT, D], F32); xg_r = [Res(f"xg{i}") for i in range(NT)]
    mod = SB("mod", [128, 6144], BF16); mod_r = Res("mod")
    HT = SB("HT", [128, 8, NP], BF16); HT_r = Res("HT")
    NW = 2
    wring = [SB(f"wr{i}", [128, 4096], BF16) for i in range(NW)]; wring_r = [Res(f"wr{i}") for i in range(NW)]
    wr_i = [0]
    Pb = [SB(f"Pb{i}", [128, 4, NP], BF16) for i in range(3)]; Pb_r = [Res(f"Pb{i}") for i in range(3)]
    OA = SB("OA", [128, 4, NP], BF16); OA_r = Res("OA")
    OB = SB("OB", [128, 4, NP], BF16); OB_r = Res("OB")
    OC = SB("OC", [128, 4, NP], BF16); OC_r = Res("OC")
    BIG = SB("BIG", [128, 24, NP], BF16); BIG_r = [Res(f"BIG{j}") for j in range(24)]
    FB = SB("FB", [128, NT, 512], F32); FB_r = [Res(f"FB{i}") for i in range(NT)]
    NTMP = 6
    tmpf = [SB(f"tf{i}", [128, 1024], F32) for i in range(NTMP)]; tmpf_r = [Res(f"tf{i}") for i in range(NTMP)]
    tf_i = [0]
    NTB = 8
    tmpb = [SB(f"tb{i}", [128, 1024], BF16) for i in range(NTB)]; tmpb_r = [Res(f"tb{i}") for i in range(NTB)]
    tb_i = [0]
    NSM = 8
    small = [SB(f"sm{i}", [128, 8], F32) for i in range(NSM)]; small_r = [Res(f"sm{i}") for i in range(NSM)]
    sm_i = [0]
    fT_d = SB("fT_d", [128, 1024], F32); fT_dr = Res("fT_d")
    S_f = SB("S_f", [128, 4, 128], F32); S_fr = [Res(f"S_f{h}") for h in range(4)]
    S_b = SB("S_b", [128, 4, 128], BF16); S_br = [Res(f"S_b{h}") for h in range(4)]
    Rf4 = SB("Rf4", [128, 4, NP], F32); Rf4_r = [Res(f"Rf{h}") for h in range(4)]
    Rb4 = SB("Rb4", [128, 4, NP], BF16); Rb4_r = [Res(f"Rb{h}") for h in range(4)]
    WT = SB("WT", [128, 4, 128], BF16); WT_r = Res("WT")
    bsrow = SB("bsrow", [1, 512], BF16); bsrow_r = Res("bsrow")
    lc = SB("lc", [128, 64], F32); lc_r = Res("lc")
    lbt = SB("lbt", [128, 512], F32); lbt_r = Res("lbt")
    oml = SB("oml", [128, 512], F32); oml_r = Res("oml")
    vnb = SB("vnb", [128, 512], F32); vnb_r = Res("vnb")
    IDX = SB("IDX", [128, NCOL], mybir.dt.uint32); IDX_r = Res("IDX")
    PTs = SB("PTs", [128, NCOL], I32); PTs_r = Res("PTs")
    rmask = SB("rmask", [128, 8 * NPG], F32); rmask_r = Res("rmask")
    Sb_t = [SB(f"Sbt{i}", [128, 4, 128], F32) for i in range(1)]; Sb_r = [Res(f"Sbt{i}") for i in range(1)]
    assert NTILES >= NPG
    Vpg = VH; Vpg_r = VH_r
    QB = [FB[:, i % NT, :] for i in range(2)]; QB_r = [FB_r[i % NT] for i in range(2)]
    zb = SB("zb", [128, 8, NPG], F32); zb_r = Res("zb")
    eb_s = SB("eb_s", [128, 8, NPG], F32); eb_sr = Res("eb_s")
    spb_s = SB("spb_s", [128, 8 * NPG], BF16); spb_sr = Res("spb_s")
    wb_s = SB("wb_s", [128, 8, NPG], BF16); wb_sr = Res("wb_s")

    NPS = 5
    psr = [PS(f"ps{i}", [128, 512], F32) for i in range(NPS)]; psr_r = [Res(f"ps{i}") for i in range(NPS)]
    ps_i = [0]
    psT = PS("psT", [128, 1024], BF16); psT_r = Res("psT")
    psAcc = PS("psAcc", [128, 512], F32); psAcc_r = Res("psAcc")
    psAcc2 = PS("psAcc2", [128, 512], F32); psAcc2_r = Res("psAcc2")

    def ring(lst, rl, idx):
        i = idx[0] % len(lst)
        idx[0] += 1
        return lst[i], rl[i]

    def psum():
        return ring(psr, psr_r, ps_i)

    def tf():
        return ring(tmpf, tmpf_r, tf_i)

    def tbf():
        return ring(tmpb, tmpb_r, tb_i)

    def sm():
        return ring(small, small_r, sm_i)

    def mm(out, lhsT, rhs, start, stop, R, W):
        P.op("pe", lambda e: e.matmul(out, lhsT=lhsT, rhs=rhs, start=start, stop=stop), reads=R, writes=W,
             selfsync=False)

    def tr(out, in_, ident, R, W):
        P.op("pe", lambda e: e.transpose(out=out, in_=in_, identity=ident), reads=R, writes=W, selfsync=False)

    def act(out, in_, func, R, W, bias=None, scale=None, accum=None):
        kw = {}
        if bias is not None:
            kw["bias"] = bias
        if scale is not None:
            kw["scale"] = scale
        if accum is not None:
            kw["accum_out"] = accum
        P.op("act", lambda e: e.activation(out=out, in_=in_, func=func, **kw), reads=R, writes=W)

    def tt(out, a, b, op, R, W, eng="dve"):
        P.op(eng, lambda e: e.tensor_tensor(out=out, in0=a, in1=b, op=op), reads=R, writes=W)

    def ts(out, a, s1, s2, op0, op1, R, W, eng="dve"):
        if op1 is None:
            P.op(eng, lambda e: e.tensor_scalar(out=out, in0=a, scalar1=s1, scalar2=None, op0=op0), reads=R, writes=W)
        else:
            P.op(eng, lambda e: e.tensor_scalar(out=out, in0=a, scalar1=s1, scalar2=s2, op0=op0, op1=op1),
                 reads=R, writes=W)

    def stt(out, a, s, b, op0, op1, R, W):
        P.op("dve", lambda e: e.scalar_tensor_tensor(out=out, in0=a, scalar=s, in1=b, op0=op0, op1=op1),
             reads=R, writes=W)

    def cp_(out, in_, R, W, eng="dve"):
        if eng == "act":
            P.op(eng, lambda e: e.activation(out=out, in_=in_, func=AF.Identity), reads=R, writes=W)
        else:
            P.op(eng, lambda e: e.tensor_copy(out=out, in_=in_), reads=R, writes=W)

    def recip(out, in_, R, W):
        P.op("dve", lambda e: e.reciprocal(out=out, in_=in_), reads=R, writes=W)

    def mset(ap, val, W, eng="dve"):
        P.op(eng, lambda e: e.memset(ap, val), reads=(), writes=W)

    def ld(out, in_, R, W, q="sp"):
        P.dma(q, lambda e: e.dma_start(out=out, in_=in_), reads=R, writes=W)

    ld(cst[:], consts.ap(), [], [cst_r])
    cp_(cb[:], cst[:, 0:1152], [cst_r], [cb_r])
    mset(zeros_b[:], 0.0, [zeros_r])
    for l in range(2):
        for k, (src, K, C) in wsp.items():
            for r0 in range(0, K, 128):
                P.dma("pool", lambda e, l=l, k=k, src=src, r0=r0: e.dma_start(
                    out=wb[(l, k)].ap()[r0:r0 + 128, :], in_=src.ap()[l, r0:r0 + 128, :], max_dma_last_dim=4096),
                    reads=[], writes=[wres[(l, k)]])
    ld(PTs[:], pt.ap().partition_broadcast(128), [], [PTs_r])
    ts(IDX[:], PTs[:], 128.0, iota_c, ALU.mult, ALU.add, [PTs_r, cst_r], [IDX_r])
    mset(rmask[:], 1.0, [rmask_r])
    for h in range(8):
        mset(rmask[:, h * NPG:h * NPG + 1], 0.0, [rmask_r])

    def wchunk(l, k, r0, nr, c0, ncol):
        slot, sr = ring(wring, wring_r, wr_i)
        view = slot[:, 0:nr * ncol].rearrange("p (a b) -> p a b", a=nr)
        src = wb[(l, k)].ap()[r0:r0 + nr * 128, c0:c0 + ncol].rearrange("(a p) c -> p a c", p=128)
        ld(view, src, [wres[(l, k)]], [sr])
        return view, sr

    def bcast_row(dst, dst_r, src_ap):
        ld(dst, src_ap.partition_broadcast(128), [], [dst_r])

    def layer_consts(l):
        for g in range(4):
            t1, t1r = tf()
            ld(t1[:, 0:128], w_s.ap()[l, g], [], [t1r])
            ps, pr = psum()
            tr(ps[:, 0:128], t1[:, 0:128], ident_f, [t1r, cst_r], [pr])
            tt(WT[:, g, :], ps[:, 0:128], causal_b, ALU.mult, [pr, cb_r], [WT_r])
        t1, t1r = tf()
        ld(t1[0:1, 0:512], b_s.ap()[l:l + 1].rearrange("o g t -> o (g t)"), [], [t1r])
        cp_(bsrow[:], t1[0:1, 0:512], [t1r], [bsrow_r])
        bcast_row(lc[:, 0:8], lc_r, sbb.ap()[l:l + 1, :])
        for h in range(4):
            ld(lc[:, 8 + h:9 + h], onw.ap()[l, h * 128:(h + 1) * 128].rearrange("(p o) -> p o", o=1), [], [lc_r])
            ld(lc[:, 20 + h:21 + h], w_s.ap()[l, h, 0:1, 0:1].partition_broadcast(128), [], [lc_r])
            ld(lc[:, 24 + h:25 + h], b_s.ap()[l, h:h + 1, 0:1].partition_broadcast(128), [], [lc_r])
        bcast_row(vnb[:], vnb_r, vnw.ap()[l:l + 1, :])
        if l == 0:
            mset(lbt[:], 0.0, [lbt_r]); mset(oml[:], 1.0, [oml_r])
            mset(lc[:, 12:16], 0.0, [lc_r]); mset(lc[:, 16:20], 1.0, [lc_r])
        else:
            a, ar = tf(); b, br = tf()
            bcast_row(a[:, 0:512], ar, lbl.ap()[0:1, :])
            bcast_row(b[:, 0:512], br, lbl.ap()[1:2, :])
            tt(a[:, 0:512], b[:, 0:512], a[:, 0:512], ALU.subtract, [ar, br], [ar])
            act(lbt[:], a[:, 0:512], AF.Sigmoid, [ar], [lbt_r])
            ts(oml[:], lbt[:], -1.0, 1.0, ALU.mult, ALU.add, [lbt_r], [oml_r])
            c, cr = tf()
            for h in range(4):
                ld(c[:, h:h + 1], lbl.ap()[0, h * 128:(h + 1) * 128].rearrange("(p o) -> p o", o=1), [], [cr])
                ld(c[:, 4 + h:5 + h], lbl.ap()[1, h * 128:(h + 1) * 128].rearrange("(p o) -> p o", o=1), [], [cr])
            tt(c[:, 8:12], c[:, 4:8], c[:, 0:4], ALU.subtract, [cr], [cr])
            act(lc[:, 12:16], c[:, 8:12], AF.Sigmoid, [cr], [lc_r])
            ts(lc[:, 16:20], lc[:, 12:16], -1.0, 1.0, ALU.mult, ALU.add, [lc_r], [lc_r])
        for h in range(4):
            mset(S_f[:, h, :], 0.0, [S_fr[h]])
            mset(S_b[:, h, :], 0.0, [S_br[h]])

    def compute_mod(l, csrc):
        c, cr = tf()
        ld(c[:], csrc.ap(), [], [cr])
        c2, c2r = tbf()
        act(c2[:], c[:], AF.Silu, [cr], [c2r])
        for kc in range(8):
            tr(psT[:, kc * 128:(kc + 1) * 128], c2[:, kc * 128:(kc + 1) * 128], ident_b, [c2r, cb_r], [psT_r])
        cT2, cTb_r = tbf()
        cTb = cT2[:, :].rearrange("p (a b) -> p a b", a=8)
        cp_(cTb, psT[:, :].rearrange("p (a b) -> p a b", a=8), [psT_r], [cTb_r])
        for blk in range(12):
            wv, wr = wchunk(l, "ada", 0, 8, blk * 512, 512)
            ps, pr = psum()
            for kc in range(8):
                mm(ps[:], cTb[:, kc, :], wv[:, kc, :], kc == 0, kc == 7, [cTb_r, wr], [pr])
            bb, bbr = tf()
            bcast_row(bb[:, 0:512], bbr, b_ada.ap()[l:l + 1, blk * 512:(blk + 1) * 512])
            which = blk // 2
            if which in (1, 4):
                nsrc = n1w if which == 1 else n2w
                off = (blk % 2) * 512
                bcast_row(bb[:, 512:1024], bbr, nsrc.ap()[l:l + 1, off:off + 512])
                t2, t2r = tf()
                tt(t2[:, 0:512], ps[:], bb[:, 0:512], ALU.add, [pr, bbr], [t2r])
                stt(mod[:, blk * 512:(blk + 1) * 512], t2[:, 0:512], 1.0, bb[:, 512:1024], ALU.add, ALU.mult,
                    [t2r, bbr], [mod_r])
            else:
                tt(mod[:, blk * 512:(blk + 1) * 512], ps[:], bb[:, 0:512], ALU.add, [pr, bbr], [mod_r])

    SH1, SC1, G1, SH2, SC2, G2 = 0, 1024, 2048, 3072, 4096, 5120

    def norm_to_HT(T, a_off, b_off):
        for i in range(T):
            junk, jr = tbf()
            s1, s1r = sm()
            act(junk[:], xg[:, i, :], AF.Square, [xg_r[i]], [jr, s1r], accum=s1[:, 0:1])
            act(s1[:, 1:2], s1[:, 0:1], AF.Sqrt, [s1r], [s1r], bias=EPS, scale=1.0 / D)
            recip(s1[:, 2:3], s1[:, 1:2], [s1r], [s1r])
            t1, t1r = tf()
            stt(t1[:], xg[:, i, :], s1[:, 2:3], mod[:, a_off:a_off + D], ALU.mult, ALU.mult, [xg_r[i], s1r, mod_r], [t1r])
            hb, hbr = tbf()
            tt(hb[:], t1[:], mod[:, b_off:b_off + D], ALU.add, [t1r, mod_r], [hbr])
            for kc in range(8):
                tr(psT[:, kc * 128:(kc + 1) * 128], hb[:, kc * 128:(kc + 1) * 128], ident_b, [hbr, cb_r], [psT_r])
            cp_(HT[:, :, i * 128:(i + 1) * 128], psT[:, :].rearrange("p (a b) -> p a b", a=8), [psT_r], [HT_r],
                eng="act")

    def proj_F(wv, wr, N, evac, src=None, src_r=None, nkc=8):
        src = HT if src is None else src
        src_r = HT_r if src_r is None else src_r
        for j in range(4):
            ps, pr = psum()
            for kc in range(nkc):
                mm(ps[:, 0:N], wv[:, kc, j * 128:(j + 1) * 128], src[:, kc, 0:N], kc == 0, kc == nkc - 1,
                   [wr, src_r], [pr])
            evac(j, ps, pr)

    def proj_T(wv, wr, T, evac):
        for i in range(T):
            ps, pr = psum()
            for kc in range(8):
                mm(ps[:], HT[:, kc, i * 128:(i + 1) * 128], wv[:, kc, :], kc == 0, kc == 7, [wr, HT_r], [pr])
            evac(i, ps, pr)

    def tokview(buf):
        return buf[:, :, :].rearrange("p a b -> p (a b)").rearrange("p (t c) -> p t c", c=512)

    def process_group(l, gi, sample):
        T = 1 if sample else NT
        N = T * 128
        tok0 = SEQ if sample else gi * NP
        for i in range(T):
            if l == 0:
                src = xs.ap() if sample else xp.ap()[gi * NP + i * 128: gi * NP + (i + 1) * 128, :]
                ld(xg[:, i, :], src, [], [xg_r[i]])
            else:
                ld(xg[:, i, :], x1.ap()[tok0 + i * 128: tok0 + (i + 1) * 128, :], [x1_res[tok0 // 128 + i]], [xg_r[i]])
        norm_to_HT(T, SC1, SH1)
        uT, uT_r = Pb[0], Pb_r[0]
        va, va_r = tokview(Pb[1]), Pb_r[1]

        wv, wr = wchunk(l, "in", 0, 8, 0, 512)
        proj_F(wv, wr, N, lambda j, ps, pr: act(uT[:, j, 0:N], ps[:, 0:N], AF.Gelu, [pr], [uT_r]))
        wv, wr = wchunk(l, "in", 0, 8, 512, 512)

        def ev_av(i, ps, pr):
            ga, gar = tf()
            act(ga[:, 0:512], ps[:], AF.Gelu, [pr], [gar])
            s1, s1r = sm()
            act(ga[:, 512:1024], ga[:, 0:512], AF.Square, [gar], [gar, s1r], accum=s1[:, 0:1])
            act(s1[:, 1:2], s1[:, 0:1], AF.Sqrt, [s1r], [s1r], bias=EPS, scale=1.0 / 512)
            recip(s1[:, 2:3], s1[:, 1:2], [s1r], [s1r])
            stt(ga[:, 512:1024], ga[:, 0:512], s1[:, 2:3], vnb[:], ALU.mult, ALU.mult, [gar, s1r, vnb_r], [gar])
            cp_(va[:, i, :], ga[:, 512:1024], [gar], [va_r], eng="act")
            if sample:
                P.dma("pool", lambda e: e.dma_start(out=gv.ap()[l], in_=ga[0:NS, 512:1024]), [gar], [outs_res["gv"]], store=True)
        proj_T(wv, wr, T, ev_av)
        for i in range(T):
            ps, pr = psum()
            for g in range(4):
                if sample:
                    mm(ps[:, g * 128:(g + 1) * 128], va[:, i, g * 128:(g + 1) * 128], ident_b, True, True,
                       [va_r, cb_r], [pr])
                else:
                    mm(ps[:, g * 128:(g + 1) * 128], va[:, i, g * 128:(g + 1) * 128], WT[:, g, :], True, False,
                       [va_r, WT_r], [pr])
                    mm(ps[:, g * 128:(g + 1) * 128], ones_b[0:1, :], bsrow[0:1, g * 128:(g + 1) * 128], False, True,
                       [cb_r, bsrow_r], [pr])
            if sample:
                t1, t1r = tf()
                for g in range(4):
                    ts(t1[:, g * 128:(g + 1) * 128], ps[:, g * 128:(g + 1) * 128], lc[:, 20 + g:21 + g],
                       lc[:, 24 + g:25 + g], ALU.mult, ALU.add, [pr, lc_r], [t1r])
                tt(OA[:, :, i * 128:(i + 1) * 128], t1[:, 0:512].rearrange("p (a b) -> p a b", a=4),
                   uT[:, :, i * 128:(i + 1) * 128], ALU.mult, [t1r, uT_r], [OA_r])
            else:
                tt(OA[:, :, i * 128:(i + 1) * 128], ps[:, :].rearrange("p (a b) -> p a b", a=4),
                   uT[:, :, i * 128:(i + 1) * 128], ALU.mult, [pr, uT_r], [OA_r])

        qT, qT_r = Pb[0], Pb_r[0]
        vb, vb_r = tokview(Pb[1]), Pb_r[1]
        gT, gT_r = Pb[2], Pb_r[2]
        wv, wr = wchunk(l, "in", 0, 8, 1024, 512)
        proj_F(wv, wr, N, lambda j, ps, pr: act(qT[:, j, 0:N], ps[:, 0:N], AF.Silu, [pr], [qT_r]))
        wv, wr = wchunk(l, "in", 0, 8, 1536, 512)
        fT, fT_r = fT_d, fT_dr
        if sample:
            def ev_f(j, ps, pr):
                sg, sgr = tf()
                act(sg[:, 0:128], ps[:, 0:128], AF.Sigmoid, [pr], [sgr])
                ts(fT[:, j * 128:(j + 1) * 128], sg[:, 0:128], lc[:, 16 + j:17 + j], lc[:, 12 + j:13 + j], ALU.mult,
                   ALU.add, [sgr, lc_r], [fT_r])
                ts(fT[:, 512 + j * 128:512 + (j + 1) * 128], fT[:, j * 128:(j + 1) * 128], -1.0, 1.0, ALU.mult, ALU.add,
                   [fT_r], [fT_r])
            proj_F(wv, wr, N, ev_f)
        else:
            proj_T(wv, wr, T, lambda i, ps, pr: cp_(FB[:, i, :], ps[:], [pr], [FB_r[i]], eng="act"))
        wv, wr = wchunk(l, "in", 0, 8, 2048, 512)
        if sample:
            def ev_i(i, ps, pr):
                t1, t1r = tf()
                cp_(t1[:, 0:512], ps[:], [pr], [t1r], eng="act")
                P.dma("pool", lambda e: e.dma_start(out=vscr.ap(), in_=t1[0:NS, 0:512]), [t1r], [vscr_res], store=True)
            proj_T(wv, wr, T, ev_i)
        else:
            proj_T(wv, wr, T, lambda i, ps, pr: cp_(vb[:, i, :], ps[:], [pr], [vb_r], eng="act"))
        wv, wr = wchunk(l, "in", 0, 8, 2560, 512)
        proj_F(wv, wr, N, lambda j, ps, pr: act(gT[:, j, 0:N], ps[:, 0:N], AF.Silu, [pr], [gT_r]))

        def onorm(psO, psO_r, ncols, dst, tcols):
            W4 = 4 * ncols
            o2, o2r = tbf()
            act(o2[:, 0:W4], psO[:, 0:W4], AF.Square, [psO_r], [o2r])
            pss, pssr = psum()
            mm(pss[:, 0:W4], ones_b, o2[:, 0:W4], True, True, [cb_r, o2r], [pssr])
            rs, rsr = tf()
            act(rs[:, 0:W4], pss[:, 0:W4], AF.Sqrt, [pssr], [rsr], bias=EPS, scale=1.0 / 128)
            recip(rs[:, 0:W4], rs[:, 0:W4], [rsr], [rsr])
            for h in range(4):
                stt(rs[:, 512 + h * ncols:512 + (h + 1) * ncols], psO[:, h * ncols:(h + 1) * ncols], lc[:, 8 + h:9 + h],
                    rs[:, h * ncols:(h + 1) * ncols], ALU.mult, ALU.mult, [psO_r, lc_r, rsr], [rsr])
            tt(dst, rs[:, 512:512 + W4].rearrange("p (a b) -> p a b", a=4), tcols, ALU.mult, [rsr, gT_r], [OB_r])

        if sample:
            mset(OB[:, :, :], 0.0, [OB_r])
            for b in range(NS):
                Sb, Sbr = Sb_t[0], Sb_r[0]
                ld(Sb[:], st.ap()[l, b].rearrange("h k v -> k h v"), [], [Sbr])
                vB, vBr = QB[b % 2], QB_r[b % 2]
                ld(vB, vscr.ap()[b:b + 1, :].partition_broadcast(128), [vscr_res], [vBr])
                sn, snr = tf()
                for h in range(4):
                    ts(sn[:, 512 + h * 128:512 + (h + 1) * 128], vB[:, h * 128:(h + 1) * 128],
                       fT[:, 512 + h * 128 + b:512 + h * 128 + b + 1], None, ALU.mult, None, [vBr, fT_r], [snr])
                    stt(sn[:, h * 128:(h + 1) * 128], Sb[:, h, :], fT[:, h * 128 + b:h * 128 + b + 1],
                        sn[:, 512 + h * 128:512 + (h + 1) * 128], ALU.mult, ALU.add, [Sbr, fT_r, snr], [snr])
                P.dma("pool", lambda e, b=b, sn=sn: e.dma_start(
                    out=Ss.ap()[l, b].rearrange("h k v -> k h v"),
                    in_=sn[:, 0:512].rearrange("p (a b) -> p a b", a=4)), [snr], [outs_res["Ss"]], store=True)
                snb, snbr = tbf()
                cp_(snb[:, 0:512], sn[:, 0:512], [snr], [snbr], eng="act")
                for h in range(4):
                    mm(psAcc[:, h * NS + b:h * NS + b + 1], snb[:, h * 128:(h + 1) * 128], qT[:, h, b:b + 1], True, True,
                       [snbr, qT_r], [psAcc_r])
            onorm(psAcc, psAcc_r, NS, OB[:, :, 0:NS], gT[:, :, 0:NS])
        else:
            for i in range(T):
                sg, sgr = tf()
                act(sg[:, 0:512], FB[:, i, :], AF.Sigmoid, [FB_r[i]], [sgr])
                tt(sg[:, 0:512], sg[:, 0:512], oml[:], ALU.mult, [sgr, oml_r], [sgr])
                tt(sg[:, 0:512], sg[:, 0:512], lbt[:], ALU.add, [sgr, lbt_r], [sgr])
                act(sg[:, 512:1024], sg[:, 0:512], AF.Ln, [sgr], [sgr])
                lf = sg[:, 512:1024]
                psb, psbr = psum()
                mm(psb[:], bmid_f, lf, True, True, [cst_r, sgr], [psbr])
                e1, e1r = tf()
                ts(e1[:, 0:512], psb[:], -80.0, 80.0, ALU.max, ALU.min, [psbr], [e1r])
                act(e1[:, 0:512], e1[:, 0:512], AF.Exp, [e1r], [e1r], scale=-1.0)
                ts(e1[:, 512:1024], sg[:, 0:512], -1.0, 1.0, ALU.mult, ALU.add, [sgr], [e1r])
                kh, khr = tbf()
                tt(kh[:, 0:512], e1[:, 512:1024], e1[:, 0:512], ALU.mult, [e1r], [khr])
                psr_, psrr = psum()
                mm(psr_[:], rev32_f, lf, True, True, [cst_r, sgr], [psrr])
                e2, e2r = tf()
                act(e2[:, 0:512], psr_[:], AF.Exp, [psrr], [e2r])
                kd, kdr = tbf()
                tt(kd[:, 0:512], e1[:, 512:1024], e2[:, 0:512], ALU.mult, [e1r, e2r], [kdr])
                for h in range(4):
                    tr(psT[:, h * 128:(h + 1) * 128], kh[:, h * 128:(h + 1) * 128], ident_b, [khr, cb_r], [psT_r])
                cp_(kh[:, 512:1024], psT[:, 0:512], [psT_r], [khr], eng="act")
                psbt, psbtr = psum()
                for h in range(4):
                    mm(psbt[:, h * 128:(h + 1) * 128], sg[:, 512 + h * 128:512 + (h + 1) * 128], bmid_f, True, True,
                       [sgr, cst_r], [psbtr])
                eb, ebr = tf()
                ts(eb[:, 0:512], psbt[:], -80.0, 80.0, ALU.max, ALU.min, [psbtr], [ebr])
                act(eb[:, 0:512], eb[:, 0:512], AF.Exp, [ebr], [ebr])
                psbs, psbsr = psum()
                for h in range(4):
                    mm(psbs[:, h * 128:(h + 1) * 128], sg[:, 512 + h * 128:512 + (h + 1) * 128], bt32_f, True, True,
                       [sgr, cst_r], [psbsr])
                act(eb[:, 512:1024], psbs[:], AF.Exp, [psbsr], [ebr])
                ql, qlr = tbf()
                tt(ql[:, 0:512].rearrange("p (a b) -> p a b", a=4), qT[:, :, i * 128:(i + 1) * 128],
                   eb[:, 0:512].rearrange("p (a b) -> p a b", a=4), ALU.mult, [qT_r, ebr], [qlr])
                qs, qsr = tbf()
                tt(qs[:, 0:512].rearrange("p (a b) -> p a b", a=4), qT[:, :, i * 128:(i + 1) * 128],
                   eb[:, 512:1024].rearrange("p (a b) -> p a b", a=4), ALU.mult, [qT_r, ebr], [qsr])
                psA, psAr = psum()
                for h in range(4):
                    mm(psA[:, h * 128:(h + 1) * 128], kh[:, 512 + h * 128:512 + (h + 1) * 128],
                       ql[:, h * 128:(h + 1) * 128], True, True, [khr, qlr], [psAr])
                at_, atr = tf()
                ts(at_[:, 0:512], psA[:], 1e30, -1e30, ALU.min, ALU.max, [psAr], [atr])
                tt(ql[:, 512:1024], at_[:, 0:512], bt32x4_b, ALU.mult, [atr, cb_r], [qlr])
                vm = []
                for jj in range(2):
                    vmt, vmr = tbf()
                    for j2 in range(2):
                        j = jj * 2 + j2
                        ts(vmt[:, j2 * 512:(j2 + 1) * 512], vb[:, i, :], submask[:, j:j + 1], None, ALU.mult, None,
                           [vb_r, cst_r], [vmr])
                        vm.append((vmt[:, j2 * 512:(j2 + 1) * 512], vmr))
                for h in range(4):
                    hs = slice(h * 128, (h + 1) * 128)
                    mm(psAcc[:, hs], vb[:, i, hs], ql[:, 512 + h * 128:512 + (h + 1) * 128], True, False,
                       [vb_r, qlr], [psAcc_r])
                    for j in range(4):
                        c0 = h * 128 + j * 32
                        mm(psAcc[:, c0:c0 + 32], S_b[:, h, :], qs[:, c0:c0 + 32], False, j == 3,
                           [S_br[h], qsr], [psAcc_r])
                        psd, psdr = psum()
                        mm(psd[:, 0:128], kd[:, hs], vm[j][0][:, hs], True, True, [kdr, vm[j][1]], [psdr])
                        ecol = 512 + c0 + 31
                        stt(S_f[:, h, :], S_f[:, h, :], eb[:, ecol:ecol + 1], psd[:, 0:128], ALU.mult, ALU.add,
                            [S_fr[h], ebr, psdr], [S_fr[h]])
                        cp_(S_b[:, h, :], S_f[:, h, :], [S_fr[h]], [S_br[h]], eng="act")
                onorm(psAcc, psAcc_r, 128, OB[:, :, i * 128:(i + 1) * 128], gT[:, :, i * 128:(i + 1) * 128])
            if gi == NG - 1:
                P.dma("pool", lambda e: e.dma_start(out=Sp.ap()[l].rearrange("h k v -> k h v"), in_=S_f[:]),
                      S_fr, [outs_res["Sp"]], store=True)

        cqT, cqT_r = Pb[0], Pb_r[0]
        wv, wr = wchunk(l, "in", 0, 8, 3072, 512)
        if sample:
            def ev_q(i, ps, pr):
                t1, t1r = tf()
                cp_(t1[:, 0:512], ps[:], [pr], [t1r], eng="act")
                P.dma("pool", lambda e: e.dma_start(out=qscr.ap(), in_=t1[0:NS, 0:512]), [t1r], [qscr_res], store=True)
            proj_T(wv, wr, T, ev_q)
        else:
            proj_F(wv, wr, N, lambda j, ps, pr: cp_(cqT[:, j, 0:N], ps[:, 0:N], [pr], [cqT_r], eng="act"))
        wv, wr = wchunk(l, "in", 0, 8, 3584, 512)

        def ev_k(i, ps, pr):
            kf, kfr = tf()
            cp_(kf[:, 0:512], ps[:], [pr], [kfr], eng="act")
            if sample:
                P.dma("pool", lambda e: e.dma_start(out=ks.ap()[l], in_=kf[0:NS, 0:512]), [kfr], [outs_res["ks"]], store=True)
            else:
                ti = gi * NT + i
                P.dma("pool", lambda e: e.dma_start(out=kp.ap()[l, ti * 128:(ti + 1) * 128, :], in_=kf[:, 0:512]),
                      [kfr], [outs_res["kp"]], store=True)
                ps2, p2r = psum()
                for j in range(4):
                    tr(ps2[:, j * 128:(j + 1) * 128], kf[:, j * 128:(j + 1) * 128], ident_f, [kfr, cst_r], [p2r])
                cp_(KT[:, :, ti * 128:(ti + 1) * 128], ps2[:, :].rearrange("p (a b) -> p a b", a=4), [p2r], [KT_r[ti]])
        proj_T(wv, wr, T, ev_k)
        wv, wr = wchunk(l, "in", 0, 8, 4096, 512)

        def ev_v(i, ps, pr):
            vf, vfr = tf()
            cp_(vf[:, 0:512], ps[:], [pr], [vfr], eng="act")
            if sample:
                P.dma("pool", lambda e: e.dma_start(out=vs.ap()[l], in_=vf[0:NS, 0:512]), [vfr], [outs_res["vs"]], store=True)
            else:
                ti = gi * NT + i
                P.dma("pool", lambda e: e.dma_start(out=vp.ap()[l, ti * 128:(ti + 1) * 128, :], in_=vf[:, 0:512]),
                      [vfr], [outs_res["vp"]], store=True)
                cp_(VH[:, ti, :], vf[:, 0:512], [vfr], [VH_r[ti]])
        proj_T(wv, wr, T, ev_v)

        if sample:
            mset(OC[:, :, :], 0.0, [OC_r])
            for b in range(NS):
                qb, qbr = QB[b % 2], QB_r[b % 2]
                ld(qb, qscr.ap()[b:b + 1, :].partition_broadcast(128), [qscr_res], [qbr])
                for p in range(NPG):
                    col = b * NPG + p
                    r = NPG - 1 - p
                    kpg_t, kpr = tf()
                    kpg = kpg_t[:, 0:512]
                    P.dma("pool", lambda e, kpg=kpg, col=col: e.indirect_dma_start(
                        out=kpg, out_offset=None, in_=ck.ap().rearrange("l r c -> (l r) c"),
                        in_offset=bass.IndirectOffsetOnAxis(ap=IDX[:, col:col + 1], axis=0),
                        element_offset=l * NPOOL * 128 * 512), [IDX_r], [kpr])
                    pr_, prr = tf()
                    tt(pr_[:, 0:512], kpg, qb, ALU.mult, [kpr, qbr], [prr])
                    P.op("dve", lambda e, pr_=pr_, r=r: e.tensor_reduce(
                        out=zb[:, :, r], in_=pr_[:, 0:512].rearrange("p (a b) -> p a b", a=8), axis=AX.X, op=ALU.add),
                        reads=[prr], writes=[zb_r])
                vbase = (b % 2) * NPG if NTILES >= 2 * NPG else 0
                for p in range(NPG):
                    col = b * NPG + p
                    P.dma("pool", lambda e, p=p, col=col, vbase=vbase: e.indirect_dma_start(
                        out=Vpg[:, vbase + p, :], out_offset=None, in_=cv.ap().rearrange("l r c -> (l r) c"),
                        in_offset=bass.IndirectOffsetOnAxis(ap=IDX[:, col:col + 1], axis=0),
                        element_offset=l * NPOOL * 128 * 512), [IDX_r], [Vpg_r[vbase + p]])
                for h in range(8):
                    act(eb_s[:, h, :], zb[:, h, :], AF.Exp, [zb_r, lc_r], [eb_sr], bias=lc[:, h:h + 1], scale=0.125)
                act(spb_s[:], eb_s[:, :, :].rearrange("p a b -> p (a b)"), AF.Ln, [eb_sr], [spb_sr], bias=1.0)
                psg, psgr = psum()
                mm(psg[:, 0:8 * NPG], tri_b, spb_s[:], True, True, [cb_r, spb_sr], [psgr])
                pst, pstr = psum()
                mm(pst[:, 0:8 * NPG], ones_b, spb_s[:], True, True, [cb_r, spb_sr], [pstr])
                t1, t1r = tf()
                W8 = 8 * NPG
                P.op("dve", lambda e, t1=t1, pst=pst: e.tensor_tensor_scan(
                    out=t1[:, 0:W8], data0=rmask[:], data1=pst[:, 0:W8], initial=0.0, op0=ALU.mult, op1=ALU.add),
                    reads=[rmask_r, pstr], writes=[t1r])
                tt(t1[:, 0:W8], t1[:, 0:W8], pst[:, 0:W8], ALU.subtract, [t1r, pstr], [t1r])
                tt(t1[:, 0:W8], t1[:, 0:W8], psg[:, 0:W8], ALU.add, [t1r, psgr], [t1r])
                act(t1[:, 0:W8], t1[:, 0:W8], AF.Exp, [t1r], [t1r], scale=-1.0)
                tt(wb_s[:, :, :].rearrange("p a b -> p (a b)"), t1[:, 0:W8],
                   eb_s[:, :, :].rearrange("p a b -> p (a b)"), ALU.mult, [t1r, eb_sr], [wb_sr])
                for h in range(8):
                    hp, po = h // 2, (h % 2) * 64
                    for p in range(NPG):
                        r = NPG - 1 - p
                        mm(psAcc2[po:po + 64, hp * NS + b:hp * NS + b + 1], Vpg[:, vbase + p, h * 64:(h + 1) * 64],
                           wb_s[:, h, r:r + 1], p == 0, p == NPG - 1, [Vpg_r[vbase + p], wb_sr], [psAcc2_r])
            cp_(OC[:, :, 0:NS], psAcc2[:, 0:4 * NS].rearrange("p (a b) -> p a b", a=4), [psAcc2_r], [OC_r], eng="act")
        else:
            last = gi * NT + NT - 1
            pack = N <= 256
            for q in range(2):
                heads = list(range(4 * q, 4 * q + 4))
                accs = {}
                for h in heads:
                    hp, po = h // 2, (h % 2) * 64
                    acc, accr = (psAcc2, psAcc2_r) if (h % 4) < 2 else (psAcc, psAcc_r)
                    accs[h] = (acc, accr)
                    mset(Rf4[:, h % 4, :], 0.0, [Rf4_r[h % 4]], eng="pool")
                    mset(Rb4[:, h % 4, :], 0.0, [Rb4_r[h % 4]], eng="pool")
                    mm(acc[po:po + 64, 0:N], zeros_b[:, 0:64], zeros_b[:, 0:N], True, False, [zeros_r], [accr])
                for kt in range(last, -1, -1):
                    jq = max(0, kt - gi * NT)
                    c0 = jq * 128
                    diag = kt >= gi * NT
                    first = kt == last
                    stg = {}
                    for h in heads:
                        hp, po = h // 2, (h % 2) * 64
                        psz, pszr = psum()
                        if pack:
                            psg, psgr, go = psz, pszr, 256
                        else:
                            psg, psgr = psum()
                            go = 0
                        mm(psz[:, c0:N], KT[po:po + 64, hp, kt * 128:(kt + 1) * 128], cqT[po:po + 64, hp, c0:N],
                           True, True, [KT_r[kt], cqT_r], [pszr])
                        e_, er = tf()
                        act(e_[:, c0:N], psz[:, c0:N], AF.Exp, [pszr, lc_r], [er], bias=lc[:, h:h + 1], scale=0.125)
                        sp_, spr = tbf()
                        stg[h] = (psg, psgr, go, e_, er, sp_, spr)
                    for h in heads:
                        psg, psgr, go, e_, er, sp_, spr = stg[h]
                        act(sp_[:, c0:N], e_[:, c0:N], AF.Ln, [er], [spr], bias=1.0)
                        if diag:
                            tt(sp_[:, c0:c0 + 128], sp_[:, c0:c0 + 128], mstrict_b, ALU.mult, [spr, cb_r], [spr])
                    for h in heads:
                        psg, psgr, go, e_, er, sp_, spr = stg[h]
                        Rb, Rb_r = Rb4[:, h % 4, :], Rb4_r[h % 4]
                        mm(psg[:, go + c0:go + N], tri_b, sp_[:, c0:N], True, first, [cb_r, spr], [psgr])
                        if not first:
                            mm(psg[:, go + c0:go + N], ones_b, Rb[:, c0:N], False, True, [cb_r, Rb_r], [psgr])
                        act(psg[:, go + c0:go + N], psg[:, go + c0:go + N], AF.Exp, [psgr], [psgr], scale=-1.0)
                    for h in heads:
                        hp, po = h // 2, (h % 2) * 64
                        acc, accr = accs[h]
                        psg, psgr, go, e_, er, sp_, spr = stg[h]
                        Rf, Rf_r, Rb, Rb_r = Rf4[:, h % 4, :], Rf4_r[h % 4], Rb4[:, h % 4, :], Rb4_r[h % 4]
                        tt(sp_[:, 512 + c0:512 + N], e_[:, c0:N], psg[:, go + c0:go + N], ALU.mult, [er, psgr], [spr])
                        if diag:
                            tt(sp_[:, 512 + c0:512 + c0 + 128], sp_[:, 512 + c0:512 + c0 + 128], mstrict_b, ALU.mult,
                               [spr, cb_r], [spr])
                        mm(acc[po:po + 64, c0:N], VH[:, kt, h * 64:(h + 1) * 64], sp_[:, 512 + c0:512 + N], False,
                           kt == 0, [VH_r[kt], spr], [accr])
                    for h in heads:
                        psg, psgr, go, e_, er, sp_, spr = stg[h]
                        Rf, Rf_r, Rb, Rb_r = Rf4[:, h % 4, :], Rf4_r[h % 4], Rb4[:, h % 4, :], Rb4_r[h % 4]
                        if kt > 0:
                            tt(Rf[:, c0:N], Rf[:, c0:N], sp_[:, c0:N], ALU.add, [Rf_r, spr], [Rf_r])
                            cp_(Rb[:, c0:N], Rf[:, c0:N], [Rf_r], [Rb_r])
                for h in heads:
                    hp, po = h // 2, (h % 2) * 64
                    acc, accr = accs[h]
                    cp_(OC[po:po + 64, hp, 0:N], acc[po:po + 64, 0:N], [accr], [OC_r], eng="act")

        for c in range(6):
            wv, wr = wchunk(l, "in", 0, 8, 4608 + c * 512, 512)
            proj_F(wv, wr, N, lambda j, ps, pr, c=c: act(BIG[:, c * 4 + j, 0:N], ps[:, 0:N], AF.Sigmoid, [pr],
                                                        [BIG_r[c * 4 + j]]))
        for xi, (wk, osrc, osr) in enumerate([("ba", OA, OA_r), ("bb", OB, OB_r), ("bc", OC, OC_r)]):
            for half in range(2):
                wv, wr = wchunk(l, wk, 0, 4, half * 512, 512)

                def ev_b(j, ps, pr, xi=xi, half=half):
                    jj = half * 4 + j
                    gate = BIG[:, xi * 8 + jj, 0:N]
                    if xi == 0:
                        tt(HT[:, jj, 0:N], ps[:, 0:N], gate, ALU.mult, [pr, BIG_r[xi * 8 + jj]], [HT_r])
                    else:
                        t1, t1r = tf()
                        tt(t1[:, 0:N], ps[:, 0:N], gate, ALU.mult, [pr, BIG_r[xi * 8 + jj]], [t1r])
                        tt(HT[:, jj, 0:N], HT[:, jj, 0:N], t1[:, 0:N], ALU.add, [HT_r, t1r], [HT_r])
                proj_F(wv, wr, N, ev_b, src=osrc, src_r=osr, nkc=4)
        for half in range(2):
            wv, wr = wchunk(l, "out", 0, 8, half * 512, 512)

            def ev_o(i, ps, pr, half=half):
                t1, t1r = tf()
                tt(t1[:, 0:512], ps[:], mod[:, G1 + half * 512:G1 + (half + 1) * 512], ALU.mult, [pr, mod_r], [t1r])
                tt(xg[:, i, half * 512:(half + 1) * 512], xg[:, i, half * 512:(half + 1) * 512], t1[:, 0:512], ALU.add,
                   [xg_r[i], t1r], [xg_r[i]])
            proj_T(wv, wr, T, ev_o)
        norm_to_HT(T, SC2, SH2)
        for half in range(2):
            for c in range(4):
                wv, wr = wchunk(l, "ff1", 0, 8, half * 2048 + c * 512, 512)

                def ev_f1(j, ps, pr, c=c):
                    r_, rr = tbf()
                    act(r_[:, 0:N], ps[:, 0:N], AF.Relu, [pr], [rr])
                    tt(BIG[:, c * 4 + j, 0:N], r_[:, 0:N], r_[:, 0:N], ALU.mult, [rr], [BIG_r[c * 4 + j]], eng="pool")
                proj_F(wv, wr, N, ev_f1)
            for c in range(4):
                slot, sr = ring(wring, wring_r, wr_i)
                view = slot[:, :].rearrange("p (a b) -> p a b", a=16)
                src = wb[(l, "ff2")].ap()[half * 2048:(half + 1) * 2048, c * 256:(c + 1) * 256].rearrange(
                    "(a p) c -> p a c", p=128)
                ld(view, src, [wres[(l, "ff2")]], [sr])
                for i in range(T):
                    ps, pr = psum()
                    for kc in range(16):
                        mm(ps[:, 0:256], BIG[:, kc, i * 128:(i + 1) * 128], view[:, kc, :], kc == 0, kc == 15,
                           [BIG_r[kc], sr], [pr])
                    t1, t1r = tf()
                    tt(t1[:, 0:256], ps[:, 0:256], mod[:, G2 + c * 256:G2 + (c + 1) * 256], ALU.mult, [pr, mod_r], [t1r])
                    tt(xg[:, i, c * 256:(c + 1) * 256], xg[:, i, c * 256:(c + 1) * 256], t1[:, 0:256], ALU.add,
                       [xg_r[i], t1r], [xg_r[i]])
        for i in range(T):
            if l == 0:
                P.dma("pool", lambda e, i=i: e.dma_start(out=x1.ap()[tok0 + i * 128: tok0 + (i + 1) * 128, :],
                                                         in_=xg[:, i, :]), [xg_r[i]], [x1_res[tok0 // 128 + i]], store=True)
            else:
                junk, jr = tbf()
                s1, s1r = sm()
                act(junk[:], xg[:, i, :], AF.Square, [xg_r[i]], [jr, s1r], accum=s1[:, 0:1])
                act(s1[:, 1:2], s1[:, 0:1], AF.Sqrt, [s1r], [s1r], bias=EPS, scale=1.0 / D)
                recip(s1[:, 2:3], s1[:, 1:2], [s1r], [s1r])
                fb_, fbr = tf()
                bcast_row(fb_[:], fbr, fnw.ap())
                y_, yr = tf()
                stt(y_[:], xg[:, i, :], s1[:, 2:3], fb_[:], ALU.mult, ALU.mult, [xg_r[i], s1r, fbr], [yr])
                if sample:
                    P.dma("pool", lambda e, y_=y_: e.dma_start(out=ys.ap(), in_=y_[0:NS, :]), [yr], [outs_res["ys"]], store=True)
                else:
                    P.dma("pool", lambda e, y_=y_, i=i: e.dma_start(
                        out=yp.ap()[gi * NP + i * 128: gi * NP + (i + 1) * 128, :], in_=y_[:]), [yr], [outs_res["yp"]], store=True)

    for l in range(2):
        layer_consts(l)
        compute_mod(l, cp)
        for gi in range(NG):
            process_group(l, gi, False)
        compute_mod(l, cs)
        process_group(l, 0, True)
    P.wait_all("pool", list(outs_res.values()))
    es.close()
    return nc


_CACHE = {}


def _consts():
    c = np.zeros((128, 1413), np.float32)
    i = np.arange(128)
    s, t = i[:, None], i[None, :]
    c[:, 0:128] = np.eye(128)
    c[:, 128:256] = (s >= t)
    c[:, 256:384] = 1.0
    c[:, 384:512] = (s < t)
    same = (s // 32) == (t // 32)
    bt = ((s <= t) & same).astype(np.float32)
    mid = (t // 32) * 32 + 15
    c[:, 1153:1281] = (((s <= t) & same).astype(np.float32) - ((s <= mid) & same).astype(np.float32))
    c[:, 1281:1409] = ((s > t) & same)
    for j in range(4):
        c[:, 1409 + j] = ((i // 32) == j)
    c[:, 512:1024] = np.tile(bt, (1, 4))
    c[:, 1024:1152] = (s <= t)
    c[:, 1152] = i
    return c


def kernel(x_prompt, x_sample, c_prompt, c_sample, cache_k, cache_v, state_hgrn, page_table,
           w_ada, b_ada, norm1_w, norm2_w, w_in, gmlp_vnorm_w, gmlp_w_s, gmlp_b_s, sb_bias,
           hgrn_lb_logits, hgrn_onorm_w, w_branch_a, w_branch_b, w_branch_c, w_out,
           w_ff1, w_ff2, final_norm_w, NT=2):
    f = lambda a: np.ascontiguousarray(np.asarray(a, dtype=np.float32))
    x_prompt = f(x_prompt); x_sample = f(x_sample)
    B, SEQ, _ = x_prompt.shape
    DB = x_sample.shape[0]
    NPG = page_table.shape[1]
    NPOOL = cache_k.shape[1]
    assert DB == 8 * NS and B == 4
    key = (SEQ, NPG, NPOOL, NT)
    if key not in _CACHE:
        _CACHE[key] = build(SEQ, NPG, NPOOL, NT)
    nc = _CACHE[key]
    ckf = f(cache_k).reshape(2, NPOOL * 128, 512)
    cvf = f(cache_v).reshape(2, NPOOL * 128, 512)
    consts = _consts()
    shared = dict(ck=ckf, cv=cvf, consts=consts, w_ada=f(w_ada), b_ada=f(b_ada), n1w=f(norm1_w), n2w=f(norm2_w),
                  w_in=f(w_in), vnw=f(gmlp_vnorm_w), w_s=f(gmlp_w_s), b_s=f(gmlp_b_s), sbb=f(sb_bias),
                  lbl=f(hgrn_lb_logits), onw=f(hgrn_onorm_w), w_ba=f(w_branch_a), w_bb=f(w_branch_b),
                  w_bc=f(w_branch_c), w_out=f(w_out), w_ff1=f(w_ff1), w_ff2=f(w_ff2),
                  fnw=f(final_norm_w).reshape(1, D))
    cpf = f(c_prompt); csf = f(c_sample); stf = f(state_hgrn)
    pti = np.ascontiguousarray(np.asarray(page_table, dtype=np.int32))
    in_maps = []
    for c in range(8):
        b = c % 4
        xs_t = np.zeros((128, D), np.float32); xs_t[:NS] = x_sample[c * NS:(c + 1) * NS, 0, :]
        cs_t = np.zeros((128, D), np.float32); cs_t[:NS] = csf[c * NS:(c + 1) * NS]
        m = dict(shared)
        m.update(xp=x_prompt[b], xs=xs_t, cp=np.ascontiguousarray(np.broadcast_to(cpf[b], (128, D))), cs=cs_t,
                 st=np.ascontiguousarray(stf[:, c * NS:(c + 1) * NS]),
                 pt=np.ascontiguousarray(pti[c * NS:(c + 1) * NS].reshape(1, NS * NPG)))
        in_maps.append(m)
    res = run_bass_kernel_spmd(nc, in_maps, core_ids=list(range(8))).results
    y_prompt = np.stack([res[b]["yp"] for b in range(4)])
    y_sample = np.concatenate([res[c]["ys"] for c in range(8)])[:, None, :]
    k_prompt = np.stack([res[b]["kp"] for b in range(4)], axis=1).reshape(2, 4, SEQ, 8, 64)
    v_prompt = np.stack([res[b]["vp"] for b in range(4)], axis=1).reshape(2, 4, SEQ, 8, 64)
    S_prompt = np.stack([res[b]["Sp"] for b in range(4)], axis=1)
    k_sample = np.concatenate([res[c]["ks"] for c in range(8)], axis=1).reshape(2, DB, 1, 8, 64)
    v_sample = np.concatenate([res[c]["vs"] for c in range(8)], axis=1).reshape(2, DB, 1, 8, 64)
    S_sample = np.concatenate([res[c]["Ss"] for c in range(8)], axis=1)
    g_sample = np.concatenate([res[c]["gv"] for c in range(8)], axis=1)[:, :, None, :]
    return (y_prompt, y_sample, k_prompt, v_prompt, S_prompt, k_sample, v_sample, S_sample, g_sample)
```
